# Optimizing a Trainium2 kernel written in Bass

```python
import math
import jax, jax.numpy as jnp
from jax import lax
import numpy as np

D_MODEL = 1024
BATCH = 4
SEQ = 8192
DEPTH = 2

CTX_LEN = 256
GRID_W = 64
N_EVEN = (DEPTH + 1) // 2
N_ODD = DEPTH // 2
NORM_EPS = 1e-6
ROPE_THETA = 10000.0
ROPE_DIM = 64
NEG_INF = -1e30

DA_HEADS = 4
DA_QK_DIM = 64
DA_V_DIM = 2 * DA_QK_DIM
DA_QK_W = DA_HEADS * 2 * DA_QK_DIM
WA_Q_HEADS = 8
WA_KV_HEADS = 2
WA_GROUP = WA_Q_HEADS // WA_KV_HEADS
WA_HEAD_DIM = 64
WINDOW = 128
BLOCK = 128

EVEN_WIDTHS = (DA_QK_W, DA_QK_W, DA_HEADS * DA_V_DIM, WA_Q_HEADS * WA_HEAD_DIM,
               WA_KV_HEADS * WA_HEAD_DIM, WA_KV_HEADS * WA_HEAD_DIM)
EVEN_IN = sum(EVEN_WIDTHS)
EVEN_SPLITS = tuple(int(v) for v in np.cumsum(EVEN_WIDTHS)[:-1])
EVEN_MIX = DA_HEADS * DA_V_DIM + WA_Q_HEADS * WA_HEAD_DIM

GLA_HEADS = 4
GLA_K_DIM = 64
GLA_V_DIM = 128
GLA_GATE_RANK = 16
GLA_GATE_NORM = 16.0
GLA_CHUNK = 64
LRU_WIDTH = 512
LRU_BLOCKS = 8
LRU_CONV = 4
CONV_LEFT = (LRU_CONV - 1) // 2
LRU_C = 8.0

ODD_WIDTHS = (GLA_HEADS * GLA_K_DIM, GLA_HEADS * GLA_K_DIM, GLA_HEADS * GLA_V_DIM,
              GLA_HEADS * GLA_V_DIM, 2 * GLA_GATE_RANK, LRU_WIDTH, LRU_WIDTH)
ODD_IN = sum(ODD_WIDTHS)
ODD_SPLITS = tuple(int(v) for v in np.cumsum(ODD_WIDTHS)[:-1])
ODD_MIX = GLA_HEADS * GLA_V_DIM + LRU_WIDTH

D_FF = ((8 * D_MODEL + 3 * 256 - 1) // (3 * 256)) * 256

kernel_name = 'hybrid_diffusion_trunk'


def rms_norm(x, g):
    xf = x.astype(jnp.float32)
    y = xf * lax.rsqrt(jnp.mean(xf * xf, axis=-1, keepdims=True) + NORM_EPS)
    return (y * g.astype(jnp.float32)).astype(x.dtype)


def head_rms(x):
    xf = x.astype(jnp.float32)
    return xf * lax.rsqrt(jnp.mean(xf * xf, axis=-1, keepdims=True) + NORM_EPS)


def modulate(h, shift, scale):
    return h * (1 + scale) + shift


def swiglu(h, w_in, w_out):
    gate, up = jnp.split(h @ w_in, 2, axis=-1)
    return (jax.nn.silu(gate) * up) @ w_out


def rope_tables(n_tokens):
    n_rows = n_tokens // GRID_W
    row = jnp.broadcast_to(jnp.arange(n_rows)[:, None], (n_rows, GRID_W)).reshape(-1)
    col = jnp.broadcast_to(jnp.arange(GRID_W)[None, :], (n_rows, GRID_W)).reshape(-1)
    axis_dim = ROPE_DIM // 2
    inv_freq = ROPE_THETA ** (-jnp.arange(0, axis_dim, 2, dtype=jnp.float32) / axis_dim)
    ang_r = row.astype(jnp.float32)[:, None] * inv_freq
    ang_c = col.astype(jnp.float32)[:, None] * inv_freq
    return (jnp.cos(ang_r), jnp.sin(ang_r), jnp.cos(ang_c), jnp.sin(ang_c))


def _rot_half(x, cos, sin):
    x1, x2 = jnp.split(x, 2, axis=-1)
    return jnp.concatenate([x1 * cos - x2 * sin, x1 * sin + x2 * cos], axis=-1)


def apply_axial_rope(x, tables):
    shape = (1, x.shape[1]) + (1,) * (x.ndim - 3) + (-1,)
    cos_r, sin_r, cos_c, sin_c = (t.reshape(shape) for t in tables)
    xf = x.astype(jnp.float32)
    half = x.shape[-1] // 2
    out = jnp.concatenate([_rot_half(xf[..., :half], cos_r, sin_r),
                           _rot_half(xf[..., half:], cos_c, sin_c)], axis=-1)
    return out.astype(x.dtype)


def diff_attn_core(q, k, v, lam):
    s = jnp.einsum('bqhmd,bkhmd->bhmqk', q, k).astype(jnp.float32) * (DA_QK_DIM ** -0.5)
    p = jax.nn.softmax(s, axis=-1)
    w = p[:, :, 0] - lam * p[:, :, 1]
    return jnp.einsum('bhqk,bkhd->bqhd', w.astype(v.dtype), v)


def diff_head_out(o, lam_init):
    y = head_rms(o) * (1.0 - lam_init)
    return y.reshape(o.shape[0], o.shape[1], -1).astype(o.dtype)


def gqa_scores(q, k):
    return jnp.einsum('bqhgd,bkhd->bhgqk', q, k).astype(jnp.float32) * (WA_HEAD_DIM ** -0.5)


def gqa_values(p, v):
    return jnp.einsum('bhgqk,bkhd->bqhgd', p.astype(v.dtype), v)


def sink_softmax(scores, sink):
    s_sink = jnp.broadcast_to(sink[None, :, :, None, None], scores[0].shape[:-1] + (1,))
    p = jax.nn.softmax(jnp.concatenate([s_sink] + scores, axis=-1), axis=-1)
    bounds = [1]
    for s in scores:
        bounds.append(bounds[-1] + s.shape[-1])
    return [p[..., bounds[i]:bounds[i + 1]] for i in range(len(scores))]


def window_gqa(q, k, v, kc, vc, sink):
    B, L = q.shape[0], q.shape[1]
    nb = L // BLOCK

    def band(t):
        tp = jnp.pad(t, ((0, 0), (BLOCK, BLOCK)) + ((0, 0),) * (t.ndim - 2))
        tp = tp.reshape((B, nb + 2, BLOCK) + t.shape[2:])
        win = jnp.concatenate([tp[:, :-2], tp[:, 1:-1], tp[:, 2:]], axis=2)
        return win.swapaxes(0, 1)

    q_blocks = q.reshape((B, nb, BLOCK) + q.shape[2:]).swapaxes(0, 1)
    offs_q = jnp.arange(BLOCK)
    offs_k = jnp.arange(3 * BLOCK) - BLOCK
    rel_ok = jnp.abs(offs_k[None, :] - offs_q[:, None]) <= WINDOW

    def one_block(args):
        qq, kk, vv, bi = args
        kpos = bi * BLOCK + offs_k
        valid = rel_ok & ((kpos >= 0) & (kpos < L))[None, :]
        s_win = jnp.where(valid, gqa_scores(qq, kk), NEG_INF)
        p_ctx, p_win = sink_softmax([gqa_scores(qq, kc), s_win], sink)
        return gqa_values(p_ctx, vc) + gqa_values(p_win, vv)

    out = lax.map(one_block, (q_blocks, band(k), band(v), jnp.arange(nb)))
    return out.swapaxes(0, 1).reshape(B, L, -1)


def ctx_gqa(qc, kc, vc, sink):
    (p,) = sink_softmax([gqa_scores(qc, kc)], sink)
    o = gqa_values(p, vc)
    return o.reshape(o.shape[0], o.shape[1], -1)


def _split_even(p):
    B, T = p.shape[0], p.shape[1]
    qa, ka, va, qb, kb, vb = jnp.split(p, EVEN_SPLITS, axis=-1)
    return (qa.reshape(B, T, DA_HEADS, 2, DA_QK_DIM), ka.reshape(B, T, DA_HEADS, 2, DA_QK_DIM),
            va.reshape(B, T, DA_HEADS, DA_V_DIM),
            qb.reshape(B, T, WA_KV_HEADS, WA_GROUP, WA_HEAD_DIM),
            kb.reshape(B, T, WA_KV_HEADS, WA_HEAD_DIM), vb.reshape(B, T, WA_KV_HEADS, WA_HEAD_DIM))


def even_mixer(hc, hl, w_in, w_out, lam_vec, sink, lam_init, rope, need_ctx):
    B, L = hl.shape[0], hl.shape[1]
    qa_c, ka_c, va_c, qb_c, kb_c, vb_c = _split_even(hc @ w_in)
    qa_l, ka_l, va_l, qb_l, kb_l, vb_l = _split_even(hl @ w_in)
    qa_l, ka_l, qb_l, kb_l = (apply_axial_rope(t, rope) for t in (qa_l, ka_l, qb_l, kb_l))
    lv = lam_vec.astype(jnp.float32)
    lam = jnp.exp(jnp.sum(lv[0] * lv[1])) - jnp.exp(jnp.sum(lv[2] * lv[3])) + lam_init
    sink_g = sink.astype(jnp.float32).reshape(WA_KV_HEADS, WA_GROUP)

    k_all = jnp.concatenate([ka_c, ka_l], axis=1)
    v_all = jnp.concatenate([va_c, va_l], axis=1)
    nb = L // BLOCK
    qa_blocks = qa_l.reshape((B, nb, BLOCK) + qa_l.shape[2:]).swapaxes(0, 1)
    a_l = lax.map(lambda qq: diff_attn_core(qq, k_all, v_all, lam), qa_blocks)
    a_l = diff_head_out(a_l.swapaxes(0, 1).reshape(B, L, DA_HEADS, DA_V_DIM), lam_init)
    b_l = window_gqa(qb_l, kb_l, vb_l, kb_c, vb_c, sink_g)
    y_l = jnp.concatenate([a_l, b_l], axis=-1) @ w_out
    y_c = None
    if need_ctx:
        a_c = diff_head_out(diff_attn_core(qa_c, ka_c, va_c, lam), lam_init)
        b_c = ctx_gqa(qb_c, kb_c, vb_c, sink_g)
        y_c = jnp.concatenate([a_c, b_c], axis=-1) @ w_out
    return y_c, y_l


def gla_chunked(q, k, v, log_a, S0):
    B, T, H = q.shape[0], q.shape[1], q.shape[2]
    C = GLA_CHUNK
    n = T // C

    def to_chunks(t):
        return t.reshape(B, n, C, H, t.shape[-1]).transpose(1, 0, 3, 2, 4)

    causal = jnp.tril(jnp.ones((C, C), dtype=bool))[:, :, None]

    def step(S, inp):
        qc, kc, vc, gc = inp
        b = jnp.cumsum(gc, axis=2)
        b_last = b[:, :, -1]
        o_inter = jnp.einsum('bhcd,bhde->bhce', qc * jnp.exp(b), S)
        diff = jnp.where(causal, b[:, :, :, None, :] - b[:, :, None, :, :], -jnp.inf)
        A = jnp.einsum('bhid,bhjd,bhijd->bhij', qc, kc, jnp.exp(diff))
        o = o_inter + jnp.einsum('bhij,bhje->bhie', A, vc)
        S_new = jnp.exp(b_last)[..., None] * S + jnp.einsum(
            'bhcd,bhce->bhde', kc * jnp.exp(b_last[:, :, None] - b), vc)
        return S_new, o

    S, o = lax.scan(step, S0, (to_chunks(q), to_chunks(k), to_chunks(v), to_chunks(log_a)))
    o = o.transpose(1, 0, 3, 2, 4).reshape(B, T, H, v.shape[-1])
    return o, S


def centred_conv(x, w, b):
    T = x.shape[1]
    xp = jnp.pad(x, ((0, 0), (CONV_LEFT, LRU_CONV - 1 - CONV_LEFT), (0, 0)))
    return sum(xp[:, j:j + T] * w[j] for j in range(LRU_CONV)) + b


def rglru_coeffs(x, wa, ba, wx, bx, lam):
    f32 = jnp.float32
    xb = x.reshape(x.shape[:-1] + (LRU_BLOCKS, LRU_WIDTH // LRU_BLOCKS))
    r = jax.nn.sigmoid(jnp.einsum('btnc,ncd->btnd', xb, wa.astype(f32)).reshape(x.shape) + ba.astype(f32))
    i = jax.nn.sigmoid(jnp.einsum('btnc,ncd->btnd', xb, wx.astype(f32)).reshape(x.shape) + bx.astype(f32))
    log_a = -LRU_C * r * jax.nn.softplus(-lam.astype(f32))
    a = jnp.exp(log_a)
    u = jnp.sqrt(-jnp.expm1(2.0 * log_a)) * (i * x)
    return a, u


def linear_scan(a, u, h0):
    u = u.at[:, 0].add(a[:, 0] * h0)

    def combine(lhs, rhs):
        return lhs[0] * rhs[0], rhs[0] * lhs[1] + rhs[1]

    _, h = lax.associative_scan(combine, (a, u), axis=1)
    return h


def odd_mixer(hc, hl, w_in, w_out, gate_w, gate_b, gla_g, conv_w, conv_b,
              wa, ba, wx, bx, lam, need_ctx):
    f32 = jnp.float32
    B = hl.shape[0]

    def project(h):
        T = h.shape[1]
        q, k, v, g, lr, zg, zx = jnp.split(h @ w_in, ODD_SPLITS, axis=-1)
        q = q.astype(f32).reshape(B, T, GLA_HEADS, GLA_K_DIM) * (GLA_K_DIM ** -0.5)
        k = k.astype(f32).reshape(B, T, GLA_HEADS, GLA_K_DIM)
        v = v.astype(f32).reshape(B, T, GLA_HEADS, GLA_V_DIM)
        lr = lr.astype(f32).reshape(B, T, 2, GLA_GATE_RANK)
        logit = jnp.einsum('btnr,nrk->btnk', lr, gate_w.astype(f32)) + gate_b.astype(f32)
        log_a = (jax.nn.log_sigmoid(logit) / GLA_GATE_NORM).reshape(B, T, 2, GLA_HEADS, GLA_K_DIM)
        xr = centred_conv(zx.astype(f32), conv_w.astype(f32), conv_b.astype(f32))
        return q, k, v, log_a, g, zg, xr

    qc, kc, vc, lac, gc, zgc, xrc = project(hc)
    ql, kl, vl, lal, gl, zgl, xrl = project(hl)
    S0 = jnp.zeros((B, GLA_HEADS, GLA_K_DIM, GLA_V_DIM), f32)
    h0 = jnp.zeros((B, LRU_WIDTH), f32)
    gla_c, gla_l, lru_c, lru_l = [], [], [], []
    for d in range(2):
        f = (lambda t: jnp.flip(t, axis=1)) if d == 1 else (lambda t: t)
        oc, S_ctx = gla_chunked(f(qc), f(kc), f(vc), f(lac[:, :, d]), S0)
        ol, _ = gla_chunked(f(ql), f(kl), f(vl), f(lal[:, :, d]), S_ctx)
        ac, uc = rglru_coeffs(f(xrc), wa[d], ba[d], wx[d], bx[d], lam[d])
        hcd = linear_scan(ac, uc, h0)
        al, ul = rglru_coeffs(f(xrl), wa[d], ba[d], wx[d], bx[d], lam[d])
        hld = linear_scan(al, ul, hcd[:, -1])
        gla_c.append(f(oc)); gla_l.append(f(ol)); lru_c.append(f(hcd)); lru_l.append(f(hld))

    def assemble(o_gla, g, h_lru, zg, dtype):
        T = o_gla.shape[1]
        y_gla = rms_norm(o_gla, gla_g).reshape(B, T, -1) * jax.nn.silu(g.astype(f32))
        y_lru = h_lru * jax.nn.gelu(zg.astype(f32))
        return jnp.concatenate([y_gla, y_lru], axis=-1).astype(dtype) @ w_out

    y_l = assemble(gla_l[0] + gla_l[1], gl, lru_l[0] + lru_l[1], zgl, hl.dtype)
    y_c = None
    if need_ctx:
        y_c = assemble(gla_c[0] + gla_c[1], gc, lru_c[0] + lru_c[1], zgc, hc.dtype)
    return y_c, y_l


def setup_inputs(seed: int = 0) -> dict:
    key = jax.random.key(seed)
    ks = iter(jax.random.split(key, 40))
    f32 = jnp.float32

    def nrm(shape, scale):
        return jax.random.normal(next(ks), shape, f32) * scale

    blk = LRU_WIDTH // LRU_BLOCKS
    x = nrm((BATCH, SEQ, D_MODEL), 1.0)
    c = nrm((BATCH, D_MODEL), 1.0)
    ctx = nrm((BATCH, CTX_LEN, D_MODEL), 1.0)
    c_ctx = nrm((D_MODEL,), 1.0)
    ada_w = nrm((DEPTH, D_MODEL, 6 * D_MODEL), 0.5 * D_MODEL ** -0.5)
    ada_b = nrm((DEPTH, 6 * D_MODEL), 0.02)
    norm_g = 1.0 + nrm((DEPTH, 2, D_MODEL), 0.02)
    even_w_in = nrm((N_EVEN, D_MODEL, EVEN_IN), D_MODEL ** -0.5)
    even_w_out = nrm((N_EVEN, EVEN_MIX, D_MODEL), EVEN_MIX ** -0.5)
    diff_lam = nrm((N_EVEN, 4, DA_QK_DIM), 0.1)
    win_sink = nrm((N_EVEN, WA_Q_HEADS), 0.5)
    odd_w_in = nrm((N_ODD, D_MODEL, ODD_IN), D_MODEL ** -0.5)
    odd_w_out = nrm((N_ODD, ODD_MIX, D_MODEL), ODD_MIX ** -0.5)
    gla_gate_w = nrm((N_ODD, 2, GLA_GATE_RANK, GLA_HEADS * GLA_K_DIM), GLA_GATE_RANK ** -0.5)
    gla_gate_b = nrm((N_ODD, 2, GLA_HEADS * GLA_K_DIM), 0.1)
    gla_norm_g = 1.0 + nrm((N_ODD, GLA_V_DIM), 0.02)
    lru_conv_w = nrm((N_ODD, LRU_CONV, LRU_WIDTH), LRU_CONV ** -0.5)
    lru_conv_b = nrm((N_ODD, LRU_WIDTH), 0.02)
    lru_wa = nrm((N_ODD, 2, LRU_BLOCKS, blk, blk), blk ** -0.5)
    lru_ba = nrm((N_ODD, 2, LRU_WIDTH), 0.02)
    lru_wx = nrm((N_ODD, 2, LRU_BLOCKS, blk, blk), blk ** -0.5)
    lru_bx = nrm((N_ODD, 2, LRU_WIDTH), 0.02)
    u = jax.random.uniform(next(ks), (N_ODD, 2, LRU_WIDTH), f32, 0.9, 0.999)
    s = u ** (1.0 / LRU_C)
    lru_lam = jnp.log(s) - jnp.log1p(-s)
    ffn_w_in = nrm((DEPTH, D_MODEL, 2 * D_FF), D_MODEL ** -0.5)
    ffn_w_out = nrm((DEPTH, D_FF, D_MODEL), D_FF ** -0.5)
    final_g = 1.0 + nrm((D_MODEL,), 0.02)
    return {'x': x, 'c': c, 'ctx': ctx, 'c_ctx': c_ctx, 'ada_w': ada_w, 'ada_b': ada_b,
            'norm_g': norm_g, 'even_w_in': even_w_in, 'even_w_out': even_w_out,
            'diff_lam': diff_lam, 'win_sink': win_sink, 'odd_w_in': odd_w_in,
            'odd_w_out': odd_w_out, 'gla_gate_w': gla_gate_w, 'gla_gate_b': gla_gate_b,
            'gla_norm_g': gla_norm_g, 'lru_conv_w': lru_conv_w, 'lru_conv_b': lru_conv_b,
            'lru_wa': lru_wa, 'lru_ba': lru_ba, 'lru_wx': lru_wx, 'lru_bx': lru_bx,
            'lru_lam': lru_lam, 'ffn_w_in': ffn_w_in, 'ffn_w_out': ffn_w_out, 'final_g': final_g}


def reference(x, c, ctx, c_ctx, ada_w, ada_b, norm_g, even_w_in, even_w_out, diff_lam, win_sink,
              odd_w_in, odd_w_out, gla_gate_w, gla_gate_b, gla_norm_g, lru_conv_w, lru_conv_b,
              lru_wa, lru_ba, lru_wx, lru_bx, lru_lam, ffn_w_in, ffn_w_out, final_g):
    rope = rope_tables(x.shape[1])
    xl, xc = x, ctx
    cond_l = jax.nn.silu(c)[:, None, :]
    cond_c = jax.nn.silu(c_ctx)
    for li in range(DEPTH):
        last = li == DEPTH - 1
        sh1, sc1, g1, sh2, sc2, g2 = jnp.split(cond_l @ ada_w[li] + ada_b[li], 6, axis=-1)
        csh1, csc1, cg1, csh2, csc2, cg2 = jnp.split(cond_c @ ada_w[li] + ada_b[li], 6, axis=-1)
        hl = modulate(rms_norm(xl, norm_g[li, 0]), sh1, sc1)
        hc = modulate(rms_norm(xc, norm_g[li, 0]), csh1, csc1)
        if li % 2 == 0:
            e = li // 2
            lam_init = 0.8 - 0.6 * math.exp(-0.3 * li)
            yc, yl = even_mixer(hc, hl, even_w_in[e], even_w_out[e], diff_lam[e], win_sink[e],
                                lam_init, rope, not last)
        else:
            o = li // 2
            yc, yl = odd_mixer(hc, hl, odd_w_in[o], odd_w_out[o], gla_gate_w[o], gla_gate_b[o],
                               gla_norm_g[o], lru_conv_w[o], lru_conv_b[o], lru_wa[o], lru_ba[o],
                               lru_wx[o], lru_bx[o], lru_lam[o], not last)
        xl = xl + g1 * yl
        xl = xl + g2 * swiglu(modulate(rms_norm(xl, norm_g[li, 1]), sh2, sc2), ffn_w_in[li], ffn_w_out[li])
        if not last:
            xc = xc + cg1 * yc
            xc = xc + cg2 * swiglu(modulate(rms_norm(xc, norm_g[li, 1]), csh2, csc2),
                                   ffn_w_in[li], ffn_w_out[li])
    return rms_norm(xl, final_g)
```

```python
import math
import numpy as np
import ml_dtypes
import concourse.bass as bass
import concourse.mybir as mybir
from concourse.bass_utils import run_bass_kernel_spmd

F32, BF16 = mybir.dt.float32, mybir.dt.bfloat16
AF = mybir.ActivationFunctionType
ALU = mybir.AluOpType

D = 1024
L = 8192
LC = 256
NT = L + LC
NCORES = 4
DFF = 2816
EPS = 1e-6
CHUNKS = [(0, 256, True)] + [(256 + 512 * i, 512, False) for i in range(16)]
NEG = -240000.0
ENGS = ['pe', 'act', 'dve', 'pool', 'sp']
NDMA = {'sp': 40, 'pool': 16}


import types


def _freeze(fn):
    if fn is None or fn.__closure__ is None:
        return fn
    cells = []
    for c in fn.__closure__:
        try:
            cells.append(types.CellType(c.cell_contents))
        except ValueError:
            cells.append(c)
    return types.FunctionType(fn.__code__, fn.__globals__, fn.__name__, fn.__defaults__, tuple(cells))


class Tr:
    def __init__(s, nc):
        s.nc = nc
        s.ops = {e: [] for e in ENGS}
        s.lastw = {}
        s.rd = {}
        s.ndma = {'sp': 0, 'pool': 0}
        s.dma_since = []
        s.arena = 16512
        s.names = 0

    END = 229248

    def alloc(s, shape, dt, name=None):
        nb = int(np.prod(shape[1:])) * (4 if dt == F32 else 2)
        nb = (nb + 63) // 64 * 64
        if not hasattr(s, 'free'):
            s.free = [[s.arena, s.END]]
            s.pf_next = []
            s.keep_next = set()
            s.in_pf = False
        off = None
        if s.in_pf:
            for seg in reversed(s.free):
                if seg[1] - seg[0] >= nb:
                    seg[1] -= nb
                    off = seg[1]
                    break
            if off is not None:
                s.pf_next.append((off, off + nb))
        else:
            for seg in s.free:
                if seg[1] - seg[0] >= nb:
                    off = seg[0]
                    seg[0] += nb
                    break
        assert off is not None, ("sbuf overflow", nb, s.free)
        s.names += 1
        return s.nc.alloc_sbuf_tensor_at(f"{name or 't'}_{s.names}", list(shape), dt, offset=off)

    def prefetch(s, fn):
        if not hasattr(s, 'free'):
            s.alloc([128, 16], F32)
        s.in_pf = True
        r = fn()
        s.in_pf = False
        return r

    def op(s, eng, fn, reads=(), writes=(), dma=False):
        raw, oth = set(), set()
        for u in reads:
            w = s.lastw.get(u)
            if w is not None:
                raw.add(w)
        for u in writes:
            w = s.lastw.get(u)
            if w is not None:
                oth.add(w)
            r = s.rd.get(u)
            if r:
                for e2, i2 in r[0].items():
                    oth.add((e2, i2))
                for x in r[1]:
                    oth.add(x)
        idx = len(s.ops[eng])
        me = (eng, idx)
        deps = set()
        for dset, israw in ((raw, True), (oth, False)):
            for (e2, i2) in dset:
                d2 = s.ops[e2][i2]
                if e2 == eng and not d2['dma'] and not dma:
                    if eng == 'pe' or not israw:
                        continue
                deps.add((e2, i2))
        rec = dict(fn=_freeze(fn), deps=deps, inc=False, dma=dma)
        if dma:
            rec['k'] = s.ndma[eng]
            rec['q'] = eng
            s.ndma[eng] += 1
            if getattr(s, 'in_pf', False):
                for u in writes:
                    s.keep_next.add(u)
            else:
                s.dma_since.append(me)
        s.ops[eng].append(rec)
        for u in reads:
            r = s.rd.setdefault(u, ({}, []))
            if dma:
                r[1].append(me)
            else:
                r[0][eng] = idx
        for u in writes:
            s.lastw[u] = me
            s.rd[u] = ({}, [])
        return me

    def dma(s, q, out, in_, reads=(), writes=()):
        return s.op(q, lambda e: e.dma_start(out=out, in_=in_), reads, writes, dma=True)

    def barrier(s, reset_to=None):
        lasts = []
        for e in ENGS:
            for i in range(len(s.ops[e]) - 1, -1, -1):
                if s.ops[e][i]['fn'] is not None:
                    if not s.ops[e][i]['dma']:
                        lasts.append((e, i))
                    break
        deps = set(lasts) | set(s.dma_since)
        for e in ENGS:
            s.ops[e].append(dict(fn=None, deps=set(d for d in deps), inc=False, dma=False))
        s.dma_since = []
        keep = getattr(s, 'keep_next', set())
        s.lastw = {u: w for u, w in s.lastw.items() if u in keep}
        s.rd = {u: ({}, []) for u in s.lastw}
        s.keep_next = set()
        if reset_to is not None:
            segs = [[reset_to, s.END]]
            for (lo, hi) in sorted(getattr(s, 'pf_next', [])):
                new = []
                for a_, b_ in segs:
                    if hi <= a_ or lo >= b_:
                        new.append([a_, b_])
                    else:
                        if a_ < lo:
                            new.append([a_, lo])
                        if hi < b_:
                            new.append([hi, b_])
                segs = new
            s.free = segs
            s.pf_next = []

    def emit(s):
        nc = s.nc
        for e in ENGS:
            for rec in s.ops[e]:
                for (e2, i2) in rec['deps']:
                    d2 = s.ops[e2][i2]
                    if not d2['dma']:
                        d2['inc'] = True
        for e in ENGS:
            c = 0
            for rec in s.ops[e]:
                if rec['inc']:
                    c += 1
                    rec['cnt'] = c
        import contextlib
        with contextlib.ExitStack() as st:
            esem = {e: st.enter_context(nc.semaphore(f"s_{e}")) for e in ENGS if e != 'sp'}
            esem['sp'] = st.enter_context(nc.semaphore("s_sp"))
            dsem = {q: [st.enter_context(nc.semaphore(f"d_{q}_{i}")) for i in range(n)] for q, n in NDMA.items()}
            block = st.enter_context(nc.Block())

            def run(ename):
                def body(eng):
                    waited = {}
                    for rec in s.ops[ename]:
                        need = {}
                        for (e2, i2) in rec['deps']:
                            d2 = s.ops[e2][i2]
                            if d2['dma']:
                                k, q = d2['k'], d2['q']
                                key, val = ('d', q, k % NDMA[q]), 16 * (k // NDMA[q] + 1)
                            else:
                                key, val = ('e', e2), d2['cnt']
                            if val > need.get(key, 0):
                                need[key] = val
                        if rec['dma'] and rec['k'] >= NDMA[rec['q']]:
                            nq = NDMA[rec['q']]
                            key, val = ('d', rec['q'], rec['k'] % nq), 16 * (rec['k'] // nq)
                            if val > need.get(key, 0):
                                need[key] = val
                        todo = []
                        for key, val in need.items():
                            if waited.get(key, 0) >= val:
                                continue
                            waited[key] = val
                            sem = dsem[key[1]][key[2]] if key[0] == 'd' else esem[key[1]]
                            todo.append((sem, val))
                        attach = None
                        if todo and rec['fn'] is not None and not rec['dma'] and ename != 'pe':
                            attach = todo.pop()
                        for sem, val in todo:
                            eng.wait_ge(sem, val)
                        if rec['fn'] is None:
                            continue
                        ins = rec['fn'](eng)
                        if attach is not None:
                            ins._wait_ge(attach[0], attach[1])
                        if rec['dma']:
                            ins.then_inc(dsem[rec['q']][rec['k'] % NDMA[rec['q']]], 16)
                        elif rec['inc']:
                            ins.then_inc(esem[ename], 1)
                return body

            block.tensor(run('pe'))
            block.scalar(run('act'))
            block.vector(run('dve'))
            block.gpsimd(run('pool'))
            block.sync(run('sp'))


def build(debug=False, stop_after=None):
    nc = bass.Bass("TRN2", target_bir_lowering=False)
    T = Tr(nc)

    def din(name, shape, dt=F32):
        return nc.dram_tensor(name, list(shape), dt, kind="ExternalInput")

    def dscr(name, shape, dt):
        return nc.dram_tensor(name, list(shape), dt, kind="Internal")

    xall = din("xall", [NT, D])
    ccols = din("ccols", [128, 8, 2])
    ada_w = din("ada_w", [2, D, 6 * D])
    ada_b2 = din("ada_b2", [2, 2, 6 * D])
    normg = din("normg", [128, 2, 2, 8])
    finalg = din("finalg", [128, 8])
    w0 = din("w0", [D, 3968])
    wo0A = din("wo0A", [128, 4, D])
    wo0B = din("wo0B", [64, 8, D])
    difflam = din("difflam", [128, 4, 64])
    sinkc = din("sinkc", [128, 8])
    ropeC = din("ropeC", [128, NT])
    ropeS = din("ropeS", [128, NT])
    maskb = din("maskb", [2, 128, 512], BF16)
    ident_in = din("ident", [128, 128])
    ffn_in = din("ffn_in", [2, D, 2 * DFF])
    ffn_out = din("ffn_out", [2, DFF, D])
    w1 = din("w1", [D, 2592])
    wo1 = din("wo1", [D, D])
    gw_in = din("gw", [2, 16, 256])
    gb2_in = din("gb2", [2, 64, 512])
    glag_in = din("glag", [128, 1])
    cw_in = din("cw", [128, 4, 4])
    cb_in = din("cb", [128, 4])
    lwa = din("lwa", [2, 8, 64, 64])
    lwx = din("lwx", [2, 8, 64, 64])
    lba = din("lba", [128, 2, 4])
    lbx = din("lbx", [128, 2, 4])
    llam = din("llam", [128, 2, 4])
    tri_in = din("tri", [4, 64, 64])
    mask4_in = din("mask4", [2, 64, 256])
    out = nc.dram_tensor("out", [L, D], F32, kind="ExternalOutput")
    QS = dscr("QS", [2, 4, 64, NT], BF16)
    KS = dscr("KS", [2, 4, 64, NT], BF16)
    KD = dscr("KD", [2, NT, 256], BF16)
    V1 = dscr("V1", [NT, 512], BF16)
    ZX = dscr("ZX", [4, 128, NT], F32)
    GZ = dscr("GZ", [4, 128, NT], BF16)
    SG = dscr("SG", [4, 128, NT], BF16)
    HL = dscr("HL", [4, 128, NT], F32)
    OG = dscr("OG", [4, 128, NT], F32)
    YT = dscr("YT", [8, 128, NT], BF16)

    R = dscr("R", [128, 8, NT], F32)
    QAT = dscr("QAT", [4, 128, NT], BF16)
    KAT = dscr("KAT", [4, 128, NT], BF16)
    QBT = dscr("QBT", [4, 128, NT], BF16)
    KBT = dscr("KBT", [128, NT], BF16)
    VA = dscr("VA", [NT, 512], BF16)
    VB = dscr("VB", [NT, 128], BF16)
    ATd = dscr("ATd", [4, 128, NT], BF16)
    BTd = dscr("BTd", [8, 64, NT], BF16)
    H2 = dscr("H2", [128, 8, NT], BF16)
    ACTd = dscr("ACTd", [22, 128, NT], BF16)
    dbg = nc.dram_tensor("dbg", [128, 8, NT], F32, kind="ExternalOutput") if debug else None
    dbg2 = nc.dram_tensor("dbg2", [8, 128, NT], BF16, kind="ExternalOutput") if debug else None

    P2 = [nc.alloc_psum_tensor(f"ps{i}", [128, 1024], F32) for i in range(4)]
    PS = []
    for i in range(4):
        PS.append(P2[i][:, 0:512])
        PS.append(P2[i][:, 512:1024])

    def psu(i):
        return ('ps', i)

    ident = T.alloc([128, 128], F32, 'ident')
    identb = T.alloc([128, 128], BF16, 'identb')
    onesb = T.alloc([128, 128], BF16, 'onesb')
    modT = T.alloc([128, 2, 48, 2], F32, 'modT')
    Acol = T.alloc([128, 2, 2, 8, 2], F32, 'Acol')
    ngc = T.alloc([128, 2, 2, 8], F32, 'ngc')
    fgc = T.alloc([128, 8], F32, 'fgc')
    neglam = T.alloc([128, 1], F32, 'neglam')
    esink = T.alloc([128, 8], F32, 'esink')
    T.dma('sp', ident[:], ident_in.ap(), writes=['ident'])
    T.dma('sp', ngc[:], normg.ap(), writes=['ngc'])
    T.dma('sp', fgc[:], finalg.ap(), writes=['fgc'])
    T.op('dve', lambda e: e.tensor_copy(out=identb[:], in_=ident[:]), ['ident'], ['identb'])
    T.op('dve', lambda e: e.memset(onesb[:], 1.0), [], ['onesb'])
    decs = T.alloc([64, 2, 4, 132], F32, 'decs')
    PERSIST = T.free[0][0]

    def eps_bias(e):
        return EPS

    def phase0():
        cc = T.alloc([128, 8, 2], F32)
        cond = T.alloc([128, 8, 2], F32)
        T.dma('sp', cc[:], ccols.ap(), writes=['cc'])
        T.op('act', lambda e: e.activation(out=cond[:], in_=cc[:], func=AF.Silu), ['cc'], ['cond'])
        wb = [T.alloc([128, 3072], F32) for _ in range(3)]
        modrow = T.alloc([2, 6144], F32)
        adab = T.alloc([2, 6144], F32)
        nload = 0
        for li in range(2):
            T.dma('sp', adab[:], ada_b2.ap()[li], writes=['adab'])
            for half in range(2):
                for kc in range(8):
                    w = wb[nload % 3]
                    wu = ('wb', nload % 3)
                    nload += 1
                    T.dma('sp', w[:], ada_w.ap()[li, kc * 128:(kc + 1) * 128, half * 3072:(half + 1) * 3072],
                          writes=[wu])
                    for cb in range(6):
                        T.op('pe', lambda e, w=w, cb=cb, kc=kc: e.matmul(
                            PS[cb][0:2, :], lhsT=cond[:, kc, :], rhs=w[:, cb * 512:(cb + 1) * 512],
                            start=(kc == 0), stop=(kc == 7)), [wu, 'cond'], [psu(cb)])
                for cb in range(6):
                    c0 = half * 3072 + cb * 512
                    T.op('dve', lambda e, cb=cb, c0=c0: e.tensor_tensor(
                        out=modrow[:, c0:c0 + 512], in0=PS[cb][0:2, :], in1=adab[:, c0:c0 + 512], op=ALU.add),
                        [psu(cb), 'adab'], ['modrow'])
            for blk in range(48):
                T.op('pe', lambda e, blk=blk: e.transpose(
                    PS[6][:, blk * 2:blk * 2 + 2], modrow[0:2, blk * 128:(blk + 1) * 128], ident[0:2, 0:2]),
                    ['modrow', 'ident'], [psu(6)])
            T.op('dve', lambda e, li=li: e.tensor_copy(
                out=modT[:, li].rearrange("p a b -> p (a b)"), in_=PS[6][:, 0:96]), [psu(6)], ['modT'])
            for n in range(2):
                for w_ in range(2):
                    T.op('dve', lambda e, li=li, n=n, w_=w_: e.scalar_tensor_tensor(
                        out=Acol[:, li, n, :, w_], in0=modT[:, li, (3 * n + 1) * 8:(3 * n + 2) * 8, w_], scalar=1.0,
                        in1=ngc[:, li, n, :], op0=ALU.add, op1=ALU.mult), ['modT', 'ngc'], ['Acol'])
        dl = T.alloc([128, 4, 64], F32)
        pr = T.alloc([128, 2, 64], F32)
        sm = T.alloc([128, 2], F32)
        ex = T.alloc([128, 2], F32)
        T.dma('sp', dl[:], difflam.ap(), writes=['dl'])
        T.op('dve', lambda e: e.tensor_tensor(out=pr[:, 0, :], in0=dl[:, 0, :], in1=dl[:, 1, :], op=ALU.mult),
             ['dl'], ['pr0'])
        T.op('dve', lambda e: e.tensor_tensor(out=pr[:, 1, :], in0=dl[:, 2, :], in1=dl[:, 3, :], op=ALU.mult),
             ['dl'], ['pr1'])
        T.op('dve', lambda e: e.reduce_sum(out=sm[:], in_=pr[:], axis=mybir.AxisListType.X), ['pr0', 'pr1'], ['sm'])
        T.op('act', lambda e: e.activation(out=ex[:], in_=sm[:], func=AF.Exp), ['sm'], ['ex'])
        T.op('dve', lambda e: e.scalar_tensor_tensor(out=neglam[:], in0=ex[:, 1:2], scalar=-0.2, in1=ex[:, 0:1],
                                                     op0=ALU.add, op1=ALU.subtract), ['ex'], ['neglam'])
        sk = T.alloc([128, 8], F32)
        T.dma('sp', sk[:], sinkc.ap(), writes=['sk'])
        T.op('act', lambda e: e.activation(out=esink[:], in_=sk[:], func=AF.Exp), ['sk'], ['esink'])

    def rmsnorm_mod(xT, xu, N, li, n, who, hT, hu, tmp, tu, sq, squ, rstd, ru, psb):
        T.op('act', lambda e: e.activation(out=sq[:, :, :N], in_=xT[:, :, :N], func=AF.Square), [xu], [squ])
        for j in range(8):
            T.op('pe', lambda e, j=j: e.matmul(PS[psb][:, :N], lhsT=onesb[:], rhs=sq[:, j, :N],
                                               start=(j == 0), stop=(j == 7)), [squ, 'onesb'], [psu(psb)])
        T.op('act', lambda e: e.activation(out=rstd[:, :N], in_=PS[psb][:, :N], func=AF.Sqrt, bias=EPS,
                                           scale=1.0 / D), [psu(psb)], [ru])
        T.op('dve', lambda e: e.reciprocal(out=rstd[:, :N], in_=rstd[:, :N]), [ru], [ru])
        for j in range(8):
            if who is None:
                a_ap = fgc[:, j:j + 1]
            else:
                a_ap = Acol[:, li, n, j, who:who + 1]
            T.op('dve', lambda e, j=j, a_ap=a_ap: e.scalar_tensor_tensor(
                out=tmp[:, j, :N], in0=xT[:, j, :N], scalar=a_ap, in1=rstd[:, :N], op0=ALU.mult, op1=ALU.mult),
                [xu, ru, 'Acol', 'fgc'], [tu])
            if who is not None:
                b_ap = modT[:, li, (3 * n) * 8 + j, who:who + 1]
                T.op('act', lambda e, j=j, b_ap=b_ap: e.activation(
                    out=hT[:, j, :N], in_=tmp[:, j, :N], func=AF.Identity, bias=b_ap, scale=1.0),
                    [tu, 'modT'], [hu])

    def phase1():
        W = T.alloc([128, 8, 3968], BF16, 'w0')
        for kc in range(8):
            T.dma('pool', W[:, kc, :], w0.ap()[kc * 128:(kc + 1) * 128, :], writes=[('W', kc)])
        Wu = [('W', kc) for kc in range(8)]
        xtok = [T.alloc([128, 4, D], F32) for _ in range(2)]
        xT = [T.alloc([128, 8, 512], F32) for _ in range(2)]
        hT = [T.alloc([128, 8, 512], BF16) for _ in range(2)]
        tmp = T.alloc([128, 8, 512], F32)
        sq = T.alloc([128, 8, 512], BF16)
        rstd = T.alloc([128, 512], F32)
        rc = [T.alloc([128, 512], F32) for _ in range(2)]
        rs = [T.alloc([128, 512], F32) for _ in range(2)]
        t1 = [T.alloc([128, 512], F32) for _ in range(2)]
        t2 = [T.alloc([128, 512], F32) for _ in range(2)]
        ob = [T.alloc([128, 512], BF16) for _ in range(4)]
        vb = [T.alloc([128, 512], BF16) for _ in range(2)]
        nrope = 0
        nv = 0
        def prep(ci):
            st, N, isctx = CHUNKS[ci]
            b = ci % 2
            who = 1 if isctx else 0
            ntt = N // 128
            for tt in range(ntt):
                T.dma('sp', xtok[b][:, tt, :], xall.ap()[st + tt * 128: st + (tt + 1) * 128, :],
                      writes=[('xtok', b, tt)])
            T.dma('sp', rc[b][:, :N], ropeC.ap()[:, st:st + N], writes=[('rc', b)])
            T.dma('sp', rs[b][:, :N], ropeS.ap()[:, st:st + N], writes=[('rs', b)])
            for j in range(8):
                pb = j % 2
                for tt in range(ntt):
                    T.op('pe', lambda e, j=j, tt=tt, pb=pb, b=b: e.transpose(
                        PS[pb][:, tt * 128:(tt + 1) * 128], xtok[b][:, tt, j * 128:(j + 1) * 128], ident[:]),
                        [('xtok', b, tt), 'ident'], [psu(pb)])
                T.op('act' if j % 2 else 'dve', (lambda e, j=j, pb=pb, b=b, N=N: e.activation(
                    out=xT[b][:, j, :N], in_=PS[pb][:, :N], func=AF.Copy)) if j % 2 else
                    (lambda e, j=j, pb=pb, b=b, N=N: e.tensor_copy(out=xT[b][:, j, :N], in_=PS[pb][:, :N])),
                    [psu(pb)], [('xT', b)])
            T.dma('sp', R.ap()[:, :, st:st + N], xT[b][:, :, :N], reads=[('xT', b)], writes=[('R', ci)])
            rmsnorm_mod(xT[b], ('xT', b), N, 0, 0, who, hT[b], ('hT', b), tmp, 'tmp', sq, 'sq', rstd, 'rstd', 2)

        def proj(ci):
            nonlocal nrope, nv
            st, N, isctx = CHUNKS[ci]
            b = ci % 2
            ntt = N // 128
            specs = []
            for h in range(4):
                specs.append((h, 4 + h, QAT.ap()[h]))
            for h in range(4):
                specs.append((8 + h, 12 + h, KAT.ap()[h]))
            for g in range(4):
                specs.append((16 + g, 20 + g, QBT.ap()[g]))
            specs.append((24, 25, KBT.ap()))
            for (bp, bq, dst) in specs:
                pp = 3 + 2 * (nrope % 2)
                k2 = nrope % 2
                k4 = nrope % 4
                nrope += 1
                for (blk, pbk) in ((bp, pp), (bq, pp + 1)):
                    for kc in range(8):
                        T.op('pe', lambda e, blk=blk, pbk=pbk, kc=kc, b=b, N=N: e.matmul(
                            PS[pbk][:, :N], lhsT=W[:, kc, blk * 128:(blk + 1) * 128], rhs=hT[b][:, kc, :N],
                            start=(kc == 0), stop=(kc == 7)), [('hT', b), Wu[kc]], [psu(pbk)])
                T.op('dve', lambda e, pp=pp, k2=k2, b=b, N=N: e.tensor_tensor(
                    out=t1[k2][:, :N], in0=PS[pp][:, :N], in1=rc[b][:, :N], op=ALU.mult),
                    [psu(pp), ('rc', b)], [('t1', k2)])
                T.op('dve', lambda e, pp=pp, k2=k2, b=b, N=N: e.tensor_tensor(
                    out=t2[k2][:, :N], in0=PS[pp + 1][:, :N], in1=rs[b][:, :N], op=ALU.mult),
                    [psu(pp + 1), ('rs', b)], [('t2', k2)])
                T.op('pool', lambda e, k2=k2, k4=k4, N=N: e.tensor_tensor(
                    out=ob[k4][:, :N], in0=t1[k2][:, :N], in1=t2[k2][:, :N], op=ALU.add),
                    [('t1', k2), ('t2', k2)], [('ob', k4)])
                T.dma('sp', dst[:, st:st + N], ob[k4][:, :N], reads=[('ob', k4)], writes=[('proj', ci)])
            for tt in range(ntt):
                k2 = nv % 2
                nv += 1
                for kc in range(8):
                    T.op('pe', lambda e, kc=kc, tt=tt, b=b: e.matmul(
                        PS[7][:, :], lhsT=hT[b][:, kc, tt * 128:(tt + 1) * 128], rhs=W[:, kc, 3328:3840],
                        start=(kc == 0), stop=(kc == 7)), [('hT', b), Wu[kc]], [psu(7)])
                T.op('act', lambda e, k2=k2: e.activation(out=vb[k2][:, :], in_=PS[7][:, :], func=AF.Copy),
                     [psu(7)], [('vb', k2)])
                T.dma('sp', VA.ap()[st + tt * 128: st + (tt + 1) * 128, :], vb[k2][:, :],
                      reads=[('vb', k2)], writes=[('proj', ci)])
                k2 = nv % 2
                nv += 1
                for kc in range(8):
                    T.op('pe', lambda e, kc=kc, tt=tt, b=b: e.matmul(
                        PS[7][:, 0:128], lhsT=hT[b][:, kc, tt * 128:(tt + 1) * 128], rhs=W[:, kc, 3840:3968],
                        start=(kc == 0), stop=(kc == 7)), [('hT', b), Wu[kc]], [psu(7)])
                T.op('act', lambda e, k2=k2: e.activation(out=vb[k2][:, 0:128], in_=PS[7][:, 0:128], func=AF.Copy),
                     [psu(7)], [('vb', k2)])
                T.dma('sp', VB.ap()[st + tt * 128: st + (tt + 1) * 128, :], vb[k2][:, 0:128],
                      reads=[('vb', k2)], writes=[('proj', ci)])

        prep(0)
        for ci in range(len(CHUNKS)):
            if ci + 1 < len(CHUNKS):
                prep(ci + 1)
            proj(ci)

    ALLPROJ = [('proj', ci) for ci in range(len(CHUNKS))]

    def phase2a():
        VAs = T.alloc([128, 66, 512], BF16, 'VAs')
        for t0 in range(0, 66, 11):
            T.dma('sp', VAs[:, t0:t0 + 11, :],
                  VA.ap()[t0 * 128:(t0 + 11) * 128, :].rearrange("(t p) c -> p t c", p=128),
                  reads=ALLPROJ, writes=[('VAs', t0)])
        VAu = [('VAs', t0) for t0 in range(0, 66, 11)]
        onesf = T.alloc([128, 128], F32)
        T.op('pool', lambda e: e.memset(onesf[:], 1.0), [], ['onesf'])
        Ks = [T.alloc([128, NT], BF16) for _ in range(2)]
        Qs = [T.alloc([128, 512], BF16) for _ in range(2)]
        NPB = 6
        Pb = [T.alloc([128, 2, 512], BF16) for _ in range(NPB)]
        acc = T.alloc([128, 2, 512], F32)
        rD = T.alloc([128, 2, 512], F32)
        rD2 = T.alloc([128, 2, 512], F32)
        u_ = T.alloc([128, 2, 512], F32)
        o_ = T.alloc([128, 512], F32)
        sqo = T.alloc([128, 512], BF16)
        rr = T.alloc([128, 512], F32)
        ao = [T.alloc([128, 512], BF16) for _ in range(2)]
        st8 = {'nq': 0, 'ns': 0}
        pending = []

        def tick():
            for p_ in pending:
                p_[0] -= 1
            while pending and pending[0][0] <= 0:
                pending.pop(0)[1]()

        def flush():
            while pending:
                pending.pop(0)[1]()

        for h in range(4):
            kb = h % 2
            T.dma('sp', Ks[kb][:, :], KAT.ap()[h], reads=ALLPROJ, writes=[('Ks', kb)])
            for ci, (st, N, isctx) in enumerate(CHUNKS):
                qb = st8['nq'] % 2
                st8['nq'] += 1
                T.dma('sp', Qs[qb][:, :N], QAT.ap()[h][:, st:st + N], reads=ALLPROJ, writes=[('Qs', qb)])
                kts = [0, 1] if isctx else list(range(66))
                dve_k = [kt for kt in kts if kt % 2 == 0]
                pe_k = [kt for kt in kts if kt % 2 == 1]

                def qk(kt, sb):
                    for m in range(2):
                        T.op('pe', lambda e, m=m: e.matmul(
                            P2[sb][:, m * 512:m * 512 + N], lhsT=Ks[kb][m * 64:(m + 1) * 64, kt * 128:(kt + 1) * 128],
                            rhs=Qs[qb][m * 64:(m + 1) * 64, :N], start=True, stop=True),
                            [('Ks', kb), ('Qs', qb)], [('p2', sb)])

                qk(kts[0], st8['ns'] % 2)
                for i, kt in enumerate(kts):
                    first, last = kt == kts[0], kt == kts[-1]
                    sb = st8['ns'] % 2
                    pbf = st8['ns'] % NPB
                    st8['ns'] += 1
                    tick()
                    if i + 1 < len(kts):
                        qk(kts[i + 1], st8['ns'] % 2)
                    T.op('act', lambda e: e.activation(
                        out=Pb[pbf][:, :, :N], in_=P2[sb][:, :].rearrange("p (m q) -> p m q", m=2)[:, :, :N],
                        func=AF.Exp, scale=0.125), [('p2', sb)], [('Pb', pbf)])
                    for m in range(2):
                        T.op('pe', lambda e, m=m: e.matmul(
                            PS[4 + m][:, :N], lhsT=VAs[:, kt, h * 128:(h + 1) * 128], rhs=Pb[pbf][:, m, :N],
                            start=first, stop=last), [('Pb', pbf)] + VAu, [psu(4 + m)])
                    if kt % 2 == 1:
                        for m in range(2):
                            T.op('pe', lambda e, m=m: e.matmul(
                                PS[6 + m][:, :N], lhsT=onesb[:], rhs=Pb[pbf][:, m, :N],
                                start=(kt == pe_k[0]), stop=(kt == pe_k[-1])), [('Pb', pbf), 'onesb'], [psu(6 + m)])
                    elif kt == dve_k[0]:
                        T.op('dve', lambda e: e.tensor_copy(out=acc[:, :, :N], in_=Pb[pbf][:, :, :N]),
                             [('Pb', pbf)], ['acc'])
                    else:
                        T.op('dve', lambda e: e.tensor_tensor(
                            out=acc[:, :, :N], in0=acc[:, :, :N], in1=Pb[pbf][:, :, :N], op=ALU.add),
                            [('Pb', pbf), 'acc'], ['acc'])
                flush()
                T.op('dve', lambda e: e.tensor_copy(out=u_[:, 0, :N], in_=PS[4][:, :N]), [psu(4)], [('u_', 0)])
                T.op('act', lambda e: e.activation(out=u_[:, 1, :N], in_=PS[5][:, :N], func=AF.Copy),
                     [psu(5)], [('u_', 1)])
                T.op('act', lambda e: e.activation(
                    out=rD[:, :, :N], in_=P2[3][:, :].rearrange("p (m q) -> p m q", m=2)[:, :, :N], func=AF.Copy),
                    [psu(6), psu(7)], ['rD'])
                fsb = st8['ns'] % 2
                st8['ns'] += 1
                for m in range(2):
                    T.op('pe', lambda e, m=m: e.matmul(P2[fsb][:, m * 512:m * 512 + N], lhsT=onesf[:],
                                                       rhs=acc[:, m, :N], start=True, stop=True),
                         ['acc', 'onesf'], [('p2', fsb)])
                T.op('act', lambda e: e.activation(
                    out=rD2[:, :, :N], in_=P2[fsb][:, :].rearrange("p (m q) -> p m q", m=2)[:, :, :N], func=AF.Copy),
                    [('p2', fsb)], ['rD2'])

                def stageB(N=N):
                    T.op('dve', lambda e: e.tensor_tensor(out=rD[:, :, :N], in0=rD[:, :, :N], in1=rD2[:, :, :N],
                                                          op=ALU.add), ['rD', 'rD2'], ['rD'])
                    T.op('dve', lambda e: e.reciprocal(out=rD[:, :, :N], in_=rD[:, :, :N]), ['rD'], ['rD'])
                    for m in range(2):
                        T.op('dve', lambda e, m=m: e.tensor_tensor(
                            out=u_[:, m, :N], in0=u_[:, m, :N], in1=rD[:, m, :N], op=ALU.mult),
                            [('u_', m), 'rD'], [('u_', m)])
                    T.op('dve', lambda e: e.scalar_tensor_tensor(
                        out=o_[:, :N], in0=u_[:, 1, :N], scalar=neglam[:, 0:1], in1=u_[:, 0, :N],
                        op0=ALU.mult, op1=ALU.add), [('u_', 0), ('u_', 1), 'neglam'], ['o_'])
                    T.op('dve', lambda e: e.tensor_tensor(out=sqo[:, :N], in0=o_[:, :N], in1=o_[:, :N], op=ALU.mult),
                         ['o_'], ['sqo'])

                def stageC(N=N):
                    f2 = st8['ns'] % 2
                    T.op('pe', lambda e: e.matmul(P2[f2][:, :N], lhsT=onesb[:], rhs=sqo[:, :N], start=True, stop=True),
                         ['sqo', 'onesb'], [('p2', f2)])
                    T.op('act', lambda e: e.activation(out=rr[:, :N], in_=P2[f2][:, :N], func=AF.Sqrt, bias=EPS,
                                                       scale=1.0 / 128), [('p2', f2)], ['rr'])

                def stageD(N=N, h=h, st=st, ci=ci):
                    T.op('dve', lambda e: e.reciprocal(out=rr[:, :N], in_=rr[:, :N]), ['rr'], ['rr'])
                    ab_ = st8['nq'] % 2
                    T.op('dve', lambda e: e.scalar_tensor_tensor(
                        out=ao[ab_][:, :N], in0=o_[:, :N], scalar=0.8, in1=rr[:, :N], op0=ALU.mult, op1=ALU.mult),
                        ['o_', 'rr'], [('ao', ab_)])
                    T.dma('sp', ATd.ap()[h][:, st:st + N], ao[ab_][:, :N], reads=[('ao', ab_)], writes=[('AT', ci)])

                pending.append([3, stageB])
                pending.append([10, stageC])
                pending.append([16, stageD])
        flush()

    def phase2b():
        Kb = T.alloc([128, NT], BF16, 'Kb')
        Vb = T.alloc([128, 66, 128], BF16, 'Vb')
        mk = T.alloc([128, 2, 512], BF16, 'mk')
        T.dma('sp', Kb[:, :], KBT.ap(), reads=ALLPROJ, writes=['Kb'])
        T.dma('sp', Vb[:, :, :], VB.ap().rearrange("(t p) c -> p t c", p=128), reads=ALLPROJ, writes=['Vb'])
        T.dma('sp', mk[:, 0, :], maskb.ap()[0], writes=['mk'])
        T.dma('sp', mk[:, 1, :], maskb.ap()[1], writes=['mk'])
        Qc = [T.alloc([128, 4, 512], BF16) for _ in range(2)]
        Pb = [T.alloc([128, 512], BF16) for _ in range(4)]
        dt_ = [T.alloc([64, 512], F32) for _ in range(2)]
        obw = [T.alloc([64, 4, 128], BF16) for _ in range(2)]
        nb = 0
        units = []
        def loadq(ci):
            st, N, isctx = CHUNKS[ci]
            T.dma('sp', Qc[ci % 2][:, :, :N], QBT.ap().rearrange("g p t -> p g t")[:, :, st:st + N],
                  reads=ALLPROJ, writes=[('Qc', ci % 2)])

        for ci, (st, N, isctx) in enumerate(CHUNKS):
            qb = ci % 2
            for bi in range(N // 128):
                if isctx:
                    tiles = [(0, None), (1, None)]
                else:
                    i = (st - 256) // 128 + bi
                    tiles = [(0, None), (1, None)]
                    if i - 1 >= 0:
                        tiles.append((2 + i - 1, 0))
                    tiles.append((2 + i, None))
                    if i + 1 < 64:
                        tiles.append((2 + i + 1, 1))
                for kvh in range(2):
                    for ti, (kt, mi) in enumerate(tiles):
                        units.append(dict(qb=qb, bi=bi, kvh=kvh, kt=kt, mi=mi, first=(ti == 0),
                                          last=(ti == len(tiles) - 1), st=st, ci=ci))

        def qk(u, sb):
            T.op('pe', lambda e: e.matmul(
                PS[sb][:, :].rearrange("p (g q) -> p g q", g=4),
                lhsT=Kb[u['kvh'] * 64:(u['kvh'] + 1) * 64, u['kt'] * 128:(u['kt'] + 1) * 128],
                rhs=Qc[u['qb']][u['kvh'] * 64:(u['kvh'] + 1) * 64, :, u['bi'] * 128:(u['bi'] + 1) * 128],
                start=True, stop=(u['mi'] is None)), ['Kb', ('Qc', u['qb'])], [psu(sb)])
            if u['mi'] is not None:
                T.op('pe', lambda e: e.matmul(
                    PS[sb][:, :], lhsT=identb[:], rhs=mk[:, u['mi'], :], start=False, stop=True),
                    ['identb', 'mk'], [psu(sb)])

        loadq(0)
        qk(units[0], 0)
        seen = set()
        for ui, u in enumerate(units):
            if u['ci'] not in seen:
                seen.add(u['ci'])
                if u['ci'] + 1 < len(CHUNKS):
                    loadq(u['ci'] + 1)
            sb = ui % 3
            pbf = ui % 4
            kvh = u['kvh']
            if ui + 1 < len(units):
                qk(units[ui + 1], (ui + 1) % 3)
            T.op('act', lambda e: e.activation(
                out=Pb[pbf][:, :], in_=PS[sb][:, :], func=AF.Exp, scale=0.125),
                [psu(sb)], [('Pb', pbf)])
            T.op('pe', lambda e: e.matmul(
                PS[3 + kvh][0:64, :], lhsT=Vb[:, u['kt'], kvh * 64:(kvh + 1) * 64], rhs=Pb[pbf][:, :],
                start=u['first'], stop=u['last']), ['Vb', ('Pb', pbf)], [psu(3 + kvh)])
            T.op('pe', lambda e: e.matmul(
                PS[5 + kvh][0:64, :], lhsT=onesb[:, 0:64], rhs=Pb[pbf][:, :],
                start=u['first'], stop=u['last']), ['onesb', ('Pb', pbf)], [psu(5 + kvh)])
            if u['last']:
                for g in range(4):
                    T.op('dve', lambda e, g=g: e.tensor_scalar(
                        out=dt_[kvh][:, g * 128:(g + 1) * 128], in0=PS[5 + kvh][0:64, g * 128:(g + 1) * 128],
                        scalar1=esink[0:64, kvh * 4 + g:kvh * 4 + g + 1], scalar2=None, op0=ALU.add),
                        [psu(5 + kvh), 'esink'], [('dt_', kvh)])
                T.op('dve', lambda e: e.reciprocal(out=dt_[kvh][:, :], in_=dt_[kvh][:, :]), [('dt_', kvh)],
                     [('dt_', kvh)])
                ob_ = nb % 2
                nb += 1
                T.op('dve', lambda e: e.tensor_tensor(
                    out=obw[ob_][:, :, :].rearrange("p g q -> p (g q)"), in0=PS[3 + kvh][0:64, :],
                    in1=dt_[kvh][:, :], op=ALU.mult), [psu(3 + kvh), ('dt_', kvh)], [('obw', ob_)])
                q0 = u['st'] + u['bi'] * 128
                T.dma('sp', BTd.ap()[kvh * 4:(kvh + 1) * 4].rearrange("g p t -> p g t")[:, :, q0:q0 + 128],
                      obw[ob_][:, :, :], reads=[('obw', ob_)], writes=[('BT', u['ci'])])

    def phase3a(li, srcs, wloads, chunk_ids):
        Wt = []
        for wi, (wap, nb_, P_) in enumerate(wloads):
            wt = T.alloc([P_, nb_, D], BF16)
            T.dma('pool', wt[:, :, :], wap, writes=[('Wo', wi)])
            Wt.append(wt)
        xT = [T.alloc([128, 8, 512], F32) for _ in range(2)]
        hT = [T.alloc([128, 8, 512], BF16) for _ in range(2)]
        mx = [[T.alloc([P_, nb_, 512], BF16) for (_, nb_, P_) in srcs] for _ in range(2)]
        tmp = T.alloc([128, 8, 512], F32)
        sq = T.alloc([128, 8, 512], BF16)
        rstd = T.alloc([128, 512], F32)
        npb = 0

        def partA(ci):
            nonlocal npb
            st, N, isctx = CHUNKS[ci]
            b = ci % 2
            who = 1 if isctx else 0
            T.dma('sp', xT[b][:, :, :N], R.ap()[:, :, st:st + N], reads=[('R', ci)], writes=[('xT', b)])
            for si, (sap, nb_, P_) in enumerate(srcs):
                T.dma('sp', mx[b][si][:, :, :N], sap.rearrange("g p t -> p g t")[:, :, st:st + N],
                      reads=[('AT', ci), ('BT', ci), ('YT', ci)], writes=[('mx', b, si)])
            nmm = sum(nb_ for (_, nb_, _) in srcs)
            for j in range(8):
                pb = npb % 2
                npb += 1
                k = 0
                for si, (sap, nb_, P_) in enumerate(srcs):
                    for q in range(nb_):
                        T.op('pe', lambda e, si=si, q=q, j=j, pb=pb, b=b, N=N, k=k, P_=P_: e.matmul(
                            PS[pb][:, :N], lhsT=Wt[si][0:P_, q, j * 128:(j + 1) * 128], rhs=mx[b][si][0:P_, q, :N],
                            start=(k == 0), stop=(k == nmm - 1)), [('Wo', si), ('mx', b, si)], [psu(pb)])
                        k += 1
                g_ap = modT[:, li, 2 * 8 + j, who:who + 1]
                T.op('dve', lambda e, j=j, pb=pb, b=b, N=N, g_ap=g_ap: e.scalar_tensor_tensor(
                    out=xT[b][:, j, :N], in0=PS[pb][:, :N], scalar=g_ap, in1=xT[b][:, j, :N],
                    op0=ALU.mult, op1=ALU.add), [psu(pb), ('xT', b), 'modT'], [('xT', b)])
            T.dma('sp', R.ap()[:, :, st:st + N], xT[b][:, :, :N], reads=[('xT', b)], writes=[('R', ci)])

        def partB(ci):
            st, N, isctx = CHUNKS[ci]
            b = ci % 2
            who = 1 if isctx else 0
            rmsnorm_mod(xT[b], ('xT', b), N, li, 1, who, hT[b], ('hT', b), tmp, 'tmp', sq, 'sq', rstd, 'rstd', 2)
            T.dma('sp', H2.ap()[:, :, st:st + N], hT[b][:, :, :N], reads=[('hT', b)], writes=[('H2', ci)])

        partA(chunk_ids[0])
        for i_, ci in enumerate(chunk_ids):
            if i_ + 1 < len(chunk_ids):
                partA(chunk_ids[i_ + 1])
            partB(ci)

    def phase3b(li, chunk_ids, after_loads=None):
        Wi = T.alloc([128, 8, 2 * DFF], BF16, 'Wi')
        for kc in range(8):
            T.dma('pool', Wi[:, kc, :], ffn_in.ap()[li, kc * 128:(kc + 1) * 128, :], writes=[('Wi', kc)])
        if after_loads is not None:
            after_loads()
        hT = [T.alloc([128, 8, 512], BF16) for _ in range(2)]
        sg = [T.alloc([128, 512], F32) for _ in range(2)]
        ac = [T.alloc([128, 512], BF16) for _ in range(3)]
        n = 0
        for ci in chunk_ids:
            st, N, isctx = CHUNKS[ci]
            b = ci % 2
            T.dma('sp', hT[b][:, :, :N], H2.ap()[:, :, st:st + N], reads=[('H2', ci)], writes=[('hT', b)])
            for fb in range(22):
                pg = (n % 4) * 2
                s2 = n % 2
                a3 = n % 3
                n += 1
                for (col0, pbk) in ((fb * 128, pg), (DFF + fb * 128, pg + 1)):
                    for kc in range(8):
                        T.op('pe', lambda e, col0=col0, pbk=pbk, kc=kc, b=b, N=N: e.matmul(
                            PS[pbk][:, :N], lhsT=Wi[:, kc, col0:col0 + 128], rhs=hT[b][:, kc, :N],
                            start=(kc == 0), stop=(kc == 7)), [('hT', b), ('Wi', kc)], [psu(pbk)])
                T.op('act', lambda e, pg=pg, s2=s2, N=N: e.activation(out=sg[s2][:, :N], in_=PS[pg][:, :N],
                                                                     func=AF.Silu), [psu(pg)], [('sg', s2)])
                T.op('dve', lambda e, pg=pg, s2=s2, a3=a3, N=N: e.tensor_tensor(
                    out=ac[a3][:, :N], in0=PS[pg + 1][:, :N], in1=sg[s2][:, :N], op=ALU.mult),
                    [psu(pg + 1), ('sg', s2)], [('ac', a3)])
                T.dma('sp', ACTd.ap()[fb][:, st:st + N], ac[a3][:, :N], reads=[('ac', a3)], writes=[('ACT', ci)])

    def load_Wo(li):
        Wo = T.alloc([128, 22, D], BF16, 'Wo2')
        for f0 in range(0, 22, 2):
            T.dma('pool', Wo[:, f0:f0 + 2, :],
                  ffn_out.ap()[li, f0 * 128:(f0 + 2) * 128, :].rearrange("(f p) d -> p f d", p=128),
                  writes=[('Wo2', f0)])
        return Wo

    def phase3c(li, chunk_ids, final, Wo=None):
        if Wo is None:
            Wo = load_Wo(li)
        Wou = [('Wo2', f0) for f0 in range(0, 22, 2)]
        xT = [T.alloc([128, 8, 512], F32) for _ in range(2)]
        ac = [T.alloc([128, 22, 512], BF16) for _ in range(2)]
        if final:
            tmp = T.alloc([128, 8, 512], F32)
            sq = T.alloc([128, 8, 512], BF16)
            rstd = T.alloc([128, 512], F32)
            otok = [T.alloc([128, D], F32) for _ in range(2)]
        npb = 0
        no = 0
        for ci in chunk_ids:
            st, N, isctx = CHUNKS[ci]
            b = ci % 2
            who = 1 if isctx else 0
            T.dma('sp', xT[b][:, :, :N], R.ap()[:, :, st:st + N], reads=[('R', ci)], writes=[('xT', b)])
            T.dma('sp', ac[b][:, :, :N], ACTd.ap().rearrange("f p t -> p f t")[:, :, st:st + N],
                  reads=[('ACT', ci)], writes=[('acl', b)])
            for j in range(8):
                pb = npb % 2
                npb += 1
                for fb in range(22):
                    T.op('pe', lambda e, fb=fb, j=j, pb=pb, b=b, N=N: e.matmul(
                        PS[pb][:, :N], lhsT=Wo[:, fb, j * 128:(j + 1) * 128], rhs=ac[b][:, fb, :N],
                        start=(fb == 0), stop=(fb == 21)), [('acl', b)] + Wou, [psu(pb)])
                g_ap = modT[:, li, 5 * 8 + j, who:who + 1]
                T.op('dve', lambda e, j=j, pb=pb, b=b, N=N, g_ap=g_ap: e.scalar_tensor_tensor(
                    out=xT[b][:, j, :N], in0=PS[pb][:, :N], scalar=g_ap, in1=xT[b][:, j, :N],
                    op0=ALU.mult, op1=ALU.add), [psu(pb), ('xT', b), 'modT'], [('xT', b)])
            if not final:
                T.dma('sp', R.ap()[:, :, st:st + N], xT[b][:, :, :N], reads=[('xT', b)], writes=[('R', ci)])
            else:
                rmsnorm_mod(xT[b], ('xT', b), N, li, 0, None, None, None, tmp, 'tmp', sq, 'sq', rstd, 'rstd', 2)
                for tt in range(N // 128):
                    o2 = no % 2
                    no += 1
                    for j in range(8):
                        pbk = 3 + (j // 4) + 2 * o2
                        T.op('pe', lambda e, j=j, tt=tt, pbk=pbk: e.transpose(
                            PS[pbk][:, (j % 4) * 128:(j % 4 + 1) * 128], tmp[:, j, tt * 128:(tt + 1) * 128],
                            ident[:]), ['tmp', 'ident'], [psu(pbk)])
                    for hf in range(2):
                        pbk = 3 + hf + 2 * o2
                        T.op('act' if hf else 'dve', (lambda e, pbk=pbk, o2=o2, hf=hf: e.activation(
                            out=otok[o2][:, hf * 512:(hf + 1) * 512], in_=PS[pbk][:, :], func=AF.Copy)) if hf else
                            (lambda e, pbk=pbk, o2=o2, hf=hf: e.tensor_copy(
                                out=otok[o2][:, hf * 512:(hf + 1) * 512], in_=PS[pbk][:, :])),
                            [psu(pbk)], [('otok', o2, hf)])
                    r0_ = st - 256 + tt * 128
                    T.dma('sp', out.ap()[r0_:r0_ + 128, :], otok[o2][:, :],
                          reads=[('otok', o2, 0), ('otok', o2, 1)], writes=[('out', ci, tt)])

    def load_W1():
        W = T.alloc([128, 8, 2592], BF16, 'w1')
        for kc in range(8):
            T.dma('pool', W[:, kc, :], w1.ap()[kc * 128:(kc + 1) * 128, :], writes=[('W', kc)])
        return W

    def phaseP1(W=None):
        if W is None:
            W = load_W1()
        Wu = [('W', kc) for kc in range(8)]
        gw = T.alloc([16, 2, 256], BF16)
        T.dma('pool', gw[:, :, :], gw_in.ap().rearrange("n r k -> r n k"), writes=['gw'])
        gb2 = T.alloc([64, 2, 512], F32)
        T.dma('sp', gb2[:, :, :], gb2_in.ap().rearrange("n p k -> p n k"), writes=['gb2'])
        tri = T.alloc([64, 4, 64], F32)
        T.dma('sp', tri[:, :, :], tri_in.ap().rearrange("n p k -> p n k"), writes=['tri'])
        xT2 = [T.alloc([128, 8, 512], F32) for _ in range(2)]
        hT2 = [T.alloc([128, 8, 512], BF16) for _ in range(2)]
        tmp = T.alloc([128, 8, 512], F32)
        sq = T.alloc([128, 8, 512], BF16)
        rstd = T.alloc([128, 512], F32)
        qf = T.alloc([64, 4, 512], F32)
        kf = T.alloc([64, 4, 512], F32)
        lrs = T.alloc([16, 2, 512], BF16)
        vt = T.alloc([64, 8, 512], BF16)
        ktf = T.alloc([64, 8, 256], F32)
        zz = T.alloc([64, 8, 256], F32)
        eb = [T.alloc([64, 512], F32) for _ in range(2)]
        enb = [T.alloc([64, 512], F32) for _ in range(2)]
        so = [T.alloc([128, 512], BF16) for _ in range(4)]
        sf = [T.alloc([128, 512], F32) for _ in range(2)]
        g1 = T.alloc([128, 512], F32)
        g2 = T.alloc([128, 512], F32)
        kdt = T.alloc([64, 8, 256], BF16)
        cnt = {'so': 0, 'sf': 0, 'pb': 0, 'e': 0}

        def nxt(k, n):
            v = cnt[k] % n
            cnt[k] += 1
            return v

        cur = {}

        def fm_block(col0, M, N):
            pb = nxt('pb', 4)
            hT, hTu = cur['hT'], cur['hTu']
            for kc in range(8):
                T.op('pe', lambda e, kc=kc, pb=pb: e.matmul(
                    PS[pb][0:M, :N], lhsT=W[:, kc, col0:col0 + M], rhs=hT[:, kc, :N],
                    start=(kc == 0), stop=(kc == 7)), [hTu, Wu[kc]], [psu(pb)])
            return pb

        def prep(ci):
            st, N, isctx = CHUNKS[ci]
            who = 1 if isctx else 0
            b = ci % 2
            T.dma('sp', xT2[b][:, :, :N], R.ap()[:, :, st:st + N], reads=[('R', ci)], writes=[('xT', b)])
            rmsnorm_mod(xT2[b], ('xT', b), N, 1, 0, who, hT2[b], ('hT', b), tmp, 'tmp', sq, 'sq', rstd, 'rstd', 7)

        prep(0)
        for ci, (st, N, isctx) in enumerate(CHUNKS):
            who = 1 if isctx else 0
            nsub = N // 64
            sub0 = st // 64
            if ci + 1 < len(CHUNKS):
                prep(ci + 1)
            hT = hT2[ci % 2]
            hTu = ('hT', ci % 2)
            cur['hT'], cur['hTu'] = hT, hTu
            for blk in range(4):
                pb = fm_block(2080 + blk * 128, 128, N)
                k = nxt('sf', 2)
                T.op('act', lambda e, pb=pb, k=k: e.activation(out=sf[k][:, :N], in_=PS[pb][:, :N], func=AF.Copy),
                     [psu(pb)], [('sf', k)])
                T.dma('sp', ZX.ap()[blk][:, st:st + N], sf[k][:, :N], reads=[('sf', k)], writes=[('ZX', ci)])
            if not isctx:
                for blk in range(4):
                    pb = fm_block(1568 + blk * 128, 128, N)
                    k = nxt('sf', 2)
                    T.op('act', lambda e, pb=pb, k=k: e.activation(out=sf[k][:, :N], in_=PS[pb][:, :N], func=AF.Copy),
                         [psu(pb)], [('sf', k)])
                    T.op('pool', lambda e, k=k: e.tensor_tensor(out=g1[:, :N], in0=sf[k][:, :N], in1=sf[k][:, :N],
                                                               op=ALU.mult), [('sf', k)], ['g1'])
                    T.op('pool', lambda e: e.tensor_scalar(out=g1[:, :N], in0=g1[:, :N], scalar1=0.044715,
                                                           scalar2=1.0, op0=ALU.mult, op1=ALU.add), ['g1'], ['g1'])
                    T.op('pool', lambda e, k=k: e.tensor_tensor(out=g2[:, :N], in0=g1[:, :N], in1=sf[k][:, :N],
                                                               op=ALU.mult), ['g1', ('sf', k)], ['g2'])
                    T.op('act', lambda e: e.activation(out=g2[:, :N], in_=g2[:, :N], func=AF.Sigmoid,
                                                       scale=1.5957691216), ['g2'], ['g2'])
                    o4 = nxt('so', 4)
                    T.op('dve', lambda e, k=k, o4=o4: e.tensor_tensor(out=so[o4][:, :N], in0=sf[k][:, :N],
                                                                     in1=g2[:, :N], op=ALU.mult),
                         ['g2', ('sf', k)], [('so', o4)])
                    T.dma('sp', GZ.ap()[blk][:, st:st + N], so[o4][:, :N], reads=[('so', o4)], writes=[('GZ', ci)])
                for blk in range(4):
                    pb = fm_block(1024 + blk * 128, 128, N)
                    o4 = nxt('so', 4)
                    T.op('act', lambda e, pb=pb, o4=o4: e.activation(out=so[o4][:, :N], in_=PS[pb][:, :N],
                                                                    func=AF.Silu), [psu(pb)], [('so', o4)])
                    T.dma('sp', SG.ap()[blk][:, st:st + N], so[o4][:, :N], reads=[('so', o4)], writes=[('SG', ci)])
            for h in range(4):
                if not isctx:
                    pb = fm_block(h * 64, 64, N)
                    T.op('act', lambda e, pb=pb, h=h: e.activation(out=qf[:, h, :N], in_=PS[pb][0:64, :N],
                                                                  func=AF.Identity, scale=0.125),
                         [psu(pb)], [('qf', h)])
                    pb = fm_block(256 + h * 64, 64, N)
                    T.op('dve', lambda e, pb=pb, h=h: e.tensor_copy(out=kf[:, h, :N], in_=PS[pb][0:64, :N]),
                         [psu(pb)], [('kf', h)])
            for d in range(2):
                pb = fm_block(1536 + d * 16, 16, N)
                T.op('act', lambda e, pb=pb, d=d: e.activation(out=lrs[:, d, :N], in_=PS[pb][0:16, :N], func=AF.Copy),
                     [psu(pb)], [('lrs', d)])
            for s_ in range(nsub):
                pb = nxt('pb', 4)
                for kc in range(8):
                    T.op('pe', lambda e, kc=kc, pb=pb, s_=s_: e.matmul(
                        PS[pb][0:64, :], lhsT=hT[:, kc, s_ * 64:(s_ + 1) * 64], rhs=W[:, kc, 512:1024],
                        start=(kc == 0), stop=(kc == 7)), [hTu, Wu[kc]], [psu(pb)])
                T.op('act', lambda e, pb=pb, s_=s_: e.activation(out=vt[:, s_, :], in_=PS[pb][0:64, :], func=AF.Copy),
                     [psu(pb)], ['vt'])
            T.dma('sp', V1.ap()[st:st + N, :].rearrange("(s p) c -> p s c", p=64), vt[:, :nsub, :],
                  reads=['vt'], writes=[('V1', ci)])
            for s2 in range(0, nsub, 2):
                pb = nxt('pb', 4)
                for s_ in (s2, s2 + 1):
                    for kc in range(8):
                        T.op('pe', lambda e, kc=kc, pb=pb, s_=s_, s2=s2: e.matmul(
                            PS[pb][0:64, (s_ - s2) * 256:(s_ - s2 + 1) * 256],
                            lhsT=hT[:, kc, s_ * 64:(s_ + 1) * 64], rhs=W[:, kc, 256:512],
                            start=(kc == 0), stop=(kc == 7)), [hTu, Wu[kc]], [psu(pb)])
                T.op('dve', lambda e, pb=pb, s2=s2: e.tensor_copy(
                    out=ktf[:, s2:s2 + 2, :].rearrange("p s c -> p (s c)"), in_=PS[pb][0:64, :]), [psu(pb)], ['ktf'])
            for d in range(2):
                for s2 in range(0, nsub, 2):
                    pb = nxt('pb', 4)
                    for s_ in (s2, s2 + 1):
                        T.op('pe', lambda e, pb=pb, s_=s_, s2=s2, d=d: e.matmul(
                            PS[pb][0:64, (s_ - s2) * 256:(s_ - s2 + 1) * 256],
                            lhsT=lrs[:, d, s_ * 64:(s_ + 1) * 64], rhs=gw[:, d, :], start=True, stop=True),
                            [('lrs', d), 'gw'], [psu(pb)])
                    T.op('dve', lambda e, pb=pb, s2=s2, d=d: e.tensor_tensor(
                        out=zz[:, s2:s2 + 2, :].rearrange("p s c -> p (s c)"), in0=PS[pb][0:64, :], in1=gb2[:, d, :],
                        op=ALU.add), [psu(pb), 'gb2'], ['zz'])
                T.op('act', lambda e: e.activation(out=zz[:, :nsub, :], in_=zz[:, :nsub, :], func=AF.Exp, scale=-1.0),
                     ['zz'], ['zz'])
                T.op('act', lambda e: e.activation(out=zz[:, :nsub, :], in_=zz[:, :nsub, :], func=AF.Ln, bias=1.0),
                     ['zz'], ['zz'])
                for s2 in range(0, nsub, 2):
                    pb = nxt('pb', 4)
                    for s_ in (s2, s2 + 1):
                        T.op('pe', lambda e, pb=pb, s_=s_, s2=s2, d=d: e.matmul(
                            PS[pb][0:64, (s_ - s2) * 256:(s_ - s2 + 1) * 256],
                            lhsT=tri[:, 2 + d, :], rhs=zz[:, s_, :], start=True, stop=True),
                            ['zz', 'tri'], [psu(pb)])
                    k = nxt('sf', 2)
                    T.op('act', lambda e, pb=pb, k=k: e.activation(out=sf[k][0:64, :], in_=PS[pb][0:64, :],
                                                                  func=AF.Exp, scale=-1.0 / 16), [psu(pb)], [('sf', k)])
                    T.op('dve', lambda e, k=k, s2=s2: e.tensor_tensor(
                        out=kdt[:, s2:s2 + 2, :].rearrange("p s c -> p (s c)"),
                        in0=ktf[:, s2:s2 + 2, :].rearrange("p s c -> p (s c)"), in1=sf[k][0:64, :], op=ALU.mult),
                        [('sf', k), 'ktf'], ['kdt'])
                T.dma('sp', KD.ap()[d, st:st + N, :].rearrange("(s p) c -> p s c", p=64), kdt[:, :nsub, :],
                      reads=['kdt'], writes=[('KD', d, ci)])
                for h in range(4):
                    pb = nxt('pb', 4)
                    for s_ in range(nsub):
                        T.op('pe', lambda e, pb=pb, s_=s_, h=h, d=d: e.matmul(
                            PS[pb][0:64, s_ * 64:(s_ + 1) * 64], lhsT=zz[:, s_, h * 64:(h + 1) * 64],
                            rhs=tri[:, d, :], start=True, stop=True), ['zz', 'tri'], [psu(pb)])
                    k = nxt('e', 2)
                    T.op('act', lambda e, pb=pb, k=k: e.activation(out=eb[k][:, :N], in_=PS[pb][0:64, :N],
                                                                  func=AF.Exp, scale=-1.0 / 16), [psu(pb)], [('eb', k)])
                    col = 63 if d == 0 else 0
                    T.op('pool', lambda e, k=k, d=d, h=h, col=col, sub0=sub0, nsub=nsub: e.tensor_copy(
                        out=decs[:, d, h, sub0:sub0 + nsub],
                        in_=eb[k][:, :N].rearrange("p (s c) -> p s c", c=64)[:, :, col]),
                        [('eb', k)], ['decs'])
                    if not isctx:
                        T.op('act', lambda e, pb=pb, k=k: e.activation(out=enb[k][:, :N], in_=PS[pb][0:64, :N],
                                                                      func=AF.Exp, scale=1.0 / 16),
                             [psu(pb)], [('enb', k)])
                        o4 = nxt('so', 4)
                        T.op('dve', lambda e, k=k, h=h, o4=o4: e.tensor_tensor(
                            out=so[o4][0:64, :N], in0=qf[:, h, :N], in1=eb[k][:, :N], op=ALU.mult),
                            [('qf', h), ('eb', k)], [('so', o4)])
                        T.dma('sp', QS.ap()[d, h][:, st:st + N], so[o4][0:64, :N], reads=[('so', o4)],
                              writes=[('QS', d, ci)])
                        o4 = nxt('so', 4)
                        T.op('pool', lambda e, k=k, h=h, o4=o4: e.tensor_tensor(
                            out=so[o4][0:64, :N], in0=kf[:, h, :N], in1=enb[k][:, :N], op=ALU.mult),
                            [('kf', h), ('enb', k)], [('so', o4)])
                        T.dma('sp', KS.ap()[d, h][:, st:st + N], so[o4][0:64, :N], reads=[('so', o4)],
                              writes=[('KS', d, ci)])

    def phaseG():
        glag = T.alloc([128, 1], F32)
        T.dma('sp', glag[:, :], glag_in.ap(), writes=['glag'])
        mask4 = T.alloc([64, 2, 256], F32)
        T.dma('sp', mask4[:, :, :], mask4_in.ap().rearrange("n p k -> p n k"), writes=['mask4'])
        S32 = T.alloc([64, 4, 128], F32)
        Sbf = [T.alloc([64, 4, 128], BF16) for _ in range(2)]
        QSs = [T.alloc([64, 4, 512], BF16) for _ in range(2)]
        KSs = [T.alloc([64, 4, 512], BF16) for _ in range(2)]
        KDs = [T.alloc([64, 8, 256], BF16) for _ in range(2)]
        V1s = [T.alloc([64, 8, 512], BF16) for _ in range(2)]
        ATm = [T.alloc([64, 256], BF16) for _ in range(2)]
        of_ = [T.alloc([128, 512], F32) for _ in range(2)]
        of4 = [T.alloc([128, 512], F32) for _ in range(4)]
        ogl4 = [T.alloc([128, 512], F32) for _ in range(4)]
        sgl4 = [[T.alloc([128, 512], BF16) for _ in range(4)] for _ in range(2)]
        sqo2 = [T.alloc([128, 512], BF16) for _ in range(2)]
        rr4 = [T.alloc([128, 512], F32) for _ in range(4)]
        yo4 = [T.alloc([128, 512], BF16) for _ in range(4)]
        pending = []

        def tick():
            for p_ in pending:
                p_[0] -= 1
            while pending and pending[0][0] <= 0:
                pending.pop(0)[1]()

        def flush():
            while pending:
                pending.pop(0)[1]()
        nchk = 0
        nb = 0
        nst = 0
        nat = 0
        nev = 0
        for d in range(2):
            T.op('dve', lambda e: e.memset(S32[:, :, :], 0.0), [], ['S32'])
            T.op('dve', lambda e, k=nst % 2: e.memset(Sbf[k][:, :, :], 0.0), [], [('Sbf', nst % 2)])
            order = [0] + (list(range(1, 17)) if d == 0 else list(range(16, 0, -1)))
            for ci in order:
                st, N, isctx = CHUNKS[ci]
                nsub = N // 64
                b = nb % 2
                nb += 1
                if not isctx:
                    T.dma('sp', QSs[b][:, :, :N], QS.ap()[d].rearrange("h p t -> p h t")[:, :, st:st + N],
                          reads=[('QS', d, ci)], writes=[('QSs', b)])
                    T.dma('sp', KSs[b][:, :, :N], KS.ap()[d].rearrange("h p t -> p h t")[:, :, st:st + N],
                          reads=[('KS', d, ci)], writes=[('KSs', b)])
                T.dma('sp', KDs[b][:, :nsub, :], KD.ap()[d, st:st + N, :].rearrange("(s p) c -> p s c", p=64),
                      reads=[('KD', d, ci)], writes=[('KDs', b)])
                T.dma('sp', V1s[b][:, :nsub, :], V1.ap()[st:st + N, :].rearrange("(s p) c -> p s c", p=64),
                      reads=[('V1', ci)], writes=[('V1s', b)])
                if d == 1 and not isctx:
                    cp_ = nchk % 2
                    nchk += 1
                    for h in range(4):
                        T.dma('sp', ogl4[h][:, :N], OG.ap()[h][:, st:st + N], reads=[('OG', ci)],
                              writes=[('ogl4', h)])
                        T.dma('sp', sgl4[cp_][h][:, :N], SG.ap()[h][:, st:st + N], reads=[('SG', ci)],
                              writes=[('sgl4', cp_, h)])
                subs = list(range(nsub)) if d == 0 else list(range(nsub - 1, -1, -1))
                for s_ in subs:
                    gs = st // 64 + s_
                    cur = nst % 2
                    nx = (nst + 1) % 2
                    nst += 1
                    c0 = s_ * 64
                    if not isctx:
                        tick()
                    for h in range(4):
                        T.op('pe', lambda e, h=h, b=b, s_=s_: e.matmul(
                            PS[1][0:64, h * 128:(h + 1) * 128], lhsT=KDs[b][:, s_, h * 64:(h + 1) * 64],
                            rhs=V1s[b][:, s_, h * 128:(h + 1) * 128], start=True, stop=True),
                            [('KDs', b), ('V1s', b)], [psu(1)])
                    if not isctx:
                        am = nat % 2
                        nat += 1
                        for h in range(4):
                            T.op('pe', lambda e, h=h, b=b, c0=c0: e.matmul(
                                PS[0][0:64, h * 64:(h + 1) * 64], lhsT=KSs[b][:, h, c0:c0 + 64],
                                rhs=QSs[b][:, h, c0:c0 + 64], start=True, stop=True),
                                [('KSs', b), ('QSs', b)], [psu(0)])
                        T.op('dve', lambda e, am=am, d=d: e.tensor_tensor(
                            out=ATm[am][:, :], in0=PS[0][0:64, 0:256], in1=mask4[:, d, :], op=ALU.mult),
                            [psu(0), 'mask4'], [('ATm', am)])
                        for h in range(4):
                            T.op('pe', lambda e, h=h, b=b, c0=c0, am=am, s_=s_: e.matmul(
                                PS[4 + h][:, c0:c0 + 64], lhsT=V1s[b][:, s_, h * 128:(h + 1) * 128],
                                rhs=ATm[am][:, h * 64:(h + 1) * 64], start=True, stop=False),
                                [('V1s', b), ('ATm', am)], [psu(4 + h)])
                            T.op('pe', lambda e, h=h, b=b, c0=c0, cur=cur: e.matmul(
                                PS[4 + h][:, c0:c0 + 64], lhsT=Sbf[cur][:, h, :], rhs=QSs[b][:, h, c0:c0 + 64],
                                start=False, stop=True), [('Sbf', cur), ('QSs', b)], [psu(4 + h)])
                    for h in range(4):
                        T.op('dve', lambda e, h=h, d=d, gs=gs: e.scalar_tensor_tensor(
                            out=S32[:, h, :], in0=S32[:, h, :], scalar=decs[:, d, h, gs:gs + 1],
                            in1=PS[1][0:64, h * 128:(h + 1) * 128], op0=ALU.mult, op1=ALU.add),
                            ['S32', psu(1), 'decs'], ['S32'])
                    T.op('act', lambda e, nx=nx: e.activation(out=Sbf[nx][:, :, :], in_=S32[:, :, :], func=AF.Copy),
                         ['S32'], [('Sbf', nx)])
                if isctx:
                    continue
                flush()
                for h in range(4):
                    k = nev % 2
                    nev += 1
                    if d == 0:
                        T.op('act', lambda e, h=h, k=k: e.activation(out=of_[k][:, :N], in_=PS[4 + h][:, :N],
                                                                    func=AF.Copy), [psu(4 + h)], [('of', k)])
                        T.dma('sp', OG.ap()[h][:, st:st + N], of_[k][:, :N], reads=[('of', k)], writes=[('OG', ci)])
                    else:
                        T.op('dve', lambda e, h=h: e.tensor_tensor(
                            out=of4[h][:, :N], in0=PS[4 + h][:, :N], in1=ogl4[h][:, :N], op=ALU.add),
                            [psu(4 + h), ('ogl4', h)], [('of4', h)])

                        def st2(h=h, N=N):
                            q2 = h % 2
                            T.op('act', lambda e: e.activation(out=sqo2[q2][:, :N], in_=of4[h][:, :N], func=AF.Square),
                                 [('of4', h)], [('sqo2', q2)])
                            T.op('pe', lambda e: e.matmul(PS[2 + q2][:, :N], lhsT=onesb[:], rhs=sqo2[q2][:, :N],
                                                          start=True, stop=True), [('sqo2', q2), 'onesb'], [psu(2 + q2)])
                            T.op('act', lambda e: e.activation(out=rr4[h][:, :N], in_=PS[2 + q2][:, :N], func=AF.Sqrt,
                                                               bias=EPS, scale=1.0 / 128), [psu(2 + q2)], [('rr4', h)])

                        def st3(h=h, N=N, st=st, ci=ci, cp_=cp_):
                            T.op('dve', lambda e: e.reciprocal(out=rr4[h][:, :N], in_=rr4[h][:, :N]), [('rr4', h)],
                                 [('rr4', h)])
                            T.op('dve', lambda e: e.scalar_tensor_tensor(
                                out=of4[h][:, :N], in0=of4[h][:, :N], scalar=glag[:, 0:1], in1=rr4[h][:, :N],
                                op0=ALU.mult, op1=ALU.mult), [('of4', h), ('rr4', h), 'glag'], [('of4', h)])
                            T.op('dve', lambda e: e.tensor_tensor(
                                out=yo4[h][:, :N], in0=of4[h][:, :N], in1=sgl4[cp_][h][:, :N], op=ALU.mult),
                                [('of4', h), ('sgl4', cp_, h)], [('yo4', h)])
                            T.dma('sp', YT.ap()[h][:, st:st + N], yo4[h][:, :N], reads=[('yo4', h)],
                                  writes=[('YT', ci)])

                        pending.append([1 + h, st2])
                        pending.append([3 + h, st3])
                        pending.sort(key=lambda p_: p_[0])
        flush()

    def phaseLr():
        cw = T.alloc([128, 4, 4], F32)
        cb = T.alloc([128, 4], F32)
        ba = T.alloc([128, 2, 4], F32)
        bx = T.alloc([128, 2, 4], F32)
        lam = T.alloc([128, 2, 4], F32)
        cl = T.alloc([128, 2, 4], F32)
        cl2 = T.alloc([128, 2, 4], F32)
        T.dma('sp', cw[:, :, :], cw_in.ap(), writes=['cw'])
        T.dma('sp', cb[:, :], cb_in.ap(), writes=['cb'])
        T.dma('sp', ba[:, :, :], lba.ap(), writes=['ba'])
        T.dma('sp', bx[:, :, :], lbx.ap(), writes=['bx'])
        T.dma('sp', lam[:, :, :], llam.ap(), writes=['lam'])
        T.op('act', lambda e: e.activation(out=lam[:, :, :], in_=lam[:, :, :], func=AF.Exp, scale=-1.0), ['lam'], ['lam'])
        T.op('act', lambda e: e.activation(out=lam[:, :, :], in_=lam[:, :, :], func=AF.Ln, bias=1.0), ['lam'], ['lam'])
        T.op('dve', lambda e: e.tensor_scalar(out=cl[:, :, :], in0=lam[:, :, :], scalar1=-8.0, scalar2=None,
                                              op0=ALU.mult), ['lam'], ['cl'])
        T.op('dve', lambda e: e.tensor_scalar(out=cl2[:, :, :], in0=lam[:, :, :], scalar1=-16.0, scalar2=None,
                                              op0=ALU.mult), ['lam'], ['cl2'])
        Wbd = T.alloc([128, 2, 2, 4, 128], BF16)
        T.op('pool', lambda e: e.memset(Wbd[:, :, :, :, :], 0.0), [], ['Wbd'])
        for d in range(2):
            for gi, src in enumerate((lwa, lwx)):
                for blk in range(4):
                    for half in range(2):
                        T.dma('pool', Wbd[half * 64:(half + 1) * 64, d, gi, blk, half * 64:(half + 1) * 64],
                              src.ap()[d, blk * 2 + half], reads=[], writes=['Wbd'])
        zxh = [T.alloc([128, 4, 516], F32) for _ in range(2)]
        xr_ = [T.alloc([128, 4, 512], F32) for _ in range(2)]
        xrb_ = [T.alloc([128, 4, 512], BF16) for _ in range(2)]
        rg_ = [T.alloc([128, 4, 512], F32) for _ in range(2)]
        ig_ = [T.alloc([128, 4, 512], F32) for _ in range(2)]
        aa_ = [T.alloc([128, 4, 512], F32) for _ in range(2)]
        a2_ = [T.alloc([128, 4, 512], F32) for _ in range(2)]
        hout = [T.alloc([128, 4, 512], F32) for _ in range(2)]
        hl = [T.alloc([128, 4, 512], F32) for _ in range(2)]
        gz = [T.alloc([128, 4, 512], BF16) for _ in range(2)]
        yo = [T.alloc([128, 4, 512], BF16) for _ in range(2)]
        hprev = T.alloc([128, 4], F32)
        def stage1(d, ci, b):
            if True:
                st, N, isctx = CHUNKS[ci]
                xr, xrb, rg, ig, aa, a2 = xr_[b], xrb_[b], rg_[b], ig_[b], aa_[b], a2_[b]
                lo_seq, hi_seq = (0, LC) if isctx else (LC, NT)
                lo = max(st - 2, lo_seq)
                hi = min(st + N + 2, hi_seq)
                if lo > st - 2 or hi < st + N + 2:
                    T.op('pool', lambda e: e.memset(zxh[b][:, :, :], 0.0), [], [('zxh', b)])
                T.dma('sp', zxh[b][:, :, 2 + (lo - st):2 + (hi - st)], ZX.ap().rearrange("b p t -> p b t")[:, :, lo:hi],
                      reads=[('ZX', c2) for c2 in range(len(CHUNKS))], writes=[('zxh', b)])
                if d == 1 and not isctx:
                    T.dma('sp', hl[b][:, :, :N], HL.ap().rearrange("b p t -> p b t")[:, :, st:st + N],
                          reads=[('HL', ci)], writes=[('hl', b)])
                    T.dma('sp', gz[b][:, :, :N], GZ.ap().rearrange("b p t -> p b t")[:, :, st:st + N],
                          reads=[('GZ', ci)], writes=[('gz', b)])
                xru = [('xr', b, blk) for blk in range(4)]
                for blk in range(4):
                    T.op('dve', lambda e, blk=blk: e.tensor_scalar(
                        out=xr[:, blk, :N], in0=zxh[b][:, blk, 1:1 + N], scalar1=cw[:, blk, 0:1],
                        scalar2=cb[:, blk:blk + 1], op0=ALU.mult, op1=ALU.add), [('zxh', b), 'cw', 'cb'], [xru[blk]])
                    for j in range(1, 4):
                        T.op('dve', lambda e, blk=blk, j=j: e.scalar_tensor_tensor(
                            out=xr[:, blk, :N], in0=zxh[b][:, blk, 1 + j:1 + j + N], scalar=cw[:, blk, j:j + 1],
                            in1=xr[:, blk, :N], op0=ALU.mult, op1=ALU.add), [('zxh', b), 'cw', xru[blk]], [xru[blk]])
                T.op('act', lambda e: e.activation(out=xrb[:, :, :N], in_=xr[:, :, :N], func=AF.Copy),
                     xru, [('xrb', b)])
                for blk in range(4):
                    T.op('pe', lambda e, blk=blk: e.matmul(PS[blk][:, :N], lhsT=Wbd[:, d, 0, blk, :],
                                                         rhs=xrb[:, blk, :N], start=True, stop=True),
                         ['Wbd', ('xrb', b)], [psu(blk)])
                    T.op('pe', lambda e, blk=blk: e.matmul(PS[4 + blk][:, :N], lhsT=Wbd[:, d, 1, blk, :],
                                                         rhs=xrb[:, blk, :N], start=True, stop=True),
                         ['Wbd', ('xrb', b)], [psu(4 + blk)])
                for blk in range(4):
                    T.op('act', lambda e, blk=blk: e.activation(out=rg[:, blk, :N], in_=PS[blk][:, :N], func=AF.Sigmoid,
                                                               bias=ba[:, d, blk:blk + 1]), [psu(blk), 'ba'],
                         [('rg', b, blk)])
                for blk in range(4):
                    T.op('act', lambda e, blk=blk: e.activation(out=ig[:, blk, :N], in_=PS[4 + blk][:, :N],
                                                               func=AF.Sigmoid, bias=bx[:, d, blk:blk + 1]),
                         [psu(4 + blk), 'bx'], [('ig', b, blk)])
                for blk in range(4):
                    T.op('act', lambda e, blk=blk: e.activation(out=aa[:, blk, :N], in_=rg[:, blk, :N], func=AF.Exp,
                                                               scale=cl[:, d, blk:blk + 1]), [('rg', b, blk), 'cl'],
                         [('aa', b, blk)])
                for blk in range(4):
                    T.op('act', lambda e, blk=blk: e.activation(out=a2[:, blk, :N], in_=rg[:, blk, :N], func=AF.Exp,
                                                               scale=cl2[:, d, blk:blk + 1]), [('rg', b, blk), 'cl2'],
                         [('a2', b, blk)])
                a2u = [('a2', b, blk) for blk in range(4)]
                igu = [('ig', b, blk) for blk in range(4)]
                T.op('act', lambda e: e.activation(out=a2[:, :, :N], in_=a2[:, :, :N], func=AF.Sqrt, bias=1.0,
                                                   scale=-1.0), a2u, a2u)
                T.op('dve', lambda e: e.tensor_tensor(out=ig[:, :, :N], in0=ig[:, :, :N], in1=a2[:, :, :N],
                                                      op=ALU.mult), igu + a2u, igu)
                T.op('dve', lambda e: e.tensor_tensor(out=ig[:, :, :N], in0=ig[:, :, :N], in1=xr[:, :, :N],
                                                      op=ALU.mult), igu + xru, igu)
        def stage2(d, ci, b):
            if True:
                st, N, isctx = CHUNKS[ci]
                xr, xrb, rg, ig, aa, a2 = xr_[b], xrb_[b], rg_[b], ig_[b], aa_[b], a2_[b]
                hu = [('hout', b, blk) for blk in range(4)]
                for blk in range(4):
                    if d == 0:
                        T.op('dve', lambda e, blk=blk: e.tensor_tensor_scan(
                            out=hout[b][:, blk, :N], data0=aa[:, blk, :N], data1=ig[:, blk, :N],
                            initial=hprev[:, blk:blk + 1], op0=ALU.mult, op1=ALU.add),
                            [('aa', b, blk), ('ig', b, blk), 'hprev'], [hu[blk]])
                    else:
                        T.op('dve', lambda e, blk=blk: e.tensor_tensor_scan(
                            out=hout[b][:, blk, :N][:, ::-1], data0=aa[:, blk, :N][:, ::-1],
                            data1=ig[:, blk, :N][:, ::-1], initial=hprev[:, blk:blk + 1], op0=ALU.mult, op1=ALU.add),
                            [('aa', b, blk), ('ig', b, blk), 'hprev'], [hu[blk]])
                lastc = N - 1 if d == 0 else 0
                T.op('dve', lambda e: e.tensor_copy(out=hprev[:, :], in_=hout[b][:, :, lastc]), hu, ['hprev'])
                if isctx:
                    return
                if d == 0:
                    T.dma('sp', HL.ap().rearrange("b p t -> p b t")[:, :, st:st + N], hout[b][:, :, :N],
                          reads=hu, writes=[('HL', ci)])
                else:
                    T.op('dve', lambda e: e.tensor_tensor(out=hl[b][:, :, :N], in0=hl[b][:, :, :N],
                                                          in1=hout[b][:, :, :N], op=ALU.add),
                         hu + [('hl', b)], [('hl', b)])
                    T.op('dve', lambda e: e.tensor_tensor(out=yo[b][:, :, :N], in0=hl[b][:, :, :N],
                                                          in1=gz[b][:, :, :N], op=ALU.mult),
                         [('hl', b), ('gz', b)], [('yol', b)])
                    T.dma('sp', YT.ap()[4:8].rearrange("b p t -> p b t")[:, :, st:st + N], yo[b][:, :, :N],
                          reads=[('yol', b)], writes=[('YT', ci)])

        steps = []
        for d in range(2):
            order = [0] + (list(range(1, 17)) if d == 0 else list(range(16, 0, -1)))
            for ci in order:
                steps.append((d, ci))
        stage1(steps[0][0], steps[0][1], 0)
        for i_, (d, ci) in enumerate(steps):
            if i_ + 1 < len(steps):
                stage1(steps[i_ + 1][0], steps[i_ + 1][1], (i_ + 1) % 2)
            if ci == 0:
                T.op('dve', lambda e: e.memset(hprev[:, :], 0.0), [], ['hprev'])
            stage2(d, ci, i_ % 2)

    def phase3bc(li, chunk_ids, final):
        Wi = T.alloc([128, 8, 2 * DFF], BF16, 'Wi')
        for kc in range(8):
            T.dma('pool', Wi[:, kc, :], ffn_in.ap()[li, kc * 128:(kc + 1) * 128, :], writes=[('Wi', kc)])
        Wo = load_Wo(li)
        Wou = [('Wo2', f0) for f0 in range(0, 22, 2)]
        NS = 256
        hT = [T.alloc([128, 8, NS], BF16) for _ in range(2)]
        xT = [T.alloc([128, 8, NS], F32) for _ in range(2)]
        ac = [T.alloc([128, 22, NS], BF16) for _ in range(2)]
        sg = [T.alloc([128, NS], F32) for _ in range(2)]
        if final:
            tmp = T.alloc([128, 8, NS], F32)
            sq = T.alloc([128, 8, NS], BF16)
            rstd = T.alloc([128, NS], F32)
            otok = [T.alloc([128, D], F32) for _ in range(2)]
        subs = []
        for ci in chunk_ids:
            st, N, isctx = CHUNKS[ci]
            for s0 in range(0, N, NS):
                subs.append((ci, st + s0, isctx))
        cnt = {'n': 0, 'o': 0, 'no': 0}

        def load(si):
            ci, st, isctx = subs[si]
            b = si % 2
            T.dma('sp', hT[b][:, :, :], H2.ap()[:, :, st:st + NS], reads=[('H2', ci)], writes=[('hT', b)])
            T.dma('sp', xT[b][:, :, :], R.ap()[:, :, st:st + NS], reads=[('R', ci)], writes=[('xT', b)])

        load(0)
        for si, (ci, st, isctx) in enumerate(subs):
            b = si % 2
            who = 1 if isctx else 0
            if si + 1 < len(subs):
                load(si + 1)
            for fb in range(22):
                pg = (cnt['n'] % 3) * 2
                s2 = cnt['n'] % 2
                cnt['n'] += 1
                for (col0, pbk) in ((fb * 128, pg), (DFF + fb * 128, pg + 1)):
                    for kc in range(8):
                        T.op('pe', lambda e, col0=col0, pbk=pbk, kc=kc: e.matmul(
                            PS[pbk][:, :NS], lhsT=Wi[:, kc, col0:col0 + 128], rhs=hT[b][:, kc, :],
                            start=(kc == 0), stop=(kc == 7)), [('hT', b), ('Wi', kc)], [psu(pbk)])
                T.op('act', lambda e: e.activation(out=sg[s2][:, :], in_=PS[pg][:, :NS], func=AF.Silu),
                     [psu(pg)], [('sg', s2)])
                T.op('dve', lambda e, fb=fb: e.tensor_tensor(
                    out=ac[b][:, fb, :], in0=PS[pg + 1][:, :NS], in1=sg[s2][:, :], op=ALU.mult),
                    [psu(pg + 1), ('sg', s2)], [('ac', b, fb)])
            for j in range(8):
                pb = 6 + cnt['o'] % 2
                cnt['o'] += 1
                for fb in range(22):
                    T.op('pe', lambda e, fb=fb, j=j: e.matmul(
                        PS[pb][:, :NS], lhsT=Wo[:, fb, j * 128:(j + 1) * 128], rhs=ac[b][:, fb, :],
                        start=(fb == 0), stop=(fb == 21)), [('ac', b, fb)] + Wou, [psu(pb)])
                g_ap = modT[:, li, 5 * 8 + j, who:who + 1]
                T.op('dve', lambda e, j=j: e.scalar_tensor_tensor(
                    out=xT[b][:, j, :], in0=PS[pb][:, :NS], scalar=g_ap, in1=xT[b][:, j, :],
                    op0=ALU.mult, op1=ALU.add), [psu(pb), ('xT', b), 'modT'], [('xT', b)])
            if not final:
                T.dma('sp', R.ap()[:, :, st:st + NS], xT[b][:, :, :], reads=[('xT', b)], writes=[('R', ci)])
            else:
                rmsnorm_mod(xT[b], ('xT', b), NS, li, 0, None, None, None, tmp, 'tmp', sq, 'sq', rstd, 'rstd', 0)
                for tt in range(NS // 128):
                    o2 = cnt['no'] % 2
                    cnt['no'] += 1
                    for j in range(8):
                        pbk = 2 + (j // 4) + 2 * o2
                        T.op('pe', lambda e, j=j, tt=tt, pbk=pbk: e.transpose(
                            PS[pbk][:, (j % 4) * 128:(j % 4 + 1) * 128], tmp[:, j, tt * 128:(tt + 1) * 128],
                            ident[:]), ['tmp', 'ident'], [psu(pbk)])
                    for hf in range(2):
                        pbk = 2 + hf + 2 * o2
                        T.op('act' if hf else 'dve', (lambda e, pbk=pbk, o2=o2, hf=hf: e.activation(
                            out=otok[o2][:, hf * 512:(hf + 1) * 512], in_=PS[pbk][:, :], func=AF.Copy)) if hf else
                            (lambda e, pbk=pbk, o2=o2, hf=hf: e.tensor_copy(
                                out=otok[o2][:, hf * 512:(hf + 1) * 512], in_=PS[pbk][:, :])),
                            [psu(pbk)], [('otok', o2, hf)])
                    r0_ = st - 256 + tt * 128
                    T.dma('sp', out.ap()[r0_:r0_ + 128, :], otok[o2][:, :],
                          reads=[('otok', o2, 0), ('otok', o2, 1)], writes=[('out', ci, si, tt)])

    ALLC = list(range(len(CHUNKS)))
    LAT = list(range(1, len(CHUNKS)))
    phase0()
    T.barrier(PERSIST)
    phase1()
    T.barrier(PERSIST)
    phase2a()
    T.barrier(PERSIST)
    phase2b()
    T.barrier(PERSIST)
    phase3a(0, [(ATd.ap(), 4, 128), (BTd.ap(), 8, 64)],
            [(wo0A.ap(), 4, 128), (wo0B.ap(), 8, 64)], ALLC)
    T.barrier(PERSIST)
    phase3bc(0, ALLC, final=False)
    T.barrier(PERSIST)
    phaseP1()
    T.barrier(PERSIST)
    phaseG()
    T.barrier(PERSIST)
    phaseLr()
    T.barrier(PERSIST)
    phase3a(1, [(YT.ap(), 8, 128)], [(wo1.ap().rearrange("(q p) d -> p q d", p=128), 8, 128)], LAT)
    T.barrier(PERSIST)
    phase3bc(1, LAT, final=True)
    T.barrier(PERSIST)
    if debug:
        for ci in ALLC:
            st, N, _ = CHUNKS[ci]
            for j in range(8):
                T.dma('sp', dbg.ap()[:, j, st:st + N], R.ap()[:, j, st:st + N], reads=[('R', ci)], writes=[('dbg', ci, j)])
        for h in range(8):
            T.dma('sp', dbg2.ap()[h], YT.ap()[h], reads=[('YT', ci) for ci in ALLC], writes=[('dbg2', h)])
        T.barrier(PERSIST)
    T.emit()
    return nc


def _bf(a):
    return np.asarray(a).astype(ml_dtypes.bfloat16)


def make_consts():
    inv = (10000.0 ** (-np.arange(0, 32, 2, dtype=np.float32) / 32.0)).astype(np.float32)
    t = np.arange(L)
    row = (t // 64).astype(np.float32)
    col = (t % 64).astype(np.float32)
    C = np.ones((128, NT), np.float32)
    S = np.zeros((128, NT), np.float32)
    for p in range(128):
        dd = p % 64
        grp, within = dd // 32, dd % 32
        f, part = within % 16, within // 16
        ang = (row if grp == 0 else col) * inv[f]
        C[p, LC:] = np.cos(ang.astype(np.float32))
        sn = np.sin(ang.astype(np.float32))
        S[p, LC:] = -sn if part == 0 else sn
    j = np.arange(128)[:, None]
    q = np.arange(128)[None, :]
    m0 = np.where(j >= q, 0.0, NEG).astype(np.float32)
    m1 = np.where(j <= q, 0.0, NEG).astype(np.float32)
    maskb = np.stack([np.tile(m0, (1, 4)), np.tile(m1, (1, 4))]).astype(ml_dtypes.bfloat16)
    cp = np.arange(64)[:, None]
    c_ = np.arange(64)[None, :]
    tri = np.stack([cp <= c_, cp >= c_, cp > c_, cp < c_]).astype(np.float32)
    mk4 = np.stack([np.tile((cp <= c_), (1, 4)), np.tile((cp >= c_), (1, 4))]).astype(np.float32)
    return C, S, maskb, tri, mk4


def perm64(cols):
    cols = np.asarray(cols)
    dd = np.arange(64)
    within = dd % 32
    partner = np.where(within // 16 == 0, dd + 16, dd - 16)
    out = cols.reshape(-1, 64)[:, partner].reshape(-1)
    return out


def prep_inputs(inp, b):
    f = np.float32
    x, c, ctx = inp['x'], inp['c'], inp['ctx']
    m = {}
    m['xall'] = np.ascontiguousarray(np.concatenate([ctx[b], x[b]], 0), f)
    cc = np.stack([c[b].reshape(8, 128).T, inp['c_ctx'].reshape(8, 128).T], -1)
    m['ccols'] = np.ascontiguousarray(cc, f)
    m['ada_w'] = inp['ada_w']
    m['ada_b2'] = np.ascontiguousarray(np.repeat(inp['ada_b'][:, None, :], 2, 1), f)
    m['normg'] = np.ascontiguousarray(inp['norm_g'].reshape(2, 2, 8, 128).transpose(3, 0, 1, 2), f)
    m['finalg'] = np.ascontiguousarray(inp['final_g'].reshape(8, 128).T, f)
    w = inp['even_w_in'][0]
    qa = np.arange(0, 512)
    ka = np.arange(512, 1024)
    va = np.arange(1024, 1536)
    qb = 1536 + np.arange(512).reshape(2, 4, 64).transpose(1, 0, 2).reshape(-1)
    kb = np.arange(2048, 2176)
    vb = np.arange(2176, 2304)
    cols = np.concatenate([qa, perm64(qa), ka, perm64(ka), qb, perm64(qb), kb, perm64(kb), va, vb])
    m['w0'] = np.ascontiguousarray(w[:, cols], f)
    wo = inp['even_w_out'][0]
    m['wo0A'] = np.ascontiguousarray(wo[:512].reshape(4, 128, D).transpose(1, 0, 2), f)
    m['wo0B'] = np.ascontiguousarray(wo[512:].reshape(8, 64, D).transpose(1, 0, 2), f)
    m['difflam'] = np.ascontiguousarray(np.broadcast_to(inp['diff_lam'][0][None], (128, 4, 64)), f)
    m['sinkc'] = np.ascontiguousarray(np.broadcast_to(inp['win_sink'][0][None], (128, 8)), f)
    m['ident'] = np.eye(128, dtype=f)
    m['ffn_in'] = inp['ffn_w_in']
    m['w1'] = np.ascontiguousarray(inp['odd_w_in'][0], f)
    m['wo1'] = np.ascontiguousarray(inp['odd_w_out'][0], f)
    m['gw'] = np.ascontiguousarray(inp['gla_gate_w'][0], f)
    gb = inp['gla_gate_b'][0]
    m['gb2'] = np.ascontiguousarray(np.broadcast_to(np.concatenate([gb, gb], -1)[:, None, :], (2, 64, 512)), f)
    m['glag'] = np.ascontiguousarray(inp['gla_norm_g'][0].reshape(128, 1), f)
    m['cw'] = np.ascontiguousarray(inp['lru_conv_w'][0].reshape(4, 4, 128).transpose(2, 1, 0), f)
    m['cb'] = np.ascontiguousarray(inp['lru_conv_b'][0].reshape(4, 128).T, f)
    m['lwa'] = np.ascontiguousarray(inp['lru_wa'][0], f)
    m['lwx'] = np.ascontiguousarray(inp['lru_wx'][0], f)
    m['lba'] = np.ascontiguousarray(inp['lru_ba'][0].reshape(2, 4, 128).transpose(2, 0, 1), f)
    m['lbx'] = np.ascontiguousarray(inp['lru_bx'][0].reshape(2, 4, 128).transpose(2, 0, 1), f)
    m['llam'] = np.ascontiguousarray(inp['lru_lam'][0].reshape(2, 4, 128).transpose(2, 0, 1), f)
    m['ffn_out'] = inp['ffn_w_out']
    return m


_CACHE = {}


def kernel(**inputs):
    inp = {k: np.asarray(v) for k, v in inputs.items()}
    if 'nc' not in _CACHE:
        _CACHE['nc'] = build()
        _CACHE['consts'] = make_consts()
    nc = _CACHE['nc']
    C, S, maskb, tri, mk4 = _CACHE['consts']
    in_maps = []
    for b in range(NCORES):
        m = prep_inputs(inp, b)
        m['ropeC'], m['ropeS'], m['maskb'], m['tri'], m['mask4'] = C, S, maskb, tri, mk4
        in_maps.append(m)
    res = run_bass_kernel_spmd(nc, in_maps, core_ids=list(range(NCORES)))
    return np.stack([np.asarray(res.results[b]['out']) for b in range(NCORES)], 0).astype(np.float32)
```

```python
import math
import numpy as np
import ml_dtypes
import concourse.bass as bass
import concourse.mybir as mybir
from concourse.bass_utils import run_bass_kernel_spmd

F32, BF16 = mybir.dt.float32, mybir.dt.bfloat16
AF = mybir.ActivationFunctionType
ALU = mybir.AluOpType

D = 1024
L = 8192
LC = 256
NT = L + LC
NCORES = 4
DFF = 2816
EPS = 1e-6
CHUNKS = [(0, 256, True)] + [(256 + 512 * i, 512, False) for i in range(16)]
NEG = -240000.0
ENGS = ['pe', 'act', 'dve', 'pool', 'sp']
NDMA = {'sp': 40, 'pool': 16}


import types


def _freeze(fn):
    if fn is None or fn.__closure__ is None:
        return fn
    cells = []
    for c in fn.__closure__:
        try:
            cells.append(types.CellType(c.cell_contents))
        except ValueError:
            cells.append(c)
    return types.FunctionType(fn.__code__, fn.__globals__, fn.__name__, fn.__defaults__, tuple(cells))


class Tr:
    def __init__(s, nc):
        s.nc = nc
        s.ops = {e: [] for e in ENGS}
        s.lastw = {}
        s.rd = {}
        s.ndma = {'sp': 0, 'pool': 0}
        s.dma_since = []
        s.arena = 16512
        s.names = 0

    END = 229248

    def alloc(s, shape, dt, name=None):
        nb = int(np.prod(shape[1:])) * (4 if dt == F32 else 2)
        nb = (nb + 63) // 64 * 64
        if not hasattr(s, 'free'):
            s.free = [[s.arena, s.END]]
            s.pf_next = []
            s.keep_next = set()
            s.in_pf = False
        off = None
        if s.in_pf:
            for seg in reversed(s.free):
                if seg[1] - seg[0] >= nb:
                    seg[1] -= nb
                    off = seg[1]
                    break
            if off is not None:
                s.pf_next.append((off, off + nb))
        else:
            for seg in s.free:
                if seg[1] - seg[0] >= nb:
                    off = seg[0]
                    seg[0] += nb
                    break
        assert off is not None, ("sbuf overflow", nb, s.free)
        s.names += 1
        return s.nc.alloc_sbuf_tensor_at(f"{name or 't'}_{s.names}", list(shape), dt, offset=off)

    def prefetch(s, fn):
        if not hasattr(s, 'free'):
            s.alloc([128, 16], F32)
        s.in_pf = True
        r = fn()
        s.in_pf = False
        return r

    def op(s, eng, fn, reads=(), writes=(), dma=False):
        raw, oth = set(), set()
        for u in reads:
            w = s.lastw.get(u)
            if w is not None:
                raw.add(w)
        for u in writes:
            w = s.lastw.get(u)
            if w is not None:
                oth.add(w)
            r = s.rd.get(u)
            if r:
                for e2, i2 in r[0].items():
                    oth.add((e2, i2))
                for x in r[1]:
                    oth.add(x)
        idx = len(s.ops[eng])
        me = (eng, idx)
        deps = set()
        for dset, israw in ((raw, True), (oth, False)):
            for (e2, i2) in dset:
                d2 = s.ops[e2][i2]
                if e2 == eng and not d2['dma'] and not dma:
                    if eng == 'pe' or not israw:
                        continue
                deps.add((e2, i2))
        rec = dict(fn=_freeze(fn), deps=deps, inc=False, dma=dma)
        if dma:
            rec['k'] = s.ndma[eng]
            rec['q'] = eng
            s.ndma[eng] += 1
            if getattr(s, 'in_pf', False):
                for u in writes:
                    s.keep_next.add(u)
            else:
                s.dma_since.append(me)
        s.ops[eng].append(rec)
        for u in reads:
            r = s.rd.setdefault(u, ({}, []))
            if dma:
                r[1].append(me)
            else:
                r[0][eng] = idx
        for u in writes:
            s.lastw[u] = me
            s.rd[u] = ({}, [])
        return me

    def dma(s, q, out, in_, reads=(), writes=()):
        return s.op(q, lambda e: e.dma_start(out=out, in_=in_), reads, writes, dma=True)

    def barrier(s, reset_to=None):
        lasts = []
        for e in ENGS:
            for i in range(len(s.ops[e]) - 1, -1, -1):
                if s.ops[e][i]['fn'] is not None:
                    if not s.ops[e][i]['dma']:
                        lasts.append((e, i))
                    break
        deps = set(lasts) | set(s.dma_since)
        for e in ENGS:
            s.ops[e].append(dict(fn=None, deps=set(d for d in deps), inc=False, dma=False))
        s.dma_since = []
        keep = getattr(s, 'keep_next', set())
        s.lastw = {u: w for u, w in s.lastw.items() if u in keep}
        s.rd = {u: ({}, []) for u in s.lastw}
        s.keep_next = set()
        if reset_to is not None:
            segs = [[reset_to, s.END]]
            for (lo, hi) in sorted(getattr(s, 'pf_next', [])):
                new = []
                for a_, b_ in segs:
                    if hi <= a_ or lo >= b_:
                        new.append([a_, b_])
                    else:
                        if a_ < lo:
                            new.append([a_, lo])
                        if hi < b_:
                            new.append([hi, b_])
                segs = new
            s.free = segs
            s.pf_next = []

    def emit(s):
        nc = s.nc
        for e in ENGS:
            for rec in s.ops[e]:
                for (e2, i2) in rec['deps']:
                    d2 = s.ops[e2][i2]
                    if not d2['dma']:
                        d2['inc'] = True
        for e in ENGS:
            c = 0
            for rec in s.ops[e]:
                if rec['inc']:
                    c += 1
                    rec['cnt'] = c
        import contextlib
        with contextlib.ExitStack() as st:
            esem = {e: st.enter_context(nc.semaphore(f"s_{e}")) for e in ENGS if e != 'sp'}
            esem['sp'] = st.enter_context(nc.semaphore("s_sp"))
            dsem = {q: [st.enter_context(nc.semaphore(f"d_{q}_{i}")) for i in range(n)] for q, n in NDMA.items()}
            block = st.enter_context(nc.Block())

            def run(ename):
                def body(eng):
                    waited = {}
                    for rec in s.ops[ename]:
                        need = {}
                        for (e2, i2) in rec['deps']:
                            d2 = s.ops[e2][i2]
                            if d2['dma']:
                                k, q = d2['k'], d2['q']
                                key, val = ('d', q, k % NDMA[q]), 16 * (k // NDMA[q] + 1)
                            else:
                                key, val = ('e', e2), d2['cnt']
                            if val > need.get(key, 0):
                                need[key] = val
                        if rec['dma'] and rec['k'] >= NDMA[rec['q']]:
                            nq = NDMA[rec['q']]
                            key, val = ('d', rec['q'], rec['k'] % nq), 16 * (rec['k'] // nq)
                            if val > need.get(key, 0):
                                need[key] = val
                        todo = []
                        for key, val in need.items():
                            if waited.get(key, 0) >= val:
                                continue
                            waited[key] = val
                            sem = dsem[key[1]][key[2]] if key[0] == 'd' else esem[key[1]]
                            todo.append((sem, val))
                        attach = None
                        if todo and rec['fn'] is not None and not rec['dma'] and ename != 'pe':
                            attach = todo.pop()
                        for sem, val in todo:
                            eng.wait_ge(sem, val)
                        if rec['fn'] is None:
                            continue
                        ins = rec['fn'](eng)
                        if attach is not None:
                            ins._wait_ge(attach[0], attach[1])
                        if rec['dma']:
                            ins.then_inc(dsem[rec['q']][rec['k'] % NDMA[rec['q']]], 16)
                        elif rec['inc']:
                            ins.then_inc(esem[ename], 1)
                return body

            block.tensor(run('pe'))
            block.scalar(run('act'))
            block.vector(run('dve'))
            block.gpsimd(run('pool'))
            block.sync(run('sp'))


def build(debug=False, stop_after=None):
    nc = bass.Bass("TRN2", target_bir_lowering=False)
    T = Tr(nc)

    def din(name, shape, dt=F32):
        return nc.dram_tensor(name, list(shape), dt, kind="ExternalInput")

    def dscr(name, shape, dt):
        return nc.dram_tensor(name, list(shape), dt, kind="Internal")

    xall = din("xall", [NT, D])
    ccols = din("ccols", [128, 8, 2])
    ada_w = din("ada_w", [2, D, 6 * D])
    ada_b2 = din("ada_b2", [2, 2, 6 * D])
    normg = din("normg", [128, 2, 2, 8])
    finalg = din("finalg", [128, 8])
    w0 = din("w0", [D, 3968])
    wo0A = din("wo0A", [128, 4, D])
    wo0B = din("wo0B", [64, 8, D])
    difflam = din("difflam", [128, 4, 64])
    sinkc = din("sinkc", [128, 8])
    ropeC = din("ropeC", [128, NT])
    ropeS = din("ropeS", [128, NT])
    maskb = din("maskb", [2, 128, 512], BF16)
    ident_in = din("ident", [128, 128])
    ffn_in = din("ffn_in", [2, D, 2 * DFF])
    ffn_out = din("ffn_out", [2, DFF, D])
    w1 = din("w1", [D, 2592])
    wo1 = din("wo1", [D, D])
    gw_in = din("gw", [2, 16, 256])
    gb2_in = din("gb2", [2, 64, 512])
    glag_in = din("glag", [128, 1])
    cw_in = din("cw", [128, 4, 4])
    cb_in = din("cb", [128, 4])
    lwa = din("lwa", [2, 8, 64, 64])
    lwx = din("lwx", [2, 8, 64, 64])
    lba = din("lba", [128, 2, 4])
    lbx = din("lbx", [128, 2, 4])
    llam = din("llam", [128, 2, 4])
    tri_in = din("tri", [4, 64, 64])
    mask4_in = din("mask4", [2, 64, 256])
    out = nc.dram_tensor("out", [L, D], F32, kind="ExternalOutput")
    QS = dscr("QS", [2, 4, 64, NT], BF16)
    KS = dscr("KS", [2, 4, 64, NT], BF16)
    KD = dscr("KD", [2, NT, 256], BF16)
    V1 = dscr("V1", [NT, 512], BF16)
    ZX = dscr("ZX", [4, 128, NT], F32)
    GZ = dscr("GZ", [4, 128, NT], BF16)
    SG = dscr("SG", [4, 128, NT], BF16)
    HL = dscr("HL", [4, 128, NT], F32)
    OG = dscr("OG", [4, 128, NT], F32)
    YT = dscr("YT", [8, 128, NT], BF16)

    R = dscr("R", [128, 8, NT], F32)
    QAT = dscr("QAT", [4, 128, NT], BF16)
    KAT = dscr("KAT", [4, 128, NT], BF16)
    QBT = dscr("QBT", [4, 128, NT], BF16)
    KBT = dscr("KBT", [128, NT], BF16)
    VA = dscr("VA", [NT, 512], BF16)
    VB = dscr("VB", [NT, 128], BF16)
    ATd = dscr("ATd", [4, 128, NT], BF16)
    BTd = dscr("BTd", [8, 64, NT], BF16)
    H2 = dscr("H2", [128, 8, NT], BF16)
    ACTd = dscr("ACTd", [22, 128, NT], BF16)
    dbg = nc.dram_tensor("dbg", [128, 8, NT], F32, kind="ExternalOutput") if debug else None
    dbg2 = nc.dram_tensor("dbg2", [8, 128, NT], BF16, kind="ExternalOutput") if debug else None

    P2 = [nc.alloc_psum_tensor(f"ps{i}", [128, 1024], F32) for i in range(4)]
    PS = []
    for i in range(4):
        PS.append(P2[i][:, 0:512])
        PS.append(P2[i][:, 512:1024])

    def psu(i):
        return ('ps', i)

    ident = T.alloc([128, 128], F32, 'ident')
    identb = T.alloc([128, 128], BF16, 'identb')
    onesb = T.alloc([128, 128], BF16, 'onesb')
    modT = T.alloc([128, 2, 48, 2], F32, 'modT')
    Acol = T.alloc([128, 2, 2, 8, 2], F32, 'Acol')
    ngc = T.alloc([128, 2, 2, 8], F32, 'ngc')
    fgc = T.alloc([128, 8], F32, 'fgc')
    neglam = T.alloc([128, 1], F32, 'neglam')
    esink = T.alloc([128, 8], F32, 'esink')
    T.dma('sp', ident[:], ident_in.ap(), writes=['ident'])
    T.dma('sp', ngc[:], normg.ap(), writes=['ngc'])
    T.dma('sp', fgc[:], finalg.ap(), writes=['fgc'])
    T.op('dve', lambda e: e.tensor_copy(out=identb[:], in_=ident[:]), ['ident'], ['identb'])
    T.op('dve', lambda e: e.memset(onesb[:], 1.0), [], ['onesb'])
    decs = T.alloc([64, 2, 4, 132], F32, 'decs')
    PERSIST = T.free[0][0]

    def eps_bias(e):
        return EPS

    def phase0():
        cc = T.alloc([128, 8, 2], F32)
        cond = T.alloc([128, 8, 2], F32)
        T.dma('sp', cc[:], ccols.ap(), writes=['cc'])
        T.op('act', lambda e: e.activation(out=cond[:], in_=cc[:], func=AF.Silu), ['cc'], ['cond'])
        wb = [T.alloc([128, 3072], F32) for _ in range(3)]
        modrow = T.alloc([2, 6144], F32)
        adab = T.alloc([2, 6144], F32)
        nload = 0
        for li in range(2):
            T.dma('sp', adab[:], ada_b2.ap()[li], writes=['adab'])
            for half in range(2):
                for kc in range(8):
                    w = wb[nload % 3]
                    wu = ('wb', nload % 3)
                    nload += 1
                    T.dma('sp', w[:], ada_w.ap()[li, kc * 128:(kc + 1) * 128, half * 3072:(half + 1) * 3072],
                          writes=[wu])
                    for cb in range(6):
                        T.op('pe', lambda e, w=w, cb=cb, kc=kc: e.matmul(
                            PS[cb][0:2, :], lhsT=cond[:, kc, :], rhs=w[:, cb * 512:(cb + 1) * 512],
                            start=(kc == 0), stop=(kc == 7)), [wu, 'cond'], [psu(cb)])
                for cb in range(6):
                    c0 = half * 3072 + cb * 512
                    T.op('dve', lambda e, cb=cb, c0=c0: e.tensor_tensor(
                        out=modrow[:, c0:c0 + 512], in0=PS[cb][0:2, :], in1=adab[:, c0:c0 + 512], op=ALU.add),
                        [psu(cb), 'adab'], ['modrow'])
            for blk in range(48):
                T.op('pe', lambda e, blk=blk: e.transpose(
                    PS[6][:, blk * 2:blk * 2 + 2], modrow[0:2, blk * 128:(blk + 1) * 128], ident[0:2, 0:2]),
                    ['modrow', 'ident'], [psu(6)])
            T.op('dve', lambda e, li=li: e.tensor_copy(
                out=modT[:, li].rearrange("p a b -> p (a b)"), in_=PS[6][:, 0:96]), [psu(6)], ['modT'])
            for n in range(2):
                for w_ in range(2):
                    T.op('dve', lambda e, li=li, n=n, w_=w_: e.scalar_tensor_tensor(
                        out=Acol[:, li, n, :, w_], in0=modT[:, li, (3 * n + 1) * 8:(3 * n + 2) * 8, w_], scalar=1.0,
                        in1=ngc[:, li, n, :], op0=ALU.add, op1=ALU.mult), ['modT', 'ngc'], ['Acol'])
        dl = T.alloc([128, 4, 64], F32)
        pr = T.alloc([128, 2, 64], F32)
        sm = T.alloc([128, 2], F32)
        ex = T.alloc([128, 2], F32)
        T.dma('sp', dl[:], difflam.ap(), writes=['dl'])
        T.op('dve', lambda e: e.tensor_tensor(out=pr[:, 0, :], in0=dl[:, 0, :], in1=dl[:, 1, :], op=ALU.mult),
             ['dl'], ['pr0'])
        T.op('dve', lambda e: e.tensor_tensor(out=pr[:, 1, :], in0=dl[:, 2, :], in1=dl[:, 3, :], op=ALU.mult),
             ['dl'], ['pr1'])
        T.op('dve', lambda e: e.reduce_sum(out=sm[:], in_=pr[:], axis=mybir.AxisListType.X), ['pr0', 'pr1'], ['sm'])
        T.op('act', lambda e: e.activation(out=ex[:], in_=sm[:], func=AF.Exp), ['sm'], ['ex'])
        T.op('dve', lambda e: e.scalar_tensor_tensor(out=neglam[:], in0=ex[:, 1:2], scalar=-0.2, in1=ex[:, 0:1],
                                                     op0=ALU.add, op1=ALU.subtract), ['ex'], ['neglam'])
        sk = T.alloc([128, 8], F32)
        T.dma('sp', sk[:], sinkc.ap(), writes=['sk'])
        T.op('act', lambda e: e.activation(out=esink[:], in_=sk[:], func=AF.Exp), ['sk'], ['esink'])

    def rmsnorm_mod(xT, xu, N, li, n, who, hT, hu, tmp, tu, sq, squ, rstd, ru, psb):
        T.op('act', lambda e: e.activation(out=sq[:, :, :N], in_=xT[:, :, :N], func=AF.Square), [xu], [squ])
        for j in range(8):
            T.op('pe', lambda e, j=j: e.matmul(PS[psb][:, :N], lhsT=onesb[:], rhs=sq[:, j, :N],
                                               start=(j == 0), stop=(j == 7)), [squ, 'onesb'], [psu(psb)])
        T.op('act', lambda e: e.activation(out=rstd[:, :N], in_=PS[psb][:, :N], func=AF.Sqrt, bias=EPS,
                                           scale=1.0 / D), [psu(psb)], [ru])
        T.op('dve', lambda e: e.reciprocal(out=rstd[:, :N], in_=rstd[:, :N]), [ru], [ru])
        for j in range(8):
            if who is None:
                a_ap = fgc[:, j:j + 1]
            else:
                a_ap = Acol[:, li, n, j, who:who + 1]
            T.op('dve', lambda e, j=j, a_ap=a_ap: e.scalar_tensor_tensor(
                out=tmp[:, j, :N], in0=xT[:, j, :N], scalar=a_ap, in1=rstd[:, :N], op0=ALU.mult, op1=ALU.mult),
                [xu, ru, 'Acol', 'fgc'], [tu])
            if who is not None:
                b_ap = modT[:, li, (3 * n) * 8 + j, who:who + 1]
                T.op('act', lambda e, j=j, b_ap=b_ap: e.activation(
                    out=hT[:, j, :N], in_=tmp[:, j, :N], func=AF.Identity, bias=b_ap, scale=1.0),
                    [tu, 'modT'], [hu])

    def phase1():
        W = T.alloc([128, 8, 3968], BF16, 'w0')
        for kc in range(8):
            T.dma('pool', W[:, kc, :], w0.ap()[kc * 128:(kc + 1) * 128, :], writes=[('W', kc)])
        Wu = [('W', kc) for kc in range(8)]
        xtok = [T.alloc([128, 4, D], F32) for _ in range(2)]
        xT = [T.alloc([128, 8, 512], F32) for _ in range(2)]
        hT = [T.alloc([128, 8, 512], BF16) for _ in range(2)]
        tmp = T.alloc([128, 8, 512], F32)
        sq = T.alloc([128, 8, 512], BF16)
        rstd = T.alloc([128, 512], F32)
        rc = [T.alloc([128, 512], F32) for _ in range(2)]
        rs = [T.alloc([128, 512], F32) for _ in range(2)]
        t1 = [T.alloc([128, 512], F32) for _ in range(2)]
        t2 = [T.alloc([128, 512], F32) for _ in range(2)]
        ob = [T.alloc([128, 512], BF16) for _ in range(4)]
        vb = [T.alloc([128, 512], BF16) for _ in range(2)]
        nrope = 0
        nv = 0
        def prep(ci):
            st, N, isctx = CHUNKS[ci]
            b = ci % 2
            who = 1 if isctx else 0
            ntt = N // 128
            for tt in range(ntt):
                T.dma('sp', xtok[b][:, tt, :], xall.ap()[st + tt * 128: st + (tt + 1) * 128, :],
                      writes=[('xtok', b, tt)])
            T.dma('sp', rc[b][:, :N], ropeC.ap()[:, st:st + N], writes=[('rc', b)])
            T.dma('sp', rs[b][:, :N], ropeS.ap()[:, st:st + N], writes=[('rs', b)])
            for j in range(8):
                pb = j % 2
                for tt in range(ntt):
                    T.op('pe', lambda e, j=j, tt=tt, pb=pb, b=b: e.transpose(
                        PS[pb][:, tt * 128:(tt + 1) * 128], xtok[b][:, tt, j * 128:(j + 1) * 128], ident[:]),
                        [('xtok', b, tt), 'ident'], [psu(pb)])
                T.op('act' if j % 2 else 'dve', (lambda e, j=j, pb=pb, b=b, N=N: e.activation(
                    out=xT[b][:, j, :N], in_=PS[pb][:, :N], func=AF.Copy)) if j % 2 else
                    (lambda e, j=j, pb=pb, b=b, N=N: e.tensor_copy(out=xT[b][:, j, :N], in_=PS[pb][:, :N])),
                    [psu(pb)], [('xT', b)])
            T.dma('sp', R.ap()[:, :, st:st + N], xT[b][:, :, :N], reads=[('xT', b)], writes=[('R', ci)])
            rmsnorm_mod(xT[b], ('xT', b), N, 0, 0, who, hT[b], ('hT', b), tmp, 'tmp', sq, 'sq', rstd, 'rstd', 2)

        def proj(ci):
            nonlocal nrope, nv
            st, N, isctx = CHUNKS[ci]
            b = ci % 2
            ntt = N // 128
            specs = []
            for h in range(4):
                specs.append((h, 4 + h, QAT.ap()[h]))
            for h in range(4):
                specs.append((8 + h, 12 + h, KAT.ap()[h]))
            for g in range(4):
                specs.append((16 + g, 20 + g, QBT.ap()[g]))
            specs.append((24, 25, KBT.ap()))
            for (bp, bq, dst) in specs:
                pp = 3 + 2 * (nrope % 2)
                k2 = nrope % 2
                k4 = nrope % 4
                nrope += 1
                for (blk, pbk) in ((bp, pp), (bq, pp + 1)):
                    for kc in range(8):
                        T.op('pe', lambda e, blk=blk, pbk=pbk, kc=kc, b=b, N=N: e.matmul(
                            PS[pbk][:, :N], lhsT=W[:, kc, blk * 128:(blk + 1) * 128], rhs=hT[b][:, kc, :N],
                            start=(kc == 0), stop=(kc == 7)), [('hT', b), Wu[kc]], [psu(pbk)])
                T.op('dve', lambda e, pp=pp, k2=k2, b=b, N=N: e.tensor_tensor(
                    out=t1[k2][:, :N], in0=PS[pp][:, :N], in1=rc[b][:, :N], op=ALU.mult),
                    [psu(pp), ('rc', b)], [('t1', k2)])
                T.op('dve', lambda e, pp=pp, k2=k2, b=b, N=N: e.tensor_tensor(
                    out=t2[k2][:, :N], in0=PS[pp + 1][:, :N], in1=rs[b][:, :N], op=ALU.mult),
                    [psu(pp + 1), ('rs', b)], [('t2', k2)])
                T.op('dve', lambda e, k2=k2, k4=k4, N=N: e.tensor_tensor(
                    out=ob[k4][:, :N], in0=t1[k2][:, :N], in1=t2[k2][:, :N], op=ALU.add),
                    [('t1', k2), ('t2', k2)], [('ob', k4)])
                T.dma('sp', dst[:, st:st + N], ob[k4][:, :N], reads=[('ob', k4)], writes=[('proj', ci)])
            for tt in range(ntt):
                k2 = nv % 2
                nv += 1
                for kc in range(8):
                    T.op('pe', lambda e, kc=kc, tt=tt, b=b: e.matmul(
                        PS[7][:, :], lhsT=hT[b][:, kc, tt * 128:(tt + 1) * 128], rhs=W[:, kc, 3328:3840],
                        start=(kc == 0), stop=(kc == 7)), [('hT', b), Wu[kc]], [psu(7)])
                T.op('act', lambda e, k2=k2: e.activation(out=vb[k2][:, :], in_=PS[7][:, :], func=AF.Copy),
                     [psu(7)], [('vb', k2)])
                T.dma('sp', VA.ap()[st + tt * 128: st + (tt + 1) * 128, :], vb[k2][:, :],
                      reads=[('vb', k2)], writes=[('proj', ci)])
                k2 = nv % 2
                nv += 1
                for kc in range(8):
                    T.op('pe', lambda e, kc=kc, tt=tt, b=b: e.matmul(
                        PS[7][:, 0:128], lhsT=hT[b][:, kc, tt * 128:(tt + 1) * 128], rhs=W[:, kc, 3840:3968],
                        start=(kc == 0), stop=(kc == 7)), [('hT', b), Wu[kc]], [psu(7)])
                T.op('act', lambda e, k2=k2: e.activation(out=vb[k2][:, 0:128], in_=PS[7][:, 0:128], func=AF.Copy),
                     [psu(7)], [('vb', k2)])
                T.dma('sp', VB.ap()[st + tt * 128: st + (tt + 1) * 128, :], vb[k2][:, 0:128],
                      reads=[('vb', k2)], writes=[('proj', ci)])

        prep(0)
        for ci in range(len(CHUNKS)):
            if ci + 1 < len(CHUNKS):
                prep(ci + 1)
            proj(ci)

    ALLPROJ = [('proj', ci) for ci in range(len(CHUNKS))]

    def phase2a():
        VAs = T.alloc([128, 66, 512], BF16, 'VAs')
        for t0 in range(0, 66, 11):
            T.dma('sp', VAs[:, t0:t0 + 11, :],
                  VA.ap()[t0 * 128:(t0 + 11) * 128, :].rearrange("(t p) c -> p t c", p=128),
                  reads=ALLPROJ, writes=[('VAs', t0)])
        VAu = [('VAs', t0) for t0 in range(0, 66, 11)]
        onesf = T.alloc([128, 128], F32)
        T.op('pool', lambda e: e.memset(onesf[:], 1.0), [], ['onesf'])
        Ks = [T.alloc([128, NT], BF16) for _ in range(2)]
        Qs = [T.alloc([128, 512], BF16) for _ in range(2)]
        NPB = 6
        Pb = [T.alloc([128, 2, 512], BF16) for _ in range(NPB)]
        acc = T.alloc([128, 2, 512], F32)
        rD = T.alloc([128, 2, 512], F32)
        rD2 = T.alloc([128, 2, 512], F32)
        u_ = T.alloc([128, 2, 512], F32)
        o_ = T.alloc([128, 512], F32)
        sqo = T.alloc([128, 512], BF16)
        rr = T.alloc([128, 512], F32)
        ao = [T.alloc([128, 512], BF16) for _ in range(2)]
        st8 = {'nq': 0, 'ns': 0}
        pending = []

        def tick():
            for p_ in pending:
                p_[0] -= 1
            while pending and pending[0][0] <= 0:
                pending.pop(0)[1]()

        def flush():
            while pending:
                pending.pop(0)[1]()

        for h in range(4):
            kb = h % 2
            T.dma('sp', Ks[kb][:, :], KAT.ap()[h], reads=ALLPROJ, writes=[('Ks', kb)])
            for ci, (st, N, isctx) in enumerate(CHUNKS):
                qb = st8['nq'] % 2
                st8['nq'] += 1
                T.dma('sp', Qs[qb][:, :N], QAT.ap()[h][:, st:st + N], reads=ALLPROJ, writes=[('Qs', qb)])
                kts = [0, 1] if isctx else list(range(66))
                dve_k = [kt for kt in kts if kt % 2 == 0]
                pe_k = [kt for kt in kts if kt % 2 == 1]

                def qk(kt, sb):
                    for m in range(2):
                        T.op('pe', lambda e, m=m: e.matmul(
                            P2[sb][:, m * 512:m * 512 + N], lhsT=Ks[kb][m * 64:(m + 1) * 64, kt * 128:(kt + 1) * 128],
                            rhs=Qs[qb][m * 64:(m + 1) * 64, :N], start=True, stop=True),
                            [('Ks', kb), ('Qs', qb)], [('p2', sb)])

                qk(kts[0], st8['ns'] % 2)
                for i, kt in enumerate(kts):
                    first, last = kt == kts[0], kt == kts[-1]
                    sb = st8['ns'] % 2
                    pbf = st8['ns'] % NPB
                    st8['ns'] += 1
                    tick()
                    if i + 1 < len(kts):
                        qk(kts[i + 1], st8['ns'] % 2)
                    T.op('act', lambda e: e.activation(
                        out=Pb[pbf][:, :, :N], in_=P2[sb][:, :].rearrange("p (m q) -> p m q", m=2)[:, :, :N],
                        func=AF.Exp, scale=0.125), [('p2', sb)], [('Pb', pbf)])
                    for m in range(2):
                        T.op('pe', lambda e, m=m: e.matmul(
                            PS[4 + m][:, :N], lhsT=VAs[:, kt, h * 128:(h + 1) * 128], rhs=Pb[pbf][:, m, :N],
                            start=first, stop=last), [('Pb', pbf)] + VAu, [psu(4 + m)])
                    if kt % 2 == 1:
                        for m in range(2):
                            T.op('pe', lambda e, m=m: e.matmul(
                                PS[6 + m][:, :N], lhsT=onesb[:], rhs=Pb[pbf][:, m, :N],
                                start=(kt == pe_k[0]), stop=(kt == pe_k[-1])), [('Pb', pbf), 'onesb'], [psu(6 + m)])
                    elif kt == dve_k[0]:
                        T.op('dve', lambda e: e.tensor_copy(out=acc[:, :, :N], in_=Pb[pbf][:, :, :N]),
                             [('Pb', pbf)], ['acc'])
                    else:
                        T.op('dve', lambda e: e.tensor_tensor(
                            out=acc[:, :, :N], in0=acc[:, :, :N], in1=Pb[pbf][:, :, :N], op=ALU.add),
                            [('Pb', pbf), 'acc'], ['acc'])
                flush()
                T.op('dve', lambda e: e.tensor_copy(out=u_[:, 0, :N], in_=PS[4][:, :N]), [psu(4)], [('u_', 0)])
                T.op('act', lambda e: e.activation(out=u_[:, 1, :N], in_=PS[5][:, :N], func=AF.Copy),
                     [psu(5)], [('u_', 1)])
                T.op('act', lambda e: e.activation(
                    out=rD[:, :, :N], in_=P2[3][:, :].rearrange("p (m q) -> p m q", m=2)[:, :, :N], func=AF.Copy),
                    [psu(6), psu(7)], ['rD'])
                fsb = st8['ns'] % 2
                st8['ns'] += 1
                for m in range(2):
                    T.op('pe', lambda e, m=m: e.matmul(P2[fsb][:, m * 512:m * 512 + N], lhsT=onesf[:],
                                                       rhs=acc[:, m, :N], start=True, stop=True),
                         ['acc', 'onesf'], [('p2', fsb)])
                T.op('act', lambda e: e.activation(
                    out=rD2[:, :, :N], in_=P2[fsb][:, :].rearrange("p (m q) -> p m q", m=2)[:, :, :N], func=AF.Copy),
                    [('p2', fsb)], ['rD2'])

                def stageB(N=N):
                    T.op('dve', lambda e: e.tensor_tensor(out=rD[:, :, :N], in0=rD[:, :, :N], in1=rD2[:, :, :N],
                                                          op=ALU.add), ['rD', 'rD2'], ['rD'])
                    T.op('dve', lambda e: e.reciprocal(out=rD[:, :, :N], in_=rD[:, :, :N]), ['rD'], ['rD'])
                    for m in range(2):
                        T.op('dve', lambda e, m=m: e.tensor_tensor(
                            out=u_[:, m, :N], in0=u_[:, m, :N], in1=rD[:, m, :N], op=ALU.mult),
                            [('u_', m), 'rD'], [('u_', m)])
                    T.op('dve', lambda e: e.scalar_tensor_tensor(
                        out=o_[:, :N], in0=u_[:, 1, :N], scalar=neglam[:, 0:1], in1=u_[:, 0, :N],
                        op0=ALU.mult, op1=ALU.add), [('u_', 0), ('u_', 1), 'neglam'], ['o_'])
                    T.op('dve', lambda e: e.tensor_tensor(out=sqo[:, :N], in0=o_[:, :N], in1=o_[:, :N], op=ALU.mult),
                         ['o_'], ['sqo'])

                def stageC(N=N):
                    f2 = st8['ns'] % 2
                    T.op('pe', lambda e: e.matmul(P2[f2][:, :N], lhsT=onesb[:], rhs=sqo[:, :N], start=True, stop=True),
                         ['sqo', 'onesb'], [('p2', f2)])
                    T.op('act', lambda e: e.activation(out=rr[:, :N], in_=P2[f2][:, :N], func=AF.Sqrt, bias=EPS,
                                                       scale=1.0 / 128), [('p2', f2)], ['rr'])

                def stageD(N=N, h=h, st=st, ci=ci):
                    T.op('dve', lambda e: e.reciprocal(out=rr[:, :N], in_=rr[:, :N]), ['rr'], ['rr'])
                    ab_ = st8['nq'] % 2
                    T.op('dve', lambda e: e.scalar_tensor_tensor(
                        out=ao[ab_][:, :N], in0=o_[:, :N], scalar=0.8, in1=rr[:, :N], op0=ALU.mult, op1=ALU.mult),
                        ['o_', 'rr'], [('ao', ab_)])
                    T.dma('sp', ATd.ap()[h][:, st:st + N], ao[ab_][:, :N], reads=[('ao', ab_)], writes=[('AT', ci)])

                pending.append([3, stageB])
                pending.append([10, stageC])
                pending.append([16, stageD])
        flush()

    def phase2b():
        Kb = T.alloc([128, NT], BF16, 'Kb')
        Vb = T.alloc([128, 66, 128], BF16, 'Vb')
        mk = T.alloc([128, 2, 512], BF16, 'mk')
        T.dma('sp', Kb[:, :], KBT.ap(), reads=ALLPROJ, writes=['Kb'])
        T.dma('sp', Vb[:, :, :], VB.ap().rearrange("(t p) c -> p t c", p=128), reads=ALLPROJ, writes=['Vb'])
        T.dma('sp', mk[:, 0, :], maskb.ap()[0], writes=['mk'])
        T.dma('sp', mk[:, 1, :], maskb.ap()[1], writes=['mk'])
        Qc = [T.alloc([128, 4, 512], BF16) for _ in range(2)]
        Pb = [T.alloc([128, 512], BF16) for _ in range(4)]
        dt_ = [T.alloc([64, 512], F32) for _ in range(2)]
        obw = [T.alloc([64, 4, 128], BF16) for _ in range(2)]
        nb = 0
        units = []
        def loadq(ci):
            st, N, isctx = CHUNKS[ci]
            T.dma('sp', Qc[ci % 2][:, :, :N], QBT.ap().rearrange("g p t -> p g t")[:, :, st:st + N],
                  reads=ALLPROJ, writes=[('Qc', ci % 2)])

        for ci, (st, N, isctx) in enumerate(CHUNKS):
            qb = ci % 2
            for bi in range(N // 128):
                if isctx:
                    tiles = [(0, None), (1, None)]
                else:
                    i = (st - 256) // 128 + bi
                    tiles = [(0, None), (1, None)]
                    if i - 1 >= 0:
                        tiles.append((2 + i - 1, 0))
                    tiles.append((2 + i, None))
                    if i + 1 < 64:
                        tiles.append((2 + i + 1, 1))
                for kvh in range(2):
                    for ti, (kt, mi) in enumerate(tiles):
                        units.append(dict(qb=qb, bi=bi, kvh=kvh, kt=kt, mi=mi, first=(ti == 0),
                                          last=(ti == len(tiles) - 1), st=st, ci=ci))

        def qk(u, sb):
            T.op('pe', lambda e: e.matmul(
                PS[sb][:, :].rearrange("p (g q) -> p g q", g=4),
                lhsT=Kb[u['kvh'] * 64:(u['kvh'] + 1) * 64, u['kt'] * 128:(u['kt'] + 1) * 128],
                rhs=Qc[u['qb']][u['kvh'] * 64:(u['kvh'] + 1) * 64, :, u['bi'] * 128:(u['bi'] + 1) * 128],
                start=True, stop=(u['mi'] is None)), ['Kb', ('Qc', u['qb'])], [psu(sb)])
            if u['mi'] is not None:
                T.op('pe', lambda e: e.matmul(
                    PS[sb][:, :], lhsT=identb[:], rhs=mk[:, u['mi'], :], start=False, stop=True),
                    ['identb', 'mk'], [psu(sb)])

        loadq(0)
        qk(units[0], 0)
        seen = set()
        for ui, u in enumerate(units):
            if u['ci'] not in seen:
                seen.add(u['ci'])
                if u['ci'] + 1 < len(CHUNKS):
                    loadq(u['ci'] + 1)
            sb = ui % 3
            pbf = ui % 4
            kvh = u['kvh']
            if ui + 1 < len(units):
                qk(units[ui + 1], (ui + 1) % 3)
            T.op('act', lambda e: e.activation(
                out=Pb[pbf][:, :], in_=PS[sb][:, :], func=AF.Exp, scale=0.125),
                [psu(sb)], [('Pb', pbf)])
            T.op('pe', lambda e: e.matmul(
                PS[3 + kvh][0:64, :], lhsT=Vb[:, u['kt'], kvh * 64:(kvh + 1) * 64], rhs=Pb[pbf][:, :],
                start=u['first'], stop=u['last']), ['Vb', ('Pb', pbf)], [psu(3 + kvh)])
            T.op('pe', lambda e: e.matmul(
                PS[5 + kvh][0:64, :], lhsT=onesb[:, 0:64], rhs=Pb[pbf][:, :],
                start=u['first'], stop=u['last']), ['onesb', ('Pb', pbf)], [psu(5 + kvh)])
            if u['last']:
                for g in range(4):
                    T.op('dve', lambda e, g=g: e.tensor_scalar(
                        out=dt_[kvh][:, g * 128:(g + 1) * 128], in0=PS[5 + kvh][0:64, g * 128:(g + 1) * 128],
                        scalar1=esink[0:64, kvh * 4 + g:kvh * 4 + g + 1], scalar2=None, op0=ALU.add),
                        [psu(5 + kvh), 'esink'], [('dt_', kvh)])
                T.op('dve', lambda e: e.reciprocal(out=dt_[kvh][:, :], in_=dt_[kvh][:, :]), [('dt_', kvh)],
                     [('dt_', kvh)])
                ob_ = nb % 2
                nb += 1
                T.op('dve', lambda e: e.tensor_tensor(
                    out=obw[ob_][:, :, :].rearrange("p g q -> p (g q)"), in0=PS[3 + kvh][0:64, :],
                    in1=dt_[kvh][:, :], op=ALU.mult), [psu(3 + kvh), ('dt_', kvh)], [('obw', ob_)])
                q0 = u['st'] + u['bi'] * 128
                T.dma('sp', BTd.ap()[kvh * 4:(kvh + 1) * 4].rearrange("g p t -> p g t")[:, :, q0:q0 + 128],
                      obw[ob_][:, :, :], reads=[('obw', ob_)], writes=[('BT', u['ci'])])

    def phase3a(li, srcs, wloads, chunk_ids):
        Wt = []
        for wi, (wap, nb_, P_) in enumerate(wloads):
            wt = T.alloc([P_, nb_, D], BF16)
            T.dma('pool', wt[:, :, :], wap, writes=[('Wo', wi)])
            Wt.append(wt)
        xT = [T.alloc([128, 8, 512], F32) for _ in range(2)]
        hT = [T.alloc([128, 8, 512], BF16) for _ in range(2)]
        mx = [[T.alloc([P_, nb_, 512], BF16) for (_, nb_, P_) in srcs] for _ in range(2)]
        tmp = T.alloc([128, 8, 512], F32)
        sq = T.alloc([128, 8, 512], BF16)
        rstd = T.alloc([128, 512], F32)
        npb = 0

        def partA(ci):
            nonlocal npb
            st, N, isctx = CHUNKS[ci]
            b = ci % 2
            who = 1 if isctx else 0
            T.dma('sp', xT[b][:, :, :N], R.ap()[:, :, st:st + N], reads=[('R', ci)], writes=[('xT', b)])
            for si, (sap, nb_, P_) in enumerate(srcs):
                T.dma('sp', mx[b][si][:, :, :N], sap.rearrange("g p t -> p g t")[:, :, st:st + N],
                      reads=[('AT', ci), ('BT', ci), ('YT', ci)], writes=[('mx', b, si)])
            nmm = sum(nb_ for (_, nb_, _) in srcs)
            for j in range(8):
                pb = npb % 2
                npb += 1
                k = 0
                for si, (sap, nb_, P_) in enumerate(srcs):
                    for q in range(nb_):
                        T.op('pe', lambda e, si=si, q=q, j=j, pb=pb, b=b, N=N, k=k, P_=P_: e.matmul(
                            PS[pb][:, :N], lhsT=Wt[si][0:P_, q, j * 128:(j + 1) * 128], rhs=mx[b][si][0:P_, q, :N],
                            start=(k == 0), stop=(k == nmm - 1)), [('Wo', si), ('mx', b, si)], [psu(pb)])
                        k += 1
                g_ap = modT[:, li, 2 * 8 + j, who:who + 1]
                T.op('dve', lambda e, j=j, pb=pb, b=b, N=N, g_ap=g_ap: e.scalar_tensor_tensor(
                    out=xT[b][:, j, :N], in0=PS[pb][:, :N], scalar=g_ap, in1=xT[b][:, j, :N],
                    op0=ALU.mult, op1=ALU.add), [psu(pb), ('xT', b), 'modT'], [('xT', b)])
            T.dma('sp', R.ap()[:, :, st:st + N], xT[b][:, :, :N], reads=[('xT', b)], writes=[('R', ci)])

        def partB(ci):
            st, N, isctx = CHUNKS[ci]
            b = ci % 2
            who = 1 if isctx else 0
            rmsnorm_mod(xT[b], ('xT', b), N, li, 1, who, hT[b], ('hT', b), tmp, 'tmp', sq, 'sq', rstd, 'rstd', 2)
            T.dma('sp', H2.ap()[:, :, st:st + N], hT[b][:, :, :N], reads=[('hT', b)], writes=[('H2', ci)])

        partA(chunk_ids[0])
        for i_, ci in enumerate(chunk_ids):
            if i_ + 1 < len(chunk_ids):
                partA(chunk_ids[i_ + 1])
            partB(ci)

    def phase3b(li, chunk_ids, after_loads=None):
        Wi = T.alloc([128, 8, 2 * DFF], BF16, 'Wi')
        for kc in range(8):
            T.dma('pool', Wi[:, kc, :], ffn_in.ap()[li, kc * 128:(kc + 1) * 128, :], writes=[('Wi', kc)])
        if after_loads is not None:
            after_loads()
        hT = [T.alloc([128, 8, 512], BF16) for _ in range(2)]
        sg = [T.alloc([128, 512], F32) for _ in range(2)]
        ac = [T.alloc([128, 512], BF16) for _ in range(3)]
        n = 0
        for ci in chunk_ids:
            st, N, isctx = CHUNKS[ci]
            b = ci % 2
            T.dma('sp', hT[b][:, :, :N], H2.ap()[:, :, st:st + N], reads=[('H2', ci)], writes=[('hT', b)])
            for fb in range(22):
                pg = (n % 4) * 2
                s2 = n % 2
                a3 = n % 3
                n += 1
                for (col0, pbk) in ((fb * 128, pg), (DFF + fb * 128, pg + 1)):
                    for kc in range(8):
                        T.op('pe', lambda e, col0=col0, pbk=pbk, kc=kc, b=b, N=N: e.matmul(
                            PS[pbk][:, :N], lhsT=Wi[:, kc, col0:col0 + 128], rhs=hT[b][:, kc, :N],
                            start=(kc == 0), stop=(kc == 7)), [('hT', b), ('Wi', kc)], [psu(pbk)])
                T.op('act', lambda e, pg=pg, s2=s2, N=N: e.activation(out=sg[s2][:, :N], in_=PS[pg][:, :N],
                                                                     func=AF.Silu), [psu(pg)], [('sg', s2)])
                T.op('dve', lambda e, pg=pg, s2=s2, a3=a3, N=N: e.tensor_tensor(
                    out=ac[a3][:, :N], in0=PS[pg + 1][:, :N], in1=sg[s2][:, :N], op=ALU.mult),
                    [psu(pg + 1), ('sg', s2)], [('ac', a3)])
                T.dma('sp', ACTd.ap()[fb][:, st:st + N], ac[a3][:, :N], reads=[('ac', a3)], writes=[('ACT', ci)])

    def load_Wo(li):
        Wo = T.alloc([128, 22, D], BF16, 'Wo2')
        for f0 in range(0, 22, 2):
            T.dma('pool', Wo[:, f0:f0 + 2, :],
                  ffn_out.ap()[li, f0 * 128:(f0 + 2) * 128, :].rearrange("(f p) d -> p f d", p=128),
                  writes=[('Wo2', f0)])
        return Wo

    def phase3c(li, chunk_ids, final, Wo=None):
        if Wo is None:
            Wo = load_Wo(li)
        Wou = [('Wo2', f0) for f0 in range(0, 22, 2)]
        xT = [T.alloc([128, 8, 512], F32) for _ in range(2)]
        ac = [T.alloc([128, 22, 512], BF16) for _ in range(2)]
        if final:
            tmp = T.alloc([128, 8, 512], F32)
            sq = T.alloc([128, 8, 512], BF16)
            rstd = T.alloc([128, 512], F32)
            otok = [T.alloc([128, D], F32) for _ in range(2)]
        npb = 0
        no = 0
        for ci in chunk_ids:
            st, N, isctx = CHUNKS[ci]
            b = ci % 2
            who = 1 if isctx else 0
            T.dma('sp', xT[b][:, :, :N], R.ap()[:, :, st:st + N], reads=[('R', ci)], writes=[('xT', b)])
            T.dma('sp', ac[b][:, :, :N], ACTd.ap().rearrange("f p t -> p f t")[:, :, st:st + N],
                  reads=[('ACT', ci)], writes=[('acl', b)])
            for j in range(8):
                pb = npb % 2
                npb += 1
                for fb in range(22):
                    T.op('pe', lambda e, fb=fb, j=j, pb=pb, b=b, N=N: e.matmul(
                        PS[pb][:, :N], lhsT=Wo[:, fb, j * 128:(j + 1) * 128], rhs=ac[b][:, fb, :N],
                        start=(fb == 0), stop=(fb == 21)), [('acl', b)] + Wou, [psu(pb)])
                g_ap = modT[:, li, 5 * 8 + j, who:who + 1]
                T.op('dve', lambda e, j=j, pb=pb, b=b, N=N, g_ap=g_ap: e.scalar_tensor_tensor(
                    out=xT[b][:, j, :N], in0=PS[pb][:, :N], scalar=g_ap, in1=xT[b][:, j, :N],
                    op0=ALU.mult, op1=ALU.add), [psu(pb), ('xT', b), 'modT'], [('xT', b)])
            if not final:
                T.dma('sp', R.ap()[:, :, st:st + N], xT[b][:, :, :N], reads=[('xT', b)], writes=[('R', ci)])
            else:
                rmsnorm_mod(xT[b], ('xT', b), N, li, 0, None, None, None, tmp, 'tmp', sq, 'sq', rstd, 'rstd', 2)
                for tt in range(N // 128):
                    o2 = no % 2
                    no += 1
                    for j in range(8):
                        pbk = 3 + (j // 4) + 2 * o2
                        T.op('pe', lambda e, j=j, tt=tt, pbk=pbk: e.transpose(
                            PS[pbk][:, (j % 4) * 128:(j % 4 + 1) * 128], tmp[:, j, tt * 128:(tt + 1) * 128],
                            ident[:]), ['tmp', 'ident'], [psu(pbk)])
                    for hf in range(2):
                        pbk = 3 + hf + 2 * o2
                        T.op('act' if hf else 'dve', (lambda e, pbk=pbk, o2=o2, hf=hf: e.activation(
                            out=otok[o2][:, hf * 512:(hf + 1) * 512], in_=PS[pbk][:, :], func=AF.Copy)) if hf else
                            (lambda e, pbk=pbk, o2=o2, hf=hf: e.tensor_copy(
                                out=otok[o2][:, hf * 512:(hf + 1) * 512], in_=PS[pbk][:, :])),
                            [psu(pbk)], [('otok', o2, hf)])
                    r0_ = st - 256 + tt * 128
                    T.dma('sp', out.ap()[r0_:r0_ + 128, :], otok[o2][:, :],
                          reads=[('otok', o2, 0), ('otok', o2, 1)], writes=[('out', ci, tt)])

    def load_W1():
        W = T.alloc([128, 8, 2592], BF16, 'w1')
        for kc in range(8):
            T.dma('pool', W[:, kc, :], w1.ap()[kc * 128:(kc + 1) * 128, :], writes=[('W', kc)])
        return W

    def phaseP1(W=None):
        if W is None:
            W = load_W1()
        Wu = [('W', kc) for kc in range(8)]
        gw = T.alloc([16, 2, 256], BF16)
        T.dma('pool', gw[:, :, :], gw_in.ap().rearrange("n r k -> r n k"), writes=['gw'])
        gb2 = T.alloc([64, 2, 512], F32)
        T.dma('sp', gb2[:, :, :], gb2_in.ap().rearrange("n p k -> p n k"), writes=['gb2'])
        tri = T.alloc([64, 4, 64], F32)
        T.dma('sp', tri[:, :, :], tri_in.ap().rearrange("n p k -> p n k"), writes=['tri'])
        xT2 = [T.alloc([128, 8, 512], F32) for _ in range(2)]
        hT2 = [T.alloc([128, 8, 512], BF16) for _ in range(2)]
        tmp = T.alloc([128, 8, 512], F32)
        sq = T.alloc([128, 8, 512], BF16)
        rstd = T.alloc([128, 512], F32)
        qf = T.alloc([64, 4, 512], F32)
        kf = T.alloc([64, 4, 512], F32)
        lrs = T.alloc([16, 2, 512], BF16)
        vt = T.alloc([64, 8, 512], BF16)
        ktf = T.alloc([64, 8, 256], F32)
        zz = T.alloc([64, 8, 256], F32)
        eb = [T.alloc([64, 512], F32) for _ in range(2)]
        enb = [T.alloc([64, 512], F32) for _ in range(2)]
        so = [T.alloc([128, 512], BF16) for _ in range(4)]
        sf = [T.alloc([128, 512], F32) for _ in range(2)]
        g1 = T.alloc([128, 512], F32)
        g2 = T.alloc([128, 512], F32)
        kdt = T.alloc([64, 8, 256], BF16)
        cnt = {'so': 0, 'sf': 0, 'pb': 0, 'e': 0}

        def nxt(k, n):
            v = cnt[k] % n
            cnt[k] += 1
            return v

        cur = {}

        def fm_block(col0, M, N):
            pb = nxt('pb', 4)
            hT, hTu = cur['hT'], cur['hTu']
            for kc in range(8):
                T.op('pe', lambda e, kc=kc, pb=pb: e.matmul(
                    PS[pb][0:M, :N], lhsT=W[:, kc, col0:col0 + M], rhs=hT[:, kc, :N],
                    start=(kc == 0), stop=(kc == 7)), [hTu, Wu[kc]], [psu(pb)])
            return pb

        def prep(ci):
            st, N, isctx = CHUNKS[ci]
            who = 1 if isctx else 0
            b = ci % 2
            T.dma('sp', xT2[b][:, :, :N], R.ap()[:, :, st:st + N], reads=[('R', ci)], writes=[('xT', b)])
            rmsnorm_mod(xT2[b], ('xT', b), N, 1, 0, who, hT2[b], ('hT', b), tmp, 'tmp', sq, 'sq', rstd, 'rstd', 7)

        prep(0)
        for ci, (st, N, isctx) in enumerate(CHUNKS):
            who = 1 if isctx else 0
            nsub = N // 64
            sub0 = st // 64
            if ci + 1 < len(CHUNKS):
                prep(ci + 1)
            hT = hT2[ci % 2]
            hTu = ('hT', ci % 2)
            cur['hT'], cur['hTu'] = hT, hTu
            for blk in range(4):
                pb = fm_block(2080 + blk * 128, 128, N)
                k = nxt('sf', 2)
                T.op('act', lambda e, pb=pb, k=k: e.activation(out=sf[k][:, :N], in_=PS[pb][:, :N], func=AF.Copy),
                     [psu(pb)], [('sf', k)])
                T.dma('sp', ZX.ap()[blk][:, st:st + N], sf[k][:, :N], reads=[('sf', k)], writes=[('ZX', ci)])
            if not isctx:
                for blk in range(4):
                    pb = fm_block(1568 + blk * 128, 128, N)
                    k = nxt('sf', 2)
                    T.op('act', lambda e, pb=pb, k=k: e.activation(out=sf[k][:, :N], in_=PS[pb][:, :N], func=AF.Copy),
                         [psu(pb)], [('sf', k)])
                    T.op('dve', lambda e, k=k: e.tensor_tensor(out=g1[:, :N], in0=sf[k][:, :N], in1=sf[k][:, :N],
                                                               op=ALU.mult), [('sf', k)], ['g1'])
                    T.op('dve', lambda e: e.tensor_scalar(out=g1[:, :N], in0=g1[:, :N], scalar1=0.044715,
                                                           scalar2=1.0, op0=ALU.mult, op1=ALU.add), ['g1'], ['g1'])
                    T.op('dve', lambda e, k=k: e.tensor_tensor(out=g2[:, :N], in0=g1[:, :N], in1=sf[k][:, :N],
                                                               op=ALU.mult), ['g1', ('sf', k)], ['g2'])
                    T.op('act', lambda e: e.activation(out=g2[:, :N], in_=g2[:, :N], func=AF.Sigmoid,
                                                       scale=1.5957691216), ['g2'], ['g2'])
                    o4 = nxt('so', 4)
                    T.op('dve', lambda e, k=k, o4=o4: e.tensor_tensor(out=so[o4][:, :N], in0=sf[k][:, :N],
                                                                     in1=g2[:, :N], op=ALU.mult),
                         ['g2', ('sf', k)], [('so', o4)])
                    T.dma('sp', GZ.ap()[blk][:, st:st + N], so[o4][:, :N], reads=[('so', o4)], writes=[('GZ', ci)])
                for blk in range(4):
                    pb = fm_block(1024 + blk * 128, 128, N)
                    o4 = nxt('so', 4)
                    T.op('act', lambda e, pb=pb, o4=o4: e.activation(out=so[o4][:, :N], in_=PS[pb][:, :N],
                                                                    func=AF.Silu), [psu(pb)], [('so', o4)])
                    T.dma('sp', SG.ap()[blk][:, st:st + N], so[o4][:, :N], reads=[('so', o4)], writes=[('SG', ci)])
            for h in range(4):
                if not isctx:
                    pb = fm_block(h * 64, 64, N)
                    T.op('act', lambda e, pb=pb, h=h: e.activation(out=qf[:, h, :N], in_=PS[pb][0:64, :N],
                                                                  func=AF.Identity, scale=0.125),
                         [psu(pb)], [('qf', h)])
                    pb = fm_block(256 + h * 64, 64, N)
                    T.op('dve', lambda e, pb=pb, h=h: e.tensor_copy(out=kf[:, h, :N], in_=PS[pb][0:64, :N]),
                         [psu(pb)], [('kf', h)])
            for d in range(2):
                pb = fm_block(1536 + d * 16, 16, N)
                T.op('act', lambda e, pb=pb, d=d: e.activation(out=lrs[:, d, :N], in_=PS[pb][0:16, :N], func=AF.Copy),
                     [psu(pb)], [('lrs', d)])
            for s_ in range(nsub):
                pb = nxt('pb', 4)
                for kc in range(8):
                    T.op('pe', lambda e, kc=kc, pb=pb, s_=s_: e.matmul(
                        PS[pb][0:64, :], lhsT=hT[:, kc, s_ * 64:(s_ + 1) * 64], rhs=W[:, kc, 512:1024],
                        start=(kc == 0), stop=(kc == 7)), [hTu, Wu[kc]], [psu(pb)])
                T.op('act', lambda e, pb=pb, s_=s_: e.activation(out=vt[:, s_, :], in_=PS[pb][0:64, :], func=AF.Copy),
                     [psu(pb)], ['vt'])
            T.dma('sp', V1.ap()[st:st + N, :].rearrange("(s p) c -> p s c", p=64), vt[:, :nsub, :],
                  reads=['vt'], writes=[('V1', ci)])
            for s2 in range(0, nsub, 2):
                pb = nxt('pb', 4)
                for s_ in (s2, s2 + 1):
                    for kc in range(8):
                        T.op('pe', lambda e, kc=kc, pb=pb, s_=s_, s2=s2: e.matmul(
                            PS[pb][0:64, (s_ - s2) * 256:(s_ - s2 + 1) * 256],
                            lhsT=hT[:, kc, s_ * 64:(s_ + 1) * 64], rhs=W[:, kc, 256:512],
                            start=(kc == 0), stop=(kc == 7)), [hTu, Wu[kc]], [psu(pb)])
                T.op('dve', lambda e, pb=pb, s2=s2: e.tensor_copy(
                    out=ktf[:, s2:s2 + 2, :].rearrange("p s c -> p (s c)"), in_=PS[pb][0:64, :]), [psu(pb)], ['ktf'])
            for d in range(2):
                for s2 in range(0, nsub, 2):
                    pb = nxt('pb', 4)
                    for s_ in (s2, s2 + 1):
                        T.op('pe', lambda e, pb=pb, s_=s_, s2=s2, d=d: e.matmul(
                            PS[pb][0:64, (s_ - s2) * 256:(s_ - s2 + 1) * 256],
                            lhsT=lrs[:, d, s_ * 64:(s_ + 1) * 64], rhs=gw[:, d, :], start=True, stop=True),
                            [('lrs', d), 'gw'], [psu(pb)])
                    T.op('dve', lambda e, pb=pb, s2=s2, d=d: e.tensor_tensor(
                        out=zz[:, s2:s2 + 2, :].rearrange("p s c -> p (s c)"), in0=PS[pb][0:64, :], in1=gb2[:, d, :],
                        op=ALU.add), [psu(pb), 'gb2'], ['zz'])
                T.op('act', lambda e: e.activation(out=zz[:, :nsub, :], in_=zz[:, :nsub, :], func=AF.Exp, scale=-1.0),
                     ['zz'], ['zz'])
                T.op('act', lambda e: e.activation(out=zz[:, :nsub, :], in_=zz[:, :nsub, :], func=AF.Ln, bias=1.0),
                     ['zz'], ['zz'])
                for s2 in range(0, nsub, 2):
                    pb = nxt('pb', 4)
                    for s_ in (s2, s2 + 1):
                        T.op('pe', lambda e, pb=pb, s_=s_, s2=s2, d=d: e.matmul(
                            PS[pb][0:64, (s_ - s2) * 256:(s_ - s2 + 1) * 256],
                            lhsT=tri[:, 2 + d, :], rhs=zz[:, s_, :], start=True, stop=True),
                            ['zz', 'tri'], [psu(pb)])
                    k = nxt('sf', 2)
                    T.op('act', lambda e, pb=pb, k=k: e.activation(out=sf[k][0:64, :], in_=PS[pb][0:64, :],
                                                                  func=AF.Exp, scale=-1.0 / 16), [psu(pb)], [('sf', k)])
                    T.op('dve', lambda e, k=k, s2=s2: e.tensor_tensor(
                        out=kdt[:, s2:s2 + 2, :].rearrange("p s c -> p (s c)"),
                        in0=ktf[:, s2:s2 + 2, :].rearrange("p s c -> p (s c)"), in1=sf[k][0:64, :], op=ALU.mult),
                        [('sf', k), 'ktf'], ['kdt'])
                T.dma('sp', KD.ap()[d, st:st + N, :].rearrange("(s p) c -> p s c", p=64), kdt[:, :nsub, :],
                      reads=['kdt'], writes=[('KD', d, ci)])
                for h in range(4):
                    pb = nxt('pb', 4)
                    for s_ in range(nsub):
                        T.op('pe', lambda e, pb=pb, s_=s_, h=h, d=d: e.matmul(
                            PS[pb][0:64, s_ * 64:(s_ + 1) * 64], lhsT=zz[:, s_, h * 64:(h + 1) * 64],
                            rhs=tri[:, d, :], start=True, stop=True), ['zz', 'tri'], [psu(pb)])
                    k = nxt('e', 2)
                    T.op('act', lambda e, pb=pb, k=k: e.activation(out=eb[k][:, :N], in_=PS[pb][0:64, :N],
                                                                  func=AF.Exp, scale=-1.0 / 16), [psu(pb)], [('eb', k)])
                    col = 63 if d == 0 else 0
                    T.op('pool', lambda e, k=k, d=d, h=h, col=col, sub0=sub0, nsub=nsub: e.tensor_copy(
                        out=decs[:, d, h, sub0:sub0 + nsub],
                        in_=eb[k][:, :N].rearrange("p (s c) -> p s c", c=64)[:, :, col]),
                        [('eb', k)], ['decs'])
                    if not isctx:
                        T.op('act', lambda e, pb=pb, k=k: e.activation(out=enb[k][:, :N], in_=PS[pb][0:64, :N],
                                                                      func=AF.Exp, scale=1.0 / 16),
                             [psu(pb)], [('enb', k)])
                        o4 = nxt('so', 4)
                        T.op('dve', lambda e, k=k, h=h, o4=o4: e.tensor_tensor(
                            out=so[o4][0:64, :N], in0=qf[:, h, :N], in1=eb[k][:, :N], op=ALU.mult),
                            [('qf', h), ('eb', k)], [('so', o4)])
                        T.dma('sp', QS.ap()[d, h][:, st:st + N], so[o4][0:64, :N], reads=[('so', o4)],
                              writes=[('QS', d, ci)])
                        o4 = nxt('so', 4)
                        T.op('dve', lambda e, k=k, h=h, o4=o4: e.tensor_tensor(
                            out=so[o4][0:64, :N], in0=kf[:, h, :N], in1=enb[k][:, :N], op=ALU.mult),
                            [('kf', h), ('enb', k)], [('so', o4)])
                        T.dma('sp', KS.ap()[d, h][:, st:st + N], so[o4][0:64, :N], reads=[('so', o4)],
                              writes=[('KS', d, ci)])

    def phaseG():
        glag = T.alloc([128, 1], F32)
        T.dma('sp', glag[:, :], glag_in.ap(), writes=['glag'])
        mask4 = T.alloc([64, 2, 256], F32)
        T.dma('sp', mask4[:, :, :], mask4_in.ap().rearrange("n p k -> p n k"), writes=['mask4'])
        S32 = T.alloc([64, 4, 128], F32)
        Sbf = [T.alloc([64, 4, 128], BF16) for _ in range(2)]
        QSs = [T.alloc([64, 4, 512], BF16) for _ in range(2)]
        KSs = [T.alloc([64, 4, 512], BF16) for _ in range(2)]
        KDs = [T.alloc([64, 8, 256], BF16) for _ in range(2)]
        V1s = [T.alloc([64, 8, 512], BF16) for _ in range(2)]
        ATm = [T.alloc([64, 256], BF16) for _ in range(2)]
        of_ = [T.alloc([128, 512], F32) for _ in range(2)]
        of4 = [T.alloc([128, 512], F32) for _ in range(4)]
        ogl4 = [T.alloc([128, 512], F32) for _ in range(4)]
        sgl4 = [[T.alloc([128, 512], BF16) for _ in range(4)] for _ in range(2)]
        sqo2 = [T.alloc([128, 512], BF16) for _ in range(2)]
        rr4 = [T.alloc([128, 512], F32) for _ in range(4)]
        yo4 = [T.alloc([128, 512], BF16) for _ in range(4)]
        pending = []

        def tick():
            for p_ in pending:
                p_[0] -= 1
            while pending and pending[0][0] <= 0:
                pending.pop(0)[1]()

        def flush():
            while pending:
                pending.pop(0)[1]()
        nchk = 0
        nb = 0
        nst = 0
        nat = 0
        nev = 0
        for d in range(2):
            T.op('dve', lambda e: e.memset(S32[:, :, :], 0.0), [], ['S32'])
            T.op('dve', lambda e, k=nst % 2: e.memset(Sbf[k][:, :, :], 0.0), [], [('Sbf', nst % 2)])
            order = [0] + (list(range(1, 17)) if d == 0 else list(range(16, 0, -1)))
            for ci in order:
                st, N, isctx = CHUNKS[ci]
                nsub = N // 64
                b = nb % 2
                nb += 1
                if not isctx:
                    T.dma('sp', QSs[b][:, :, :N], QS.ap()[d].rearrange("h p t -> p h t")[:, :, st:st + N],
                          reads=[('QS', d, ci)], writes=[('QSs', b)])
                    T.dma('sp', KSs[b][:, :, :N], KS.ap()[d].rearrange("h p t -> p h t")[:, :, st:st + N],
                          reads=[('KS', d, ci)], writes=[('KSs', b)])
                T.dma('sp', KDs[b][:, :nsub, :], KD.ap()[d, st:st + N, :].rearrange("(s p) c -> p s c", p=64),
                      reads=[('KD', d, ci)], writes=[('KDs', b)])
                T.dma('sp', V1s[b][:, :nsub, :], V1.ap()[st:st + N, :].rearrange("(s p) c -> p s c", p=64),
                      reads=[('V1', ci)], writes=[('V1s', b)])
                if d == 1 and not isctx:
                    cp_ = nchk % 2
                    nchk += 1
                    for h in range(4):
                        T.dma('sp', ogl4[h][:, :N], OG.ap()[h][:, st:st + N], reads=[('OG', ci)],
                              writes=[('ogl4', h)])
                        T.dma('sp', sgl4[cp_][h][:, :N], SG.ap()[h][:, st:st + N], reads=[('SG', ci)],
                              writes=[('sgl4', cp_, h)])
                subs = list(range(nsub)) if d == 0 else list(range(nsub - 1, -1, -1))
                for s_ in subs:
                    gs = st // 64 + s_
                    cur = nst % 2
                    nx = (nst + 1) % 2
                    nst += 1
                    c0 = s_ * 64
                    if not isctx:
                        tick()
                    for h in range(4):
                        T.op('pe', lambda e, h=h, b=b, s_=s_: e.matmul(
                            PS[1][0:64, h * 128:(h + 1) * 128], lhsT=KDs[b][:, s_, h * 64:(h + 1) * 64],
                            rhs=V1s[b][:, s_, h * 128:(h + 1) * 128], start=True, stop=True),
                            [('KDs', b), ('V1s', b)], [psu(1)])
                    if not isctx:
                        am = nat % 2
                        nat += 1
                        for h in range(4):
                            T.op('pe', lambda e, h=h, b=b, c0=c0: e.matmul(
                                PS[0][0:64, h * 64:(h + 1) * 64], lhsT=KSs[b][:, h, c0:c0 + 64],
                                rhs=QSs[b][:, h, c0:c0 + 64], start=True, stop=True),
                                [('KSs', b), ('QSs', b)], [psu(0)])
                        T.op('dve', lambda e, am=am, d=d: e.tensor_tensor(
                            out=ATm[am][:, :], in0=PS[0][0:64, 0:256], in1=mask4[:, d, :], op=ALU.mult),
                            [psu(0), 'mask4'], [('ATm', am)])
                        for h in range(4):
                            T.op('pe', lambda e, h=h, b=b, c0=c0, am=am, s_=s_: e.matmul(
                                PS[4 + h][:, c0:c0 + 64], lhsT=V1s[b][:, s_, h * 128:(h + 1) * 128],
                                rhs=ATm[am][:, h * 64:(h + 1) * 64], start=True, stop=False),
                                [('V1s', b), ('ATm', am)], [psu(4 + h)])
                            T.op('pe', lambda e, h=h, b=b, c0=c0, cur=cur: e.matmul(
                                PS[4 + h][:, c0:c0 + 64], lhsT=Sbf[cur][:, h, :], rhs=QSs[b][:, h, c0:c0 + 64],
                                start=False, stop=True), [('Sbf', cur), ('QSs', b)], [psu(4 + h)])
                    for h in range(4):
                        T.op('dve', lambda e, h=h, d=d, gs=gs: e.scalar_tensor_tensor(
                            out=S32[:, h, :], in0=S32[:, h, :], scalar=decs[:, d, h, gs:gs + 1],
                            in1=PS[1][0:64, h * 128:(h + 1) * 128], op0=ALU.mult, op1=ALU.add),
                            ['S32', psu(1), 'decs'], ['S32'])
                    T.op('act', lambda e, nx=nx: e.activation(out=Sbf[nx][:, :, :], in_=S32[:, :, :], func=AF.Copy),
                         ['S32'], [('Sbf', nx)])
                if isctx:
                    continue
                flush()
                for h in range(4):
                    k = nev % 2
                    nev += 1
                    if d == 0:
                        T.op('act', lambda e, h=h, k=k: e.activation(out=of_[k][:, :N], in_=PS[4 + h][:, :N],
                                                                    func=AF.Copy), [psu(4 + h)], [('of', k)])
                        T.dma('sp', OG.ap()[h][:, st:st + N], of_[k][:, :N], reads=[('of', k)], writes=[('OG', ci)])
                    else:
                        T.op('dve', lambda e, h=h: e.tensor_tensor(
                            out=of4[h][:, :N], in0=PS[4 + h][:, :N], in1=ogl4[h][:, :N], op=ALU.add),
                            [psu(4 + h), ('ogl4', h)], [('of4', h)])

                        def st2(h=h, N=N):
                            q2 = h % 2
                            T.op('act', lambda e: e.activation(out=sqo2[q2][:, :N], in_=of4[h][:, :N], func=AF.Square),
                                 [('of4', h)], [('sqo2', q2)])
                            T.op('pe', lambda e: e.matmul(PS[2 + q2][:, :N], lhsT=onesb[:], rhs=sqo2[q2][:, :N],
                                                          start=True, stop=True), [('sqo2', q2), 'onesb'], [psu(2 + q2)])
                            T.op('act', lambda e: e.activation(out=rr4[h][:, :N], in_=PS[2 + q2][:, :N], func=AF.Sqrt,
                                                               bias=EPS, scale=1.0 / 128), [psu(2 + q2)], [('rr4', h)])

                        def st3(h=h, N=N, st=st, ci=ci, cp_=cp_):
                            T.op('dve', lambda e: e.reciprocal(out=rr4[h][:, :N], in_=rr4[h][:, :N]), [('rr4', h)],
                                 [('rr4', h)])
                            T.op('dve', lambda e: e.scalar_tensor_tensor(
                                out=of4[h][:, :N], in0=of4[h][:, :N], scalar=glag[:, 0:1], in1=rr4[h][:, :N],
                                op0=ALU.mult, op1=ALU.mult), [('of4', h), ('rr4', h), 'glag'], [('of4', h)])
                            T.op('dve', lambda e: e.tensor_tensor(
                                out=yo4[h][:, :N], in0=of4[h][:, :N], in1=sgl4[cp_][h][:, :N], op=ALU.mult),
                                [('of4', h), ('sgl4', cp_, h)], [('yo4', h)])
                            T.dma('sp', YT.ap()[h][:, st:st + N], yo4[h][:, :N], reads=[('yo4', h)],
                                  writes=[('YT', ci)])

                        pending.append([1 + h, st2])
                        pending.append([3 + h, st3])
                        pending.sort(key=lambda p_: p_[0])
        flush()

    def phaseLr():
        cw = T.alloc([128, 4, 4], F32)
        cb = T.alloc([128, 4], F32)
        ba = T.alloc([128, 2, 4], F32)
        bx = T.alloc([128, 2, 4], F32)
        lam = T.alloc([128, 2, 4], F32)
        cl = T.alloc([128, 2, 4], F32)
        cl2 = T.alloc([128, 2, 4], F32)
        T.dma('sp', cw[:, :, :], cw_in.ap(), writes=['cw'])
        T.dma('sp', cb[:, :], cb_in.ap(), writes=['cb'])
        T.dma('sp', ba[:, :, :], lba.ap(), writes=['ba'])
        T.dma('sp', bx[:, :, :], lbx.ap(), writes=['bx'])
        T.dma('sp', lam[:, :, :], llam.ap(), writes=['lam'])
        T.op('act', lambda e: e.activation(out=lam[:, :, :], in_=lam[:, :, :], func=AF.Exp, scale=-1.0), ['lam'], ['lam'])
        T.op('act', lambda e: e.activation(out=lam[:, :, :], in_=lam[:, :, :], func=AF.Ln, bias=1.0), ['lam'], ['lam'])
        T.op('dve', lambda e: e.tensor_scalar(out=cl[:, :, :], in0=lam[:, :, :], scalar1=-8.0, scalar2=None,
                                              op0=ALU.mult), ['lam'], ['cl'])
        T.op('dve', lambda e: e.tensor_scalar(out=cl2[:, :, :], in0=lam[:, :, :], scalar1=-16.0, scalar2=None,
                                              op0=ALU.mult), ['lam'], ['cl2'])
        Wbd = T.alloc([128, 2, 2, 4, 128], BF16)
        T.op('pool', lambda e: e.memset(Wbd[:, :, :, :, :], 0.0), [], ['Wbd'])
        for d in range(2):
            for gi, src in enumerate((lwa, lwx)):
                for blk in range(4):
                    for half in range(2):
                        T.dma('pool', Wbd[half * 64:(half + 1) * 64, d, gi, blk, half * 64:(half + 1) * 64],
                              src.ap()[d, blk * 2 + half], reads=[], writes=['Wbd'])
        zxh = [T.alloc([128, 4, 516], F32) for _ in range(2)]
        xr_ = [T.alloc([128, 4, 512], F32) for _ in range(2)]
        xrb_ = [T.alloc([128, 4, 512], BF16) for _ in range(2)]
        rg_ = [T.alloc([128, 4, 512], F32) for _ in range(2)]
        ig_ = [T.alloc([128, 4, 512], F32) for _ in range(2)]
        aa_ = [T.alloc([128, 4, 512], F32) for _ in range(2)]
        a2_ = [T.alloc([128, 4, 512], F32) for _ in range(2)]
        hout = [T.alloc([128, 4, 512], F32) for _ in range(2)]
        hl = [T.alloc([128, 4, 512], F32) for _ in range(2)]
        gz = [T.alloc([128, 4, 512], BF16) for _ in range(2)]
        yo = [T.alloc([128, 4, 512], BF16) for _ in range(2)]
        hprev = T.alloc([128, 4], F32)
        def stage1(d, ci, b):
            if True:
                st, N, isctx = CHUNKS[ci]
                xr, xrb, rg, ig, aa, a2 = xr_[b], xrb_[b], rg_[b], ig_[b], aa_[b], a2_[b]
                lo_seq, hi_seq = (0, LC) if isctx else (LC, NT)
                lo = max(st - 2, lo_seq)
                hi = min(st + N + 2, hi_seq)
                if lo > st - 2 or hi < st + N + 2:
                    T.op('pool', lambda e: e.memset(zxh[b][:, :, :], 0.0), [], [('zxh', b)])
                T.dma('sp', zxh[b][:, :, 2 + (lo - st):2 + (hi - st)], ZX.ap().rearrange("b p t -> p b t")[:, :, lo:hi],
                      reads=[('ZX', c2) for c2 in range(len(CHUNKS))], writes=[('zxh', b)])
                if d == 1 and not isctx:
                    T.dma('sp', hl[b][:, :, :N], HL.ap().rearrange("b p t -> p b t")[:, :, st:st + N],
                          reads=[('HL', ci)], writes=[('hl', b)])
                    T.dma('sp', gz[b][:, :, :N], GZ.ap().rearrange("b p t -> p b t")[:, :, st:st + N],
                          reads=[('GZ', ci)], writes=[('gz', b)])
                xru = [('xr', b, blk) for blk in range(4)]
                for blk in range(4):
                    T.op('dve', lambda e, blk=blk: e.tensor_scalar(
                        out=xr[:, blk, :N], in0=zxh[b][:, blk, 1:1 + N], scalar1=cw[:, blk, 0:1],
                        scalar2=cb[:, blk:blk + 1], op0=ALU.mult, op1=ALU.add), [('zxh', b), 'cw', 'cb'], [xru[blk]])
                    for j in range(1, 4):
                        T.op('dve', lambda e, blk=blk, j=j: e.scalar_tensor_tensor(
                            out=xr[:, blk, :N], in0=zxh[b][:, blk, 1 + j:1 + j + N], scalar=cw[:, blk, j:j + 1],
                            in1=xr[:, blk, :N], op0=ALU.mult, op1=ALU.add), [('zxh', b), 'cw', xru[blk]], [xru[blk]])
                T.op('act', lambda e: e.activation(out=xrb[:, :, :N], in_=xr[:, :, :N], func=AF.Copy),
                     xru, [('xrb', b)])
                for blk in range(4):
                    T.op('pe', lambda e, blk=blk: e.matmul(PS[blk][:, :N], lhsT=Wbd[:, d, 0, blk, :],
                                                         rhs=xrb[:, blk, :N], start=True, stop=True),
                         ['Wbd', ('xrb', b)], [psu(blk)])
                    T.op('pe', lambda e, blk=blk: e.matmul(PS[4 + blk][:, :N], lhsT=Wbd[:, d, 1, blk, :],
                                                         rhs=xrb[:, blk, :N], start=True, stop=True),
                         ['Wbd', ('xrb', b)], [psu(4 + blk)])
                for blk in range(4):
                    T.op('act', lambda e, blk=blk: e.activation(out=rg[:, blk, :N], in_=PS[blk][:, :N], func=AF.Sigmoid,
                                                               bias=ba[:, d, blk:blk + 1]), [psu(blk), 'ba'],
                         [('rg', b, blk)])
                for blk in range(4):
                    T.op('act', lambda e, blk=blk: e.activation(out=ig[:, blk, :N], in_=PS[4 + blk][:, :N],
                                                               func=AF.Sigmoid, bias=bx[:, d, blk:blk + 1]),
                         [psu(4 + blk), 'bx'], [('ig', b, blk)])
                for blk in range(4):
                    T.op('act', lambda e, blk=blk: e.activation(out=aa[:, blk, :N], in_=rg[:, blk, :N], func=AF.Exp,
                                                               scale=cl[:, d, blk:blk + 1]), [('rg', b, blk), 'cl'],
                         [('aa', b, blk)])
                for blk in range(4):
                    T.op('act', lambda e, blk=blk: e.activation(out=a2[:, blk, :N], in_=rg[:, blk, :N], func=AF.Exp,
                                                               scale=cl2[:, d, blk:blk + 1]), [('rg', b, blk), 'cl2'],
                         [('a2', b, blk)])
                a2u = [('a2', b, blk) for blk in range(4)]
                igu = [('ig', b, blk) for blk in range(4)]
                T.op('act', lambda e: e.activation(out=a2[:, :, :N], in_=a2[:, :, :N], func=AF.Sqrt, bias=1.0,
                                                   scale=-1.0), a2u, a2u)
                T.op('dve', lambda e: e.tensor_tensor(out=ig[:, :, :N], in0=ig[:, :, :N], in1=a2[:, :, :N],
                                                      op=ALU.mult), igu + a2u, igu)
                T.op('dve', lambda e: e.tensor_tensor(out=ig[:, :, :N], in0=ig[:, :, :N], in1=xr[:, :, :N],
                                                      op=ALU.mult), igu + xru, igu)
        def stage2(d, ci, b):
            if True:
                st, N, isctx = CHUNKS[ci]
                xr, xrb, rg, ig, aa, a2 = xr_[b], xrb_[b], rg_[b], ig_[b], aa_[b], a2_[b]
                hu = [('hout', b, blk) for blk in range(4)]
                for blk in range(4):
                    if d == 0:
                        T.op('dve', lambda e, blk=blk: e.tensor_tensor_scan(
                            out=hout[b][:, blk, :N], data0=aa[:, blk, :N], data1=ig[:, blk, :N],
                            initial=hprev[:, blk:blk + 1], op0=ALU.mult, op1=ALU.add),
                            [('aa', b, blk), ('ig', b, blk), 'hprev'], [hu[blk]])
                    else:
                        T.op('dve', lambda e, blk=blk: e.tensor_tensor_scan(
                            out=hout[b][:, blk, :N][:, ::-1], data0=aa[:, blk, :N][:, ::-1],
                            data1=ig[:, blk, :N][:, ::-1], initial=hprev[:, blk:blk + 1], op0=ALU.mult, op1=ALU.add),
                            [('aa', b, blk), ('ig', b, blk), 'hprev'], [hu[blk]])
                lastc = N - 1 if d == 0 else 0
                T.op('dve', lambda e: e.tensor_copy(out=hprev[:, :], in_=hout[b][:, :, lastc]), hu, ['hprev'])
                if isctx:
                    return
                if d == 0:
                    T.dma('sp', HL.ap().rearrange("b p t -> p b t")[:, :, st:st + N], hout[b][:, :, :N],
                          reads=hu, writes=[('HL', ci)])
                else:
                    T.op('dve', lambda e: e.tensor_tensor(out=hl[b][:, :, :N], in0=hl[b][:, :, :N],
                                                          in1=hout[b][:, :, :N], op=ALU.add),
                         hu + [('hl', b)], [('hl', b)])
                    T.op('dve', lambda e: e.tensor_tensor(out=yo[b][:, :, :N], in0=hl[b][:, :, :N],
                                                          in1=gz[b][:, :, :N], op=ALU.mult),
                         [('hl', b), ('gz', b)], [('yol', b)])
                    T.dma('sp', YT.ap()[4:8].rearrange("b p t -> p b t")[:, :, st:st + N], yo[b][:, :, :N],
                          reads=[('yol', b)], writes=[('YT', ci)])

        steps = []
        for d in range(2):
            order = [0] + (list(range(1, 17)) if d == 0 else list(range(16, 0, -1)))
            for ci in order:
                steps.append((d, ci))
        stage1(steps[0][0], steps[0][1], 0)
        for i_, (d, ci) in enumerate(steps):
            if i_ + 1 < len(steps):
                stage1(steps[i_ + 1][0], steps[i_ + 1][1], (i_ + 1) % 2)
            if ci == 0:
                T.op('dve', lambda e: e.memset(hprev[:, :], 0.0), [], ['hprev'])
            stage2(d, ci, i_ % 2)

    def phase3bc(li, chunk_ids, final):
        Wi = T.alloc([128, 8, 2 * DFF], BF16, 'Wi')
        for kc in range(8):
            T.dma('pool', Wi[:, kc, :], ffn_in.ap()[li, kc * 128:(kc + 1) * 128, :], writes=[('Wi', kc)])
        Wo = load_Wo(li)
        Wou = [('Wo2', f0) for f0 in range(0, 22, 2)]
        NS = 256
        hT = [T.alloc([128, 8, NS], BF16) for _ in range(2)]
        xT = [T.alloc([128, 8, NS], F32) for _ in range(2)]
        ac = [T.alloc([128, 22, NS], BF16) for _ in range(2)]
        sg = [T.alloc([128, NS], F32) for _ in range(2)]
        if final:
            tmp = T.alloc([128, 8, NS], F32)
            sq = T.alloc([128, 8, NS], BF16)
            rstd = T.alloc([128, NS], F32)
            otok = [T.alloc([128, D], F32) for _ in range(2)]
        subs = []
        for ci in chunk_ids:
            st, N, isctx = CHUNKS[ci]
            for s0 in range(0, N, NS):
                subs.append((ci, st + s0, isctx))
        cnt = {'n': 0, 'o': 0, 'no': 0}

        def load(si):
            ci, st, isctx = subs[si]
            b = si % 2
            T.dma('sp', hT[b][:, :, :], H2.ap()[:, :, st:st + NS], reads=[('H2', ci)], writes=[('hT', b)])
            T.dma('sp', xT[b][:, :, :], R.ap()[:, :, st:st + NS], reads=[('R', ci)], writes=[('xT', b)])

        load(0)
        for si, (ci, st, isctx) in enumerate(subs):
            b = si % 2
            who = 1 if isctx else 0
            if si + 1 < len(subs):
                load(si + 1)
            for fb in range(22):
                pg = (cnt['n'] % 3) * 2
                s2 = cnt['n'] % 2
                cnt['n'] += 1
                for (col0, pbk) in ((fb * 128, pg), (DFF + fb * 128, pg + 1)):
                    for kc in range(8):
                        T.op('pe', lambda e, col0=col0, pbk=pbk, kc=kc: e.matmul(
                            PS[pbk][:, :NS], lhsT=Wi[:, kc, col0:col0 + 128], rhs=hT[b][:, kc, :],
                            start=(kc == 0), stop=(kc == 7)), [('hT', b), ('Wi', kc)], [psu(pbk)])
                T.op('act', lambda e: e.activation(out=sg[s2][:, :], in_=PS[pg][:, :NS], func=AF.Silu),
                     [psu(pg)], [('sg', s2)])
                T.op('dve', lambda e, fb=fb: e.tensor_tensor(
                    out=ac[b][:, fb, :], in0=PS[pg + 1][:, :NS], in1=sg[s2][:, :], op=ALU.mult),
                    [psu(pg + 1), ('sg', s2)], [('ac', b, fb)])
            for j in range(8):
                pb = 6 + cnt['o'] % 2
                cnt['o'] += 1
                for fb in range(22):
                    T.op('pe', lambda e, fb=fb, j=j: e.matmul(
                        PS[pb][:, :NS], lhsT=Wo[:, fb, j * 128:(j + 1) * 128], rhs=ac[b][:, fb, :],
                        start=(fb == 0), stop=(fb == 21)), [('ac', b, fb)] + Wou, [psu(pb)])
                g_ap = modT[:, li, 5 * 8 + j, who:who + 1]
                T.op('dve', lambda e, j=j: e.scalar_tensor_tensor(
                    out=xT[b][:, j, :], in0=PS[pb][:, :NS], scalar=g_ap, in1=xT[b][:, j, :],
                    op0=ALU.mult, op1=ALU.add), [psu(pb), ('xT', b), 'modT'], [('xT', b)])
            if not final:
                T.dma('sp', R.ap()[:, :, st:st + NS], xT[b][:, :, :], reads=[('xT', b)], writes=[('R', ci)])
            else:
                rmsnorm_mod(xT[b], ('xT', b), NS, li, 0, None, None, None, tmp, 'tmp', sq, 'sq', rstd, 'rstd', 0)
                for tt in range(NS // 128):
                    o2 = cnt['no'] % 2
                    cnt['no'] += 1
                    for j in range(8):
                        pbk = 2 + (j // 4) + 2 * o2
                        T.op('pe', lambda e, j=j, tt=tt, pbk=pbk: e.transpose(
                            PS[pbk][:, (j % 4) * 128:(j % 4 + 1) * 128], tmp[:, j, tt * 128:(tt + 1) * 128],
                            ident[:]), ['tmp', 'ident'], [psu(pbk)])
                    for hf in range(2):
                        pbk = 2 + hf + 2 * o2
                        T.op('act' if hf else 'dve', (lambda e, pbk=pbk, o2=o2, hf=hf: e.activation(
                            out=otok[o2][:, hf * 512:(hf + 1) * 512], in_=PS[pbk][:, :], func=AF.Copy)) if hf else
                            (lambda e, pbk=pbk, o2=o2, hf=hf: e.tensor_copy(
                                out=otok[o2][:, hf * 512:(hf + 1) * 512], in_=PS[pbk][:, :])),
                            [psu(pbk)], [('otok', o2, hf)])
                    r0_ = st - 256 + tt * 128
                    T.dma('sp', out.ap()[r0_:r0_ + 128, :], otok[o2][:, :],
                          reads=[('otok', o2, 0), ('otok', o2, 1)], writes=[('out', ci, si, tt)])

    ALLC = list(range(len(CHUNKS)))
    LAT = list(range(1, len(CHUNKS)))
    phase0()
    T.barrier(PERSIST)
    phase1()
    T.barrier(PERSIST)
    phase2a()
    T.barrier(PERSIST)
    phase2b()
    T.barrier(PERSIST)
    phase3a(0, [(ATd.ap(), 4, 128), (BTd.ap(), 8, 64)],
            [(wo0A.ap(), 4, 128), (wo0B.ap(), 8, 64)], ALLC)
    T.barrier(PERSIST)
    phase3bc(0, ALLC, final=False)
    T.barrier(PERSIST)
    phaseP1()
    T.barrier(PERSIST)
    phaseG()
    T.barrier(PERSIST)
    phaseLr()
    T.barrier(PERSIST)
    phase3a(1, [(YT.ap(), 8, 128)], [(wo1.ap().rearrange("(q p) d -> p q d", p=128), 8, 128)], LAT)
    T.barrier(PERSIST)
    phase3bc(1, LAT, final=True)
    T.barrier(PERSIST)
    if debug:
        for ci in ALLC:
            st, N, _ = CHUNKS[ci]
            for j in range(8):
                T.dma('sp', dbg.ap()[:, j, st:st + N], R.ap()[:, j, st:st + N], reads=[('R', ci)], writes=[('dbg', ci, j)])
        for h in range(8):
            T.dma('sp', dbg2.ap()[h], YT.ap()[h], reads=[('YT', ci) for ci in ALLC], writes=[('dbg2', h)])
        T.barrier(PERSIST)
    T.emit()
    return nc


def _bf(a):
    return np.asarray(a).astype(ml_dtypes.bfloat16)


def make_consts():
    inv = (10000.0 ** (-np.arange(0, 32, 2, dtype=np.float32) / 32.0)).astype(np.float32)
    t = np.arange(L)
    row = (t // 64).astype(np.float32)
    col = (t % 64).astype(np.float32)
    C = np.ones((128, NT), np.float32)
    S = np.zeros((128, NT), np.float32)
    for p in range(128):
        dd = p % 64
        grp, within = dd // 32, dd % 32
        f, part = within % 16, within // 16
        ang = (row if grp == 0 else col) * inv[f]
        C[p, LC:] = np.cos(ang.astype(np.float32))
        sn = np.sin(ang.astype(np.float32))
        S[p, LC:] = -sn if part == 0 else sn
    j = np.arange(128)[:, None]
    q = np.arange(128)[None, :]
    m0 = np.where(j >= q, 0.0, NEG).astype(np.float32)
    m1 = np.where(j <= q, 0.0, NEG).astype(np.float32)
    maskb = np.stack([np.tile(m0, (1, 4)), np.tile(m1, (1, 4))]).astype(ml_dtypes.bfloat16)
    cp = np.arange(64)[:, None]
    c_ = np.arange(64)[None, :]
    tri = np.stack([cp <= c_, cp >= c_, cp > c_, cp < c_]).astype(np.float32)
    mk4 = np.stack([np.tile((cp <= c_), (1, 4)), np.tile((cp >= c_), (1, 4))]).astype(np.float32)
    return C, S, maskb, tri, mk4


def perm64(cols):
    cols = np.asarray(cols)
    dd = np.arange(64)
    within = dd % 32
    partner = np.where(within // 16 == 0, dd + 16, dd - 16)
    out = cols.reshape(-1, 64)[:, partner].reshape(-1)
    return out


def prep_inputs(inp, b):
    f = np.float32
    x, c, ctx = inp['x'], inp['c'], inp['ctx']
    m = {}
    m['xall'] = np.ascontiguousarray(np.concatenate([ctx[b], x[b]], 0), f)
    cc = np.stack([c[b].reshape(8, 128).T, inp['c_ctx'].reshape(8, 128).T], -1)
    m['ccols'] = np.ascontiguousarray(cc, f)
    m['ada_w'] = inp['ada_w']
    m['ada_b2'] = np.ascontiguousarray(np.repeat(inp['ada_b'][:, None, :], 2, 1), f)
    m['normg'] = np.ascontiguousarray(inp['norm_g'].reshape(2, 2, 8, 128).transpose(3, 0, 1, 2), f)
    m['finalg'] = np.ascontiguousarray(inp['final_g'].reshape(8, 128).T, f)
    w = inp['even_w_in'][0]
    qa = np.arange(0, 512)
    ka = np.arange(512, 1024)
    va = np.arange(1024, 1536)
    qb = 1536 + np.arange(512).reshape(2, 4, 64).transpose(1, 0, 2).reshape(-1)
    kb = np.arange(2048, 2176)
    vb = np.arange(2176, 2304)
    cols = np.concatenate([qa, perm64(qa), ka, perm64(ka), qb, perm64(qb), kb, perm64(kb), va, vb])
    m['w0'] = np.ascontiguousarray(w[:, cols], f)
    wo = inp['even_w_out'][0]
    m['wo0A'] = np.ascontiguousarray(wo[:512].reshape(4, 128, D).transpose(1, 0, 2), f)
    m['wo0B'] = np.ascontiguousarray(wo[512:].reshape(8, 64, D).transpose(1, 0, 2), f)
    m['difflam'] = np.ascontiguousarray(np.broadcast_to(inp['diff_lam'][0][None], (128, 4, 64)), f)
    m['sinkc'] = np.ascontiguousarray(np.broadcast_to(inp['win_sink'][0][None], (128, 8)), f)
    m['ident'] = np.eye(128, dtype=f)
    m['ffn_in'] = inp['ffn_w_in']
    m['w1'] = np.ascontiguousarray(inp['odd_w_in'][0], f)
    m['wo1'] = np.ascontiguousarray(inp['odd_w_out'][0], f)
    m['gw'] = np.ascontiguousarray(inp['gla_gate_w'][0], f)
    gb = inp['gla_gate_b'][0]
    m['gb2'] = np.ascontiguousarray(np.broadcast_to(np.concatenate([gb, gb], -1)[:, None, :], (2, 64, 512)), f)
    m['glag'] = np.ascontiguousarray(inp['gla_norm_g'][0].reshape(128, 1), f)
    m['cw'] = np.ascontiguousarray(inp['lru_conv_w'][0].reshape(4, 4, 128).transpose(2, 1, 0), f)
    m['cb'] = np.ascontiguousarray(inp['lru_conv_b'][0].reshape(4, 128).T, f)
    m['lwa'] = np.ascontiguousarray(inp['lru_wa'][0], f)
    m['lwx'] = np.ascontiguousarray(inp['lru_wx'][0], f)
    m['lba'] = np.ascontiguousarray(inp['lru_ba'][0].reshape(2, 4, 128).transpose(2, 0, 1), f)
    m['lbx'] = np.ascontiguousarray(inp['lru_bx'][0].reshape(2, 4, 128).transpose(2, 0, 1), f)
    m['llam'] = np.ascontiguousarray(inp['lru_lam'][0].reshape(2, 4, 128).transpose(2, 0, 1), f)
    m['ffn_out'] = inp['ffn_w_out']
    return m


_CACHE = {}


def kernel(**inputs):
    inp = {k: np.asarray(v) for k, v in inputs.items()}
    if 'nc' not in _CACHE:
        _CACHE['nc'] = build()
        _CACHE['consts'] = make_consts()
    nc = _CACHE['nc']
    C, S, maskb, tri, mk4 = _CACHE['consts']
    in_maps = []
    for b in range(NCORES):
        m = prep_inputs(inp, b)
        m['ropeC'], m['ropeS'], m['maskb'], m['tri'], m['mask4'] = C, S, maskb, tri, mk4
        in_maps.append(m)
    res = run_bass_kernel_spmd(nc, in_maps, core_ids=list(range(NCORES)))
    return np.stack([np.asarray(res.results[b]['out']) for b in range(NCORES)], 0).astype(np.float32)
```

```python
import math
import numpy as np
import ml_dtypes
import concourse.bass as bass
import concourse.mybir as mybir
from concourse.bass_utils import run_bass_kernel_spmd

F32, BF16 = mybir.dt.float32, mybir.dt.bfloat16
AF = mybir.ActivationFunctionType
ALU = mybir.AluOpType

D = 1024
L = 8192
LC = 256
NT = L + LC
NCORES = 4
DFF = 2816
EPS = 1e-6
CHUNKS = [(0, 256, True)] + [(256 + 512 * i, 512, False) for i in range(16)]
NEG = -240000.0
ENGS = ['pe', 'act', 'dve', 'pool', 'sp']
NDMA = {'sp': 40, 'pool': 16}


import types


def _freeze(fn):
    if fn is None or fn.__closure__ is None:
        return fn
    cells = []
    for c in fn.__closure__:
        try:
            cells.append(types.CellType(c.cell_contents))
        except ValueError:
            cells.append(c)
    return types.FunctionType(fn.__code__, fn.__globals__, fn.__name__, fn.__defaults__, tuple(cells))


class Tr:
    def __init__(s, nc):
        s.nc = nc
        s.ops = {e: [] for e in ENGS}
        s.lastw = {}
        s.rd = {}
        s.ndma = {'sp': 0, 'pool': 0}
        s.dma_since = []
        s.arena = 16512
        s.names = 0

    END = 229248

    def alloc(s, shape, dt, name=None):
        nb = int(np.prod(shape[1:])) * (4 if dt == F32 else 2)
        nb = (nb + 63) // 64 * 64
        if not hasattr(s, 'free'):
            s.free = [[s.arena, s.END]]
            s.pf_next = []
            s.keep_next = set()
            s.in_pf = False
        off = None
        if s.in_pf:
            for seg in reversed(s.free):
                if seg[1] - seg[0] >= nb:
                    seg[1] -= nb
                    off = seg[1]
                    break
            if off is not None:
                s.pf_next.append((off, off + nb))
        else:
            for seg in s.free:
                if seg[1] - seg[0] >= nb:
                    off = seg[0]
                    seg[0] += nb
                    break
        assert off is not None, ("sbuf overflow", nb, s.free)
        s.names += 1
        return s.nc.alloc_sbuf_tensor_at(f"{name or 't'}_{s.names}", list(shape), dt, offset=off)

    def prefetch(s, fn):
        if not hasattr(s, 'free'):
            s.alloc([128, 16], F32)
        s.in_pf = True
        r = fn()
        s.in_pf = False
        return r

    def op(s, eng, fn, reads=(), writes=(), dma=False):
        raw, oth = set(), set()
        for u in reads:
            w = s.lastw.get(u)
            if w is not None:
                raw.add(w)
        for u in writes:
            w = s.lastw.get(u)
            if w is not None:
                oth.add(w)
            r = s.rd.get(u)
            if r:
                for e2, i2 in r[0].items():
                    oth.add((e2, i2))
                for x in r[1]:
                    oth.add(x)
        idx = len(s.ops[eng])
        me = (eng, idx)
        deps = set()
        for dset, israw in ((raw, True), (oth, False)):
            for (e2, i2) in dset:
                d2 = s.ops[e2][i2]
                if e2 == eng and not d2['dma'] and not dma:
                    if eng == 'pe' or not israw:
                        continue
                deps.add((e2, i2))
        rec = dict(fn=_freeze(fn), deps=deps, inc=False, dma=dma)
        if dma:
            rec['k'] = s.ndma[eng]
            rec['q'] = eng
            s.ndma[eng] += 1
            if getattr(s, 'in_pf', False):
                for u in writes:
                    s.keep_next.add(u)
            else:
                s.dma_since.append(me)
        s.ops[eng].append(rec)
        for u in reads:
            r = s.rd.setdefault(u, ({}, []))
            if dma:
                r[1].append(me)
            else:
                r[0][eng] = idx
        for u in writes:
            s.lastw[u] = me
            s.rd[u] = ({}, [])
        return me

    def dma(s, q, out, in_, reads=(), writes=()):
        return s.op(q, lambda e: e.dma_start(out=out, in_=in_), reads, writes, dma=True)

    def barrier(s, reset_to=None):
        lasts = []
        for e in ENGS:
            for i in range(len(s.ops[e]) - 1, -1, -1):
                if s.ops[e][i]['fn'] is not None:
                    if not s.ops[e][i]['dma']:
                        lasts.append((e, i))
                    break
        deps = set(lasts) | set(s.dma_since)
        for e in ENGS:
            s.ops[e].append(dict(fn=None, deps=set(d for d in deps), inc=False, dma=False))
        s.dma_since = []
        keep = getattr(s, 'keep_next', set())
        s.lastw = {u: w for u, w in s.lastw.items() if u in keep}
        s.rd = {u: ({}, []) for u in s.lastw}
        s.keep_next = set()
        if reset_to is not None:
            segs = [[reset_to, s.END]]
            for (lo, hi) in sorted(getattr(s, 'pf_next', [])):
                new = []
                for a_, b_ in segs:
                    if hi <= a_ or lo >= b_:
                        new.append([a_, b_])
                    else:
                        if a_ < lo:
                            new.append([a_, lo])
                        if hi < b_:
                            new.append([hi, b_])
                segs = new
            s.free = segs
            s.pf_next = []

    def emit(s):
        nc = s.nc
        for e in ENGS:
            for rec in s.ops[e]:
                for (e2, i2) in rec['deps']:
                    d2 = s.ops[e2][i2]
                    if not d2['dma']:
                        d2['inc'] = True
        for e in ENGS:
            c = 0
            for rec in s.ops[e]:
                if rec['inc']:
                    c += 1
                    rec['cnt'] = c
        import contextlib
        with contextlib.ExitStack() as st:
            esem = {e: st.enter_context(nc.semaphore(f"s_{e}")) for e in ENGS if e != 'sp'}
            esem['sp'] = st.enter_context(nc.semaphore("s_sp"))
            dsem = {q: [st.enter_context(nc.semaphore(f"d_{q}_{i}")) for i in range(n)] for q, n in NDMA.items()}
            block = st.enter_context(nc.Block())

            def run(ename):
                def body(eng):
                    waited = {}
                    for rec in s.ops[ename]:
                        need = {}
                        for (e2, i2) in rec['deps']:
                            d2 = s.ops[e2][i2]
                            if d2['dma']:
                                k, q = d2['k'], d2['q']
                                key, val = ('d', q, k % NDMA[q]), 16 * (k // NDMA[q] + 1)
                            else:
                                key, val = ('e', e2), d2['cnt']
                            if val > need.get(key, 0):
                                need[key] = val
                        if rec['dma'] and rec['k'] >= NDMA[rec['q']]:
                            nq = NDMA[rec['q']]
                            key, val = ('d', rec['q'], rec['k'] % nq), 16 * (rec['k'] // nq)
                            if val > need.get(key, 0):
                                need[key] = val
                        todo = []
                        for key, val in need.items():
                            if waited.get(key, 0) >= val:
                                continue
                            waited[key] = val
                            sem = dsem[key[1]][key[2]] if key[0] == 'd' else esem[key[1]]
                            todo.append((sem, val))
                        attach = None
                        if todo and rec['fn'] is not None and not rec['dma'] and ename != 'pe':
                            attach = todo.pop()
                        for sem, val in todo:
                            eng.wait_ge(sem, val)
                        if rec['fn'] is None:
                            continue
                        ins = rec['fn'](eng)
                        if attach is not None:
                            ins._wait_ge(attach[0], attach[1])
                        if rec['dma']:
                            ins.then_inc(dsem[rec['q']][rec['k'] % NDMA[rec['q']]], 16)
                        elif rec['inc']:
                            ins.then_inc(esem[ename], 1)
                return body

            block.tensor(run('pe'))
            block.scalar(run('act'))
            block.vector(run('dve'))
            block.gpsimd(run('pool'))
            block.sync(run('sp'))


def build(debug=False, stop_after=None):
    nc = bass.Bass("TRN2", target_bir_lowering=False)
    T = Tr(nc)

    def din(name, shape, dt=F32):
        return nc.dram_tensor(name, list(shape), dt, kind="ExternalInput")

    def dscr(name, shape, dt):
        return nc.dram_tensor(name, list(shape), dt, kind="Internal")

    xall = din("xall", [NT, D])
    ccols = din("ccols", [128, 8, 2])
    ada_w = din("ada_w", [2, D, 6 * D])
    ada_b2 = din("ada_b2", [2, 2, 6 * D])
    normg = din("normg", [128, 2, 2, 8])
    finalg = din("finalg", [128, 8])
    w0 = din("w0", [D, 3968])
    wo0A = din("wo0A", [128, 4, D])
    wo0B = din("wo0B", [64, 8, D])
    difflam = din("difflam", [128, 4, 64])
    sinkc = din("sinkc", [128, 8])
    ropeC = din("ropeC", [128, NT])
    ropeS = din("ropeS", [128, NT])
    maskb = din("maskb", [2, 128, 512], BF16)
    ident_in = din("ident", [128, 128])
    ffn_in = din("ffn_in", [2, D, 2 * DFF])
    ffn_out = din("ffn_out", [2, DFF, D])
    w1 = din("w1", [D, 2592])
    wo1 = din("wo1", [D, D])
    gw_in = din("gw", [2, 16, 256])
    gb2_in = din("gb2", [2, 64, 512])
    glag_in = din("glag", [128, 1])
    cw_in = din("cw", [128, 4, 4])
    cb_in = din("cb", [128, 4])
    lwa = din("lwa", [2, 8, 64, 64])
    lwx = din("lwx", [2, 8, 64, 64])
    lba = din("lba", [128, 2, 4])
    lbx = din("lbx", [128, 2, 4])
    llam = din("llam", [128, 2, 4])
    tri_in = din("tri", [4, 64, 64])
    mask4_in = din("mask4", [2, 64, 256])
    out = nc.dram_tensor("out", [L, D], F32, kind="ExternalOutput")
    QS = dscr("QS", [2, 4, 64, NT], BF16)
    KS = dscr("KS", [2, 4, 64, NT], BF16)
    KD = dscr("KD", [2, NT, 256], BF16)
    V1 = dscr("V1", [NT, 512], BF16)
    ZX = dscr("ZX", [4, 128, NT], F32)
    GZ = dscr("GZ", [4, 128, NT], BF16)
    SG = dscr("SG", [4, 128, NT], BF16)
    HL = dscr("HL", [4, 128, NT], F32)
    OG = dscr("OG", [4, 128, NT], F32)
    YT = dscr("YT", [8, 128, NT], BF16)

    R = dscr("R", [128, 8, NT], F32)
    QAT = dscr("QAT", [4, 128, NT], BF16)
    KAT = dscr("KAT", [4, 128, NT], BF16)
    QBT = dscr("QBT", [4, 128, NT], BF16)
    KBT = dscr("KBT", [128, NT], BF16)
    VA = dscr("VA", [NT, 512], BF16)
    VB = dscr("VB", [NT, 128], BF16)
    ATd = dscr("ATd", [4, 128, NT], BF16)
    BTd = dscr("BTd", [8, 64, NT], BF16)
    H2 = dscr("H2", [128, 8, NT], BF16)
    ACTd = dscr("ACTd", [22, 128, NT], BF16)
    dbg = nc.dram_tensor("dbg", [128, 8, NT], F32, kind="ExternalOutput") if debug else None
    dbg2 = nc.dram_tensor("dbg2", [8, 128, NT], BF16, kind="ExternalOutput") if debug else None

    P2 = [nc.alloc_psum_tensor(f"ps{i}", [128, 1024], F32) for i in range(4)]
    PS = []
    for i in range(4):
        PS.append(P2[i][:, 0:512])
        PS.append(P2[i][:, 512:1024])

    def psu(i):
        return ('ps', i)

    ident = T.alloc([128, 128], F32, 'ident')
    identb = T.alloc([128, 128], BF16, 'identb')
    onesb = T.alloc([128, 128], BF16, 'onesb')
    modT = T.alloc([128, 2, 48, 2], F32, 'modT')
    Acol = T.alloc([128, 2, 2, 8, 2], F32, 'Acol')
    ngc = T.alloc([128, 2, 2, 8], F32, 'ngc')
    fgc = T.alloc([128, 8], F32, 'fgc')
    neglam = T.alloc([128, 1], F32, 'neglam')
    esink = T.alloc([128, 8], F32, 'esink')
    T.dma('sp', ident[:], ident_in.ap(), writes=['ident'])
    T.dma('sp', ngc[:], normg.ap(), writes=['ngc'])
    T.dma('sp', fgc[:], finalg.ap(), writes=['fgc'])
    T.op('dve', lambda e: e.tensor_copy(out=identb[:], in_=ident[:]), ['ident'], ['identb'])
    T.op('dve', lambda e: e.memset(onesb[:], 1.0), [], ['onesb'])
    decs = T.alloc([64, 2, 4, 132], F32, 'decs')
    PERSIST = T.free[0][0]

    def eps_bias(e):
        return EPS

    def phase0():
        cc = T.alloc([128, 8, 2], F32)
        cond = T.alloc([128, 8, 2], F32)
        T.dma('sp', cc[:], ccols.ap(), writes=['cc'])
        T.op('act', lambda e: e.activation(out=cond[:], in_=cc[:], func=AF.Silu), ['cc'], ['cond'])
        wb = [T.alloc([128, 3072], F32) for _ in range(3)]
        modrow = T.alloc([2, 6144], F32)
        adab = T.alloc([2, 6144], F32)
        nload = 0
        for li in range(2):
            T.dma('sp', adab[:], ada_b2.ap()[li], writes=['adab'])
            for half in range(2):
                for kc in range(8):
                    w = wb[nload % 3]
                    wu = ('wb', nload % 3)
                    nload += 1
                    T.dma('sp', w[:], ada_w.ap()[li, kc * 128:(kc + 1) * 128, half * 3072:(half + 1) * 3072],
                          writes=[wu])
                    for cb in range(6):
                        T.op('pe', lambda e, w=w, cb=cb, kc=kc: e.matmul(
                            PS[cb][0:2, :], lhsT=cond[:, kc, :], rhs=w[:, cb * 512:(cb + 1) * 512],
                            start=(kc == 0), stop=(kc == 7)), [wu, 'cond'], [psu(cb)])
                for cb in range(6):
                    c0 = half * 3072 + cb * 512
                    T.op('dve', lambda e, cb=cb, c0=c0: e.tensor_tensor(
                        out=modrow[:, c0:c0 + 512], in0=PS[cb][0:2, :], in1=adab[:, c0:c0 + 512], op=ALU.add),
                        [psu(cb), 'adab'], ['modrow'])
            for blk in range(48):
                T.op('pe', lambda e, blk=blk: e.transpose(
                    PS[6][:, blk * 2:blk * 2 + 2], modrow[0:2, blk * 128:(blk + 1) * 128], ident[0:2, 0:2]),
                    ['modrow', 'ident'], [psu(6)])
            T.op('dve', lambda e, li=li: e.tensor_copy(
                out=modT[:, li].rearrange("p a b -> p (a b)"), in_=PS[6][:, 0:96]), [psu(6)], ['modT'])
            for n in range(2):
                for w_ in range(2):
                    T.op('dve', lambda e, li=li, n=n, w_=w_: e.scalar_tensor_tensor(
                        out=Acol[:, li, n, :, w_], in0=modT[:, li, (3 * n + 1) * 8:(3 * n + 2) * 8, w_], scalar=1.0,
                        in1=ngc[:, li, n, :], op0=ALU.add, op1=ALU.mult), ['modT', 'ngc'], ['Acol'])
        dl = T.alloc([128, 4, 64], F32)
        pr = T.alloc([128, 2, 64], F32)
        sm = T.alloc([128, 2], F32)
        ex = T.alloc([128, 2], F32)
        T.dma('sp', dl[:], difflam.ap(), writes=['dl'])
        T.op('dve', lambda e: e.tensor_tensor(out=pr[:, 0, :], in0=dl[:, 0, :], in1=dl[:, 1, :], op=ALU.mult),
             ['dl'], ['pr0'])
        T.op('dve', lambda e: e.tensor_tensor(out=pr[:, 1, :], in0=dl[:, 2, :], in1=dl[:, 3, :], op=ALU.mult),
             ['dl'], ['pr1'])
        T.op('dve', lambda e: e.reduce_sum(out=sm[:], in_=pr[:], axis=mybir.AxisListType.X), ['pr0', 'pr1'], ['sm'])
        T.op('act', lambda e: e.activation(out=ex[:], in_=sm[:], func=AF.Exp), ['sm'], ['ex'])
        T.op('dve', lambda e: e.scalar_tensor_tensor(out=neglam[:], in0=ex[:, 1:2], scalar=-0.2, in1=ex[:, 0:1],
                                                     op0=ALU.add, op1=ALU.subtract), ['ex'], ['neglam'])
        sk = T.alloc([128, 8], F32)
        T.dma('sp', sk[:], sinkc.ap(), writes=['sk'])
        T.op('act', lambda e: e.activation(out=esink[:], in_=sk[:], func=AF.Exp), ['sk'], ['esink'])

    def rmsnorm_mod(xT, xu, N, li, n, who, hT, hu, tmp, tu, sq, squ, rstd, ru, psb):
        T.op('act', lambda e: e.activation(out=sq[:, :, :N], in_=xT[:, :, :N], func=AF.Square), [xu], [squ])
        for j in range(8):
            T.op('pe', lambda e, j=j: e.matmul(PS[psb][:, :N], lhsT=onesb[:], rhs=sq[:, j, :N],
                                               start=(j == 0), stop=(j == 7)), [squ, 'onesb'], [psu(psb)])
        T.op('act', lambda e: e.activation(out=rstd[:, :N], in_=PS[psb][:, :N], func=AF.Sqrt, bias=EPS,
                                           scale=1.0 / D), [psu(psb)], [ru])
        T.op('dve', lambda e: e.reciprocal(out=rstd[:, :N], in_=rstd[:, :N]), [ru], [ru])
        for j in range(8):
            if who is None:
                a_ap = fgc[:, j:j + 1]
            else:
                a_ap = Acol[:, li, n, j, who:who + 1]
            T.op('dve', lambda e, j=j, a_ap=a_ap: e.scalar_tensor_tensor(
                out=tmp[:, j, :N], in0=xT[:, j, :N], scalar=a_ap, in1=rstd[:, :N], op0=ALU.mult, op1=ALU.mult),
                [xu, ru, 'Acol', 'fgc'], [tu])
            if who is not None:
                b_ap = modT[:, li, (3 * n) * 8 + j, who:who + 1]
                T.op('act', lambda e, j=j, b_ap=b_ap: e.activation(
                    out=hT[:, j, :N], in_=tmp[:, j, :N], func=AF.Identity, bias=b_ap, scale=1.0),
                    [tu, 'modT'], [hu])

    def phase1():
        W = T.alloc([128, 8, 3968], BF16, 'w0')
        for kc in range(8):
            T.dma('pool', W[:, kc, :], w0.ap()[kc * 128:(kc + 1) * 128, :], writes=[('W', kc)])
        Wu = [('W', kc) for kc in range(8)]
        xtok = [T.alloc([128, 4, D], F32) for _ in range(2)]
        xT = [T.alloc([128, 8, 512], F32) for _ in range(2)]
        hT = [T.alloc([128, 8, 512], BF16) for _ in range(2)]
        tmp = T.alloc([128, 8, 512], F32)
        sq = T.alloc([128, 8, 512], BF16)
        rstd = T.alloc([128, 512], F32)
        rc = [T.alloc([128, 512], F32) for _ in range(2)]
        rs = [T.alloc([128, 512], F32) for _ in range(2)]
        t1 = [T.alloc([128, 512], F32) for _ in range(2)]
        t2 = [T.alloc([128, 512], F32) for _ in range(2)]
        ob = [T.alloc([128, 512], BF16) for _ in range(4)]
        vb = [T.alloc([128, 512], BF16) for _ in range(2)]
        nrope = 0
        nv = 0
        def prep(ci):
            st, N, isctx = CHUNKS[ci]
            b = ci % 2
            who = 1 if isctx else 0
            ntt = N // 128
            for tt in range(ntt):
                T.dma('sp', xtok[b][:, tt, :], xall.ap()[st + tt * 128: st + (tt + 1) * 128, :],
                      writes=[('xtok', b, tt)])
            T.dma('sp', rc[b][:, :N], ropeC.ap()[:, st:st + N], writes=[('rc', b)])
            T.dma('sp', rs[b][:, :N], ropeS.ap()[:, st:st + N], writes=[('rs', b)])
            for j in range(8):
                pb = j % 2
                for tt in range(ntt):
                    T.op('pe', lambda e, j=j, tt=tt, pb=pb, b=b: e.transpose(
                        PS[pb][:, tt * 128:(tt + 1) * 128], xtok[b][:, tt, j * 128:(j + 1) * 128], ident[:]),
                        [('xtok', b, tt), 'ident'], [psu(pb)])
                T.op('act' if j % 2 else 'dve', (lambda e, j=j, pb=pb, b=b, N=N: e.activation(
                    out=xT[b][:, j, :N], in_=PS[pb][:, :N], func=AF.Copy)) if j % 2 else
                    (lambda e, j=j, pb=pb, b=b, N=N: e.tensor_copy(out=xT[b][:, j, :N], in_=PS[pb][:, :N])),
                    [psu(pb)], [('xT', b)])
            T.dma('sp', R.ap()[:, :, st:st + N], xT[b][:, :, :N], reads=[('xT', b)], writes=[('R', ci)])
            rmsnorm_mod(xT[b], ('xT', b), N, 0, 0, who, hT[b], ('hT', b), tmp, 'tmp', sq, 'sq', rstd, 'rstd', 2)

        def proj(ci):
            nonlocal nrope, nv
            st, N, isctx = CHUNKS[ci]
            b = ci % 2
            ntt = N // 128
            specs = []
            for h in range(4):
                specs.append((h, 4 + h, QAT.ap()[h]))
            for h in range(4):
                specs.append((8 + h, 12 + h, KAT.ap()[h]))
            for g in range(4):
                specs.append((16 + g, 20 + g, QBT.ap()[g]))
            specs.append((24, 25, KBT.ap()))
            for (bp, bq, dst) in specs:
                pp = 3 + 2 * (nrope % 2)
                k2 = nrope % 2
                k4 = nrope % 4
                nrope += 1
                for (blk, pbk) in ((bp, pp), (bq, pp + 1)):
                    for kc in range(8):
                        T.op('pe', lambda e, blk=blk, pbk=pbk, kc=kc, b=b, N=N: e.matmul(
                            PS[pbk][:, :N], lhsT=W[:, kc, blk * 128:(blk + 1) * 128], rhs=hT[b][:, kc, :N],
                            start=(kc == 0), stop=(kc == 7)), [('hT', b), Wu[kc]], [psu(pbk)])
                T.op('dve', lambda e, pp=pp, k2=k2, b=b, N=N: e.tensor_tensor(
                    out=t1[k2][:, :N], in0=PS[pp][:, :N], in1=rc[b][:, :N], op=ALU.mult),
                    [psu(pp), ('rc', b)], [('t1', k2)])
                T.op('dve', lambda e, pp=pp, k2=k2, b=b, N=N: e.tensor_tensor(
                    out=t2[k2][:, :N], in0=PS[pp + 1][:, :N], in1=rs[b][:, :N], op=ALU.mult),
                    [psu(pp + 1), ('rs', b)], [('t2', k2)])
                T.op('dve', lambda e, k2=k2, k4=k4, N=N: e.tensor_tensor(
                    out=ob[k4][:, :N], in0=t1[k2][:, :N], in1=t2[k2][:, :N], op=ALU.add),
                    [('t1', k2), ('t2', k2)], [('ob', k4)])
                T.dma('sp', dst[:, st:st + N], ob[k4][:, :N], reads=[('ob', k4)], writes=[('proj', ci)])
            for tt in range(ntt):
                k2 = nv % 2
                nv += 1
                for kc in range(8):
                    T.op('pe', lambda e, kc=kc, tt=tt, b=b: e.matmul(
                        PS[7][:, :], lhsT=hT[b][:, kc, tt * 128:(tt + 1) * 128], rhs=W[:, kc, 3328:3840],
                        start=(kc == 0), stop=(kc == 7)), [('hT', b), Wu[kc]], [psu(7)])
                T.op('act', lambda e, k2=k2: e.activation(out=vb[k2][:, :], in_=PS[7][:, :], func=AF.Copy),
                     [psu(7)], [('vb', k2)])
                T.dma('sp', VA.ap()[st + tt * 128: st + (tt + 1) * 128, :], vb[k2][:, :],
                      reads=[('vb', k2)], writes=[('proj', ci)])
                k2 = nv % 2
                nv += 1
                for kc in range(8):
                    T.op('pe', lambda e, kc=kc, tt=tt, b=b: e.matmul(
                        PS[7][:, 0:128], lhsT=hT[b][:, kc, tt * 128:(tt + 1) * 128], rhs=W[:, kc, 3840:3968],
                        start=(kc == 0), stop=(kc == 7)), [('hT', b), Wu[kc]], [psu(7)])
                T.op('act', lambda e, k2=k2: e.activation(out=vb[k2][:, 0:128], in_=PS[7][:, 0:128], func=AF.Copy),
                     [psu(7)], [('vb', k2)])
                T.dma('sp', VB.ap()[st + tt * 128: st + (tt + 1) * 128, :], vb[k2][:, 0:128],
                      reads=[('vb', k2)], writes=[('proj', ci)])

        prep(0)
        for ci in range(len(CHUNKS)):
            if ci + 1 < len(CHUNKS):
                prep(ci + 1)
            proj(ci)

    ALLPROJ = [('proj', ci) for ci in range(len(CHUNKS))]

    def phase2a():
        VAs = T.alloc([128, 66, 512], BF16, 'VAs')
        for t0 in range(0, 66, 11):
            T.dma('sp', VAs[:, t0:t0 + 11, :],
                  VA.ap()[t0 * 128:(t0 + 11) * 128, :].rearrange("(t p) c -> p t c", p=128),
                  reads=ALLPROJ, writes=[('VAs', t0)])
        VAu = [('VAs', t0) for t0 in range(0, 66, 11)]
        onesf = T.alloc([128, 128], F32)
        T.op('pool', lambda e: e.memset(onesf[:], 1.0), [], ['onesf'])
        Ks = [T.alloc([128, NT], BF16) for _ in range(2)]
        Qs = [T.alloc([128, 512], BF16) for _ in range(2)]
        NPB = 6
        Pb = [T.alloc([128, 2, 512], BF16) for _ in range(NPB)]
        acc = T.alloc([128, 2, 512], F32)
        rD = T.alloc([128, 2, 512], F32)
        rD2 = T.alloc([128, 2, 512], F32)
        u_ = T.alloc([128, 2, 512], F32)
        o_ = T.alloc([128, 512], F32)
        sqo = T.alloc([128, 512], BF16)
        rr = T.alloc([128, 512], F32)
        ao = [T.alloc([128, 512], BF16) for _ in range(2)]
        st8 = {'nq': 0, 'ns': 0}
        pending = []

        def tick():
            for p_ in pending:
                p_[0] -= 1
            while pending and pending[0][0] <= 0:
                pending.pop(0)[1]()

        def flush():
            while pending:
                pending.pop(0)[1]()

        asteps = [(h, ci) for h in range(4) for ci in range(len(CHUNKS))]

        def loadQ(i):
            h_, ci_ = asteps[i]
            st_, N_, _ = CHUNKS[ci_]
            T.dma('sp', Qs[i % 2][:, :N_], QAT.ap()[h_][:, st_:st_ + N_], reads=ALLPROJ, writes=[('Qs', i % 2)])

        def loadK(h_):
            T.dma('sp', Ks[h_ % 2][:, :], KAT.ap()[h_], reads=ALLPROJ, writes=[('Ks', h_ % 2)])

        loadK(0)
        loadQ(0)
        for h in range(4):
            kb = h % 2
            for ci, (st, N, isctx) in enumerate(CHUNKS):
                qb = st8['nq'] % 2
                st8['nq'] += 1
                if st8['nq'] < len(asteps):
                    loadQ(st8['nq'])
                if ci == 1 and h + 1 < 4:
                    loadK(h + 1)
                kts = [0, 1] if isctx else list(range(66))
                dve_k = [kt for kt in kts if kt % 2 == 0]
                pe_k = [kt for kt in kts if kt % 2 == 1]

                def qk(kt, sb):
                    for m in range(2):
                        T.op('pe', lambda e, m=m: e.matmul(
                            P2[sb][:, m * 512:m * 512 + N], lhsT=Ks[kb][m * 64:(m + 1) * 64, kt * 128:(kt + 1) * 128],
                            rhs=Qs[qb][m * 64:(m + 1) * 64, :N], start=True, stop=True),
                            [('Ks', kb), ('Qs', qb)], [('p2', sb)])

                qk(kts[0], st8['ns'] % 2)
                for i, kt in enumerate(kts):
                    first, last = kt == kts[0], kt == kts[-1]
                    sb = st8['ns'] % 2
                    pbf = st8['ns'] % NPB
                    st8['ns'] += 1
                    tick()
                    if i + 1 < len(kts):
                        qk(kts[i + 1], st8['ns'] % 2)
                    T.op('act', lambda e: e.activation(
                        out=Pb[pbf][:, :, :N], in_=P2[sb][:, :].rearrange("p (m q) -> p m q", m=2)[:, :, :N],
                        func=AF.Exp, scale=0.125), [('p2', sb)], [('Pb', pbf)])
                    for m in range(2):
                        T.op('pe', lambda e, m=m: e.matmul(
                            PS[4 + m][:, :N], lhsT=VAs[:, kt, h * 128:(h + 1) * 128], rhs=Pb[pbf][:, m, :N],
                            start=first, stop=last), [('Pb', pbf)] + VAu, [psu(4 + m)])
                    if kt % 2 == 1:
                        for m in range(2):
                            T.op('pe', lambda e, m=m: e.matmul(
                                PS[6 + m][:, :N], lhsT=onesb[:], rhs=Pb[pbf][:, m, :N],
                                start=(kt == pe_k[0]), stop=(kt == pe_k[-1])), [('Pb', pbf), 'onesb'], [psu(6 + m)])
                    elif kt == dve_k[0]:
                        T.op('dve', lambda e: e.tensor_copy(out=acc[:, :, :N], in_=Pb[pbf][:, :, :N]),
                             [('Pb', pbf)], ['acc'])
                    else:
                        T.op('dve', lambda e: e.tensor_tensor(
                            out=acc[:, :, :N], in0=acc[:, :, :N], in1=Pb[pbf][:, :, :N], op=ALU.add),
                            [('Pb', pbf), 'acc'], ['acc'])
                flush()
                T.op('dve', lambda e: e.tensor_copy(out=u_[:, 0, :N], in_=PS[4][:, :N]), [psu(4)], [('u_', 0)])
                T.op('act', lambda e: e.activation(out=u_[:, 1, :N], in_=PS[5][:, :N], func=AF.Copy),
                     [psu(5)], [('u_', 1)])
                T.op('act', lambda e: e.activation(
                    out=rD[:, :, :N], in_=P2[3][:, :].rearrange("p (m q) -> p m q", m=2)[:, :, :N], func=AF.Copy),
                    [psu(6), psu(7)], ['rD'])
                fsb = st8['ns'] % 2
                st8['ns'] += 1
                for m in range(2):
                    T.op('pe', lambda e, m=m: e.matmul(P2[fsb][:, m * 512:m * 512 + N], lhsT=onesf[:],
                                                       rhs=acc[:, m, :N], start=True, stop=True),
                         ['acc', 'onesf'], [('p2', fsb)])
                T.op('act', lambda e: e.activation(
                    out=rD2[:, :, :N], in_=P2[fsb][:, :].rearrange("p (m q) -> p m q", m=2)[:, :, :N], func=AF.Copy),
                    [('p2', fsb)], ['rD2'])

                def stageB(N=N):
                    T.op('dve', lambda e: e.tensor_tensor(out=rD[:, :, :N], in0=rD[:, :, :N], in1=rD2[:, :, :N],
                                                          op=ALU.add), ['rD', 'rD2'], ['rD'])
                    T.op('dve', lambda e: e.reciprocal(out=rD[:, :, :N], in_=rD[:, :, :N]), ['rD'], ['rD'])
                    for m in range(2):
                        T.op('dve', lambda e, m=m: e.tensor_tensor(
                            out=u_[:, m, :N], in0=u_[:, m, :N], in1=rD[:, m, :N], op=ALU.mult),
                            [('u_', m), 'rD'], [('u_', m)])
                    T.op('dve', lambda e: e.scalar_tensor_tensor(
                        out=o_[:, :N], in0=u_[:, 1, :N], scalar=neglam[:, 0:1], in1=u_[:, 0, :N],
                        op0=ALU.mult, op1=ALU.add), [('u_', 0), ('u_', 1), 'neglam'], ['o_'])
                    T.op('dve', lambda e: e.tensor_tensor(out=sqo[:, :N], in0=o_[:, :N], in1=o_[:, :N], op=ALU.mult),
                         ['o_'], ['sqo'])

                def stageC(N=N):
                    f2 = st8['ns'] % 2
                    T.op('pe', lambda e: e.matmul(P2[f2][:, :N], lhsT=onesb[:], rhs=sqo[:, :N], start=True, stop=True),
                         ['sqo', 'onesb'], [('p2', f2)])
                    T.op('act', lambda e: e.activation(out=rr[:, :N], in_=P2[f2][:, :N], func=AF.Sqrt, bias=EPS,
                                                       scale=1.0 / 128), [('p2', f2)], ['rr'])

                def stageD(N=N, h=h, st=st, ci=ci):
                    T.op('dve', lambda e: e.reciprocal(out=rr[:, :N], in_=rr[:, :N]), ['rr'], ['rr'])
                    ab_ = st8['nq'] % 2
                    T.op('dve', lambda e: e.scalar_tensor_tensor(
                        out=ao[ab_][:, :N], in0=o_[:, :N], scalar=0.8, in1=rr[:, :N], op0=ALU.mult, op1=ALU.mult),
                        ['o_', 'rr'], [('ao', ab_)])
                    T.dma('sp', ATd.ap()[h][:, st:st + N], ao[ab_][:, :N], reads=[('ao', ab_)], writes=[('AT', ci)])

                pending.append([3, stageB])
                pending.append([10, stageC])
                pending.append([16, stageD])
        flush()

    def phase2b():
        Kb = T.alloc([128, NT], BF16, 'Kb')
        Vb = T.alloc([128, 66, 128], BF16, 'Vb')
        mk = T.alloc([128, 2, 512], BF16, 'mk')
        T.dma('sp', Kb[:, :], KBT.ap(), reads=ALLPROJ, writes=['Kb'])
        T.dma('sp', Vb[:, :, :], VB.ap().rearrange("(t p) c -> p t c", p=128), reads=ALLPROJ, writes=['Vb'])
        T.dma('sp', mk[:, 0, :], maskb.ap()[0], writes=['mk'])
        T.dma('sp', mk[:, 1, :], maskb.ap()[1], writes=['mk'])
        Qc = [T.alloc([128, 4, 512], BF16) for _ in range(2)]
        Pb = [T.alloc([128, 512], BF16) for _ in range(4)]
        dt_ = [T.alloc([64, 512], F32) for _ in range(2)]
        obw = [T.alloc([64, 4, 128], BF16) for _ in range(2)]
        nb = 0
        units = []
        def loadq(ci):
            st, N, isctx = CHUNKS[ci]
            T.dma('sp', Qc[ci % 2][:, :, :N], QBT.ap().rearrange("g p t -> p g t")[:, :, st:st + N],
                  reads=ALLPROJ, writes=[('Qc', ci % 2)])

        for ci, (st, N, isctx) in enumerate(CHUNKS):
            qb = ci % 2
            for bi in range(N // 128):
                if isctx:
                    tiles = [(0, None), (1, None)]
                else:
                    i = (st - 256) // 128 + bi
                    tiles = [(0, None), (1, None)]
                    if i - 1 >= 0:
                        tiles.append((2 + i - 1, 0))
                    tiles.append((2 + i, None))
                    if i + 1 < 64:
                        tiles.append((2 + i + 1, 1))
                for kvh in range(2):
                    for ti, (kt, mi) in enumerate(tiles):
                        units.append(dict(qb=qb, bi=bi, kvh=kvh, kt=kt, mi=mi, first=(ti == 0),
                                          last=(ti == len(tiles) - 1), st=st, ci=ci))

        def qk(u, sb):
            T.op('pe', lambda e: e.matmul(
                PS[sb][:, :].rearrange("p (g q) -> p g q", g=4),
                lhsT=Kb[u['kvh'] * 64:(u['kvh'] + 1) * 64, u['kt'] * 128:(u['kt'] + 1) * 128],
                rhs=Qc[u['qb']][u['kvh'] * 64:(u['kvh'] + 1) * 64, :, u['bi'] * 128:(u['bi'] + 1) * 128],
                start=True, stop=(u['mi'] is None)), ['Kb', ('Qc', u['qb'])], [psu(sb)])
            if u['mi'] is not None:
                T.op('pe', lambda e: e.matmul(
                    PS[sb][:, :], lhsT=identb[:], rhs=mk[:, u['mi'], :], start=False, stop=True),
                    ['identb', 'mk'], [psu(sb)])

        loadq(0)
        qk(units[0], 0)
        seen = set()
        for ui, u in enumerate(units):
            if u['ci'] not in seen:
                seen.add(u['ci'])
                if u['ci'] + 1 < len(CHUNKS):
                    loadq(u['ci'] + 1)
            sb = ui % 3
            pbf = ui % 4
            kvh = u['kvh']
            if ui + 1 < len(units):
                qk(units[ui + 1], (ui + 1) % 3)
            T.op('act', lambda e: e.activation(
                out=Pb[pbf][:, :], in_=PS[sb][:, :], func=AF.Exp, scale=0.125),
                [psu(sb)], [('Pb', pbf)])
            T.op('pe', lambda e: e.matmul(
                PS[3 + kvh][0:64, :], lhsT=Vb[:, u['kt'], kvh * 64:(kvh + 1) * 64], rhs=Pb[pbf][:, :],
                start=u['first'], stop=u['last']), ['Vb', ('Pb', pbf)], [psu(3 + kvh)])
            T.op('pe', lambda e: e.matmul(
                PS[5 + kvh][0:64, :], lhsT=onesb[:, 0:64], rhs=Pb[pbf][:, :],
                start=u['first'], stop=u['last']), ['onesb', ('Pb', pbf)], [psu(5 + kvh)])
            if u['last']:
                for g in range(4):
                    T.op('dve', lambda e, g=g: e.tensor_scalar(
                        out=dt_[kvh][:, g * 128:(g + 1) * 128], in0=PS[5 + kvh][0:64, g * 128:(g + 1) * 128],
                        scalar1=esink[0:64, kvh * 4 + g:kvh * 4 + g + 1], scalar2=None, op0=ALU.add),
                        [psu(5 + kvh), 'esink'], [('dt_', kvh)])
                T.op('dve', lambda e: e.reciprocal(out=dt_[kvh][:, :], in_=dt_[kvh][:, :]), [('dt_', kvh)],
                     [('dt_', kvh)])
                ob_ = nb % 2
                nb += 1
                T.op('dve', lambda e: e.tensor_tensor(
                    out=obw[ob_][:, :, :].rearrange("p g q -> p (g q)"), in0=PS[3 + kvh][0:64, :],
                    in1=dt_[kvh][:, :], op=ALU.mult), [psu(3 + kvh), ('dt_', kvh)], [('obw', ob_)])
                q0 = u['st'] + u['bi'] * 128
                T.dma('sp', BTd.ap()[kvh * 4:(kvh + 1) * 4].rearrange("g p t -> p g t")[:, :, q0:q0 + 128],
                      obw[ob_][:, :, :], reads=[('obw', ob_)], writes=[('BT', u['ci'])])

    def phase3a(li, srcs, wloads, chunk_ids):
        Wt = []
        for wi, (wap, nb_, P_) in enumerate(wloads):
            wt = T.alloc([P_, nb_, D], BF16)
            T.dma('pool', wt[:, :, :], wap, writes=[('Wo', wi)])
            Wt.append(wt)
        xT = [T.alloc([128, 8, 512], F32) for _ in range(3)]
        hT = [T.alloc([128, 8, 512], BF16) for _ in range(2)]
        mx = [[T.alloc([P_, nb_, 512], BF16) for (_, nb_, P_) in srcs] for _ in range(3)]
        tmp = T.alloc([128, 8, 512], F32)
        sq = T.alloc([128, 8, 512], BF16)
        rstd = T.alloc([128, 512], F32)
        npb = 0

        def partL(ci):
            st, N, isctx = CHUNKS[ci]
            b = ci % 3
            T.dma('sp', xT[b][:, :, :N], R.ap()[:, :, st:st + N], reads=[('R', ci)], writes=[('xT', b)])
            for si, (sap, nb_, P_) in enumerate(srcs):
                T.dma('sp', mx[b][si][:, :, :N], sap.rearrange("g p t -> p g t")[:, :, st:st + N],
                      reads=[('AT', ci), ('BT', ci), ('YT', ci)], writes=[('mx', b, si)])

        def partA(ci):
            nonlocal npb
            st, N, isctx = CHUNKS[ci]
            b = ci % 3
            who = 1 if isctx else 0
            nmm = sum(nb_ for (_, nb_, _) in srcs)
            for j in range(8):
                pb = npb % 2
                npb += 1
                k = 0
                for si, (sap, nb_, P_) in enumerate(srcs):
                    for q in range(nb_):
                        T.op('pe', lambda e, si=si, q=q, j=j, pb=pb, b=b, N=N, k=k, P_=P_: e.matmul(
                            PS[pb][:, :N], lhsT=Wt[si][0:P_, q, j * 128:(j + 1) * 128], rhs=mx[b][si][0:P_, q, :N],
                            start=(k == 0), stop=(k == nmm - 1)), [('Wo', si), ('mx', b, si)], [psu(pb)])
                        k += 1
                g_ap = modT[:, li, 2 * 8 + j, who:who + 1]
                T.op('dve', lambda e, j=j, pb=pb, b=b, N=N, g_ap=g_ap: e.scalar_tensor_tensor(
                    out=xT[b][:, j, :N], in0=PS[pb][:, :N], scalar=g_ap, in1=xT[b][:, j, :N],
                    op0=ALU.mult, op1=ALU.add), [psu(pb), ('xT', b), 'modT'], [('xT', b)])
            T.dma('sp', R.ap()[:, :, st:st + N], xT[b][:, :, :N], reads=[('xT', b)], writes=[('R', ci)])

        def partB(ci):
            st, N, isctx = CHUNKS[ci]
            b = ci % 3
            hb = ci % 2
            who = 1 if isctx else 0
            rmsnorm_mod(xT[b], ('xT', b), N, li, 1, who, hT[hb], ('hT', hb), tmp, 'tmp', sq, 'sq', rstd, 'rstd', 2)
            T.dma('sp', H2.ap()[:, :, st:st + N], hT[hb][:, :, :N], reads=[('hT', hb)], writes=[('H2', ci)])

        nC = len(chunk_ids)
        partL(chunk_ids[0])
        if nC > 1:
            partL(chunk_ids[1])
        partA(chunk_ids[0])
        for i_, ci in enumerate(chunk_ids):
            if i_ + 2 < nC:
                partL(chunk_ids[i_ + 2])
            if i_ + 1 < nC:
                partA(chunk_ids[i_ + 1])
            partB(ci)

    def phase3b(li, chunk_ids, after_loads=None):
        Wi = T.alloc([128, 8, 2 * DFF], BF16, 'Wi')
        for kc in range(8):
            T.dma('pool', Wi[:, kc, :], ffn_in.ap()[li, kc * 128:(kc + 1) * 128, :], writes=[('Wi', kc)])
        if after_loads is not None:
            after_loads()
        hT = [T.alloc([128, 8, 512], BF16) for _ in range(2)]
        sg = [T.alloc([128, 512], F32) for _ in range(2)]
        ac = [T.alloc([128, 512], BF16) for _ in range(3)]
        n = 0
        for ci in chunk_ids:
            st, N, isctx = CHUNKS[ci]
            b = ci % 2
            T.dma('sp', hT[b][:, :, :N], H2.ap()[:, :, st:st + N], reads=[('H2', ci)], writes=[('hT', b)])
            for fb in range(22):
                pg = (n % 4) * 2
                s2 = n % 2
                a3 = n % 3
                n += 1
                for (col0, pbk) in ((fb * 128, pg), (DFF + fb * 128, pg + 1)):
                    for kc in range(8):
                        T.op('pe', lambda e, col0=col0, pbk=pbk, kc=kc, b=b, N=N: e.matmul(
                            PS[pbk][:, :N], lhsT=Wi[:, kc, col0:col0 + 128], rhs=hT[b][:, kc, :N],
                            start=(kc == 0), stop=(kc == 7)), [('hT', b), ('Wi', kc)], [psu(pbk)])
                T.op('act', lambda e, pg=pg, s2=s2, N=N: e.activation(out=sg[s2][:, :N], in_=PS[pg][:, :N],
                                                                     func=AF.Silu), [psu(pg)], [('sg', s2)])
                T.op('dve', lambda e, pg=pg, s2=s2, a3=a3, N=N: e.tensor_tensor(
                    out=ac[a3][:, :N], in0=PS[pg + 1][:, :N], in1=sg[s2][:, :N], op=ALU.mult),
                    [psu(pg + 1), ('sg', s2)], [('ac', a3)])
                T.dma('sp', ACTd.ap()[fb][:, st:st + N], ac[a3][:, :N], reads=[('ac', a3)], writes=[('ACT', ci)])

    def load_Wo(li):
        Wo = T.alloc([128, 22, D], BF16, 'Wo2')
        for f0 in range(0, 22, 2):
            T.dma('pool', Wo[:, f0:f0 + 2, :],
                  ffn_out.ap()[li, f0 * 128:(f0 + 2) * 128, :].rearrange("(f p) d -> p f d", p=128),
                  writes=[('Wo2', f0)])
        return Wo

    def phase3c(li, chunk_ids, final, Wo=None):
        if Wo is None:
            Wo = load_Wo(li)
        Wou = [('Wo2', f0) for f0 in range(0, 22, 2)]
        xT = [T.alloc([128, 8, 512], F32) for _ in range(2)]
        ac = [T.alloc([128, 22, 512], BF16) for _ in range(2)]
        if final:
            tmp = T.alloc([128, 8, 512], F32)
            sq = T.alloc([128, 8, 512], BF16)
            rstd = T.alloc([128, 512], F32)
            otok = [T.alloc([128, D], F32) for _ in range(2)]
        npb = 0
        no = 0
        for ci in chunk_ids:
            st, N, isctx = CHUNKS[ci]
            b = ci % 2
            who = 1 if isctx else 0
            T.dma('sp', xT[b][:, :, :N], R.ap()[:, :, st:st + N], reads=[('R', ci)], writes=[('xT', b)])
            T.dma('sp', ac[b][:, :, :N], ACTd.ap().rearrange("f p t -> p f t")[:, :, st:st + N],
                  reads=[('ACT', ci)], writes=[('acl', b)])
            for j in range(8):
                pb = npb % 2
                npb += 1
                for fb in range(22):
                    T.op('pe', lambda e, fb=fb, j=j, pb=pb, b=b, N=N: e.matmul(
                        PS[pb][:, :N], lhsT=Wo[:, fb, j * 128:(j + 1) * 128], rhs=ac[b][:, fb, :N],
                        start=(fb == 0), stop=(fb == 21)), [('acl', b)] + Wou, [psu(pb)])
                g_ap = modT[:, li, 5 * 8 + j, who:who + 1]
                T.op('dve', lambda e, j=j, pb=pb, b=b, N=N, g_ap=g_ap: e.scalar_tensor_tensor(
                    out=xT[b][:, j, :N], in0=PS[pb][:, :N], scalar=g_ap, in1=xT[b][:, j, :N],
                    op0=ALU.mult, op1=ALU.add), [psu(pb), ('xT', b), 'modT'], [('xT', b)])
            if not final:
                T.dma('sp', R.ap()[:, :, st:st + N], xT[b][:, :, :N], reads=[('xT', b)], writes=[('R', ci)])
            else:
                rmsnorm_mod(xT[b], ('xT', b), N, li, 0, None, None, None, tmp, 'tmp', sq, 'sq', rstd, 'rstd', 2)
                for tt in range(N // 128):
                    o2 = no % 2
                    no += 1
                    for j in range(8):
                        pbk = 3 + (j // 4) + 2 * o2
                        T.op('pe', lambda e, j=j, tt=tt, pbk=pbk: e.transpose(
                            PS[pbk][:, (j % 4) * 128:(j % 4 + 1) * 128], tmp[:, j, tt * 128:(tt + 1) * 128],
                            ident[:]), ['tmp', 'ident'], [psu(pbk)])
                    for hf in range(2):
                        pbk = 3 + hf + 2 * o2
                        T.op('act' if hf else 'dve', (lambda e, pbk=pbk, o2=o2, hf=hf: e.activation(
                            out=otok[o2][:, hf * 512:(hf + 1) * 512], in_=PS[pbk][:, :], func=AF.Copy)) if hf else
                            (lambda e, pbk=pbk, o2=o2, hf=hf: e.tensor_copy(
                                out=otok[o2][:, hf * 512:(hf + 1) * 512], in_=PS[pbk][:, :])),
                            [psu(pbk)], [('otok', o2, hf)])
                    r0_ = st - 256 + tt * 128
                    T.dma('sp', out.ap()[r0_:r0_ + 128, :], otok[o2][:, :],
                          reads=[('otok', o2, 0), ('otok', o2, 1)], writes=[('out', ci, tt)])

    def load_W1():
        W = T.alloc([128, 8, 2592], BF16, 'w1')
        for kc in range(8):
            T.dma('pool', W[:, kc, :], w1.ap()[kc * 128:(kc + 1) * 128, :], writes=[('W', kc)])
        return W

    def phaseP1(W=None):
        if W is None:
            W = load_W1()
        Wu = [('W', kc) for kc in range(8)]
        gw = T.alloc([16, 2, 256], BF16)
        T.dma('pool', gw[:, :, :], gw_in.ap().rearrange("n r k -> r n k"), writes=['gw'])
        gb2 = T.alloc([64, 2, 512], F32)
        T.dma('sp', gb2[:, :, :], gb2_in.ap().rearrange("n p k -> p n k"), writes=['gb2'])
        tri = T.alloc([64, 4, 64], F32)
        T.dma('sp', tri[:, :, :], tri_in.ap().rearrange("n p k -> p n k"), writes=['tri'])
        xT2 = [T.alloc([128, 8, 512], F32) for _ in range(2)]
        hT2 = [T.alloc([128, 8, 512], BF16) for _ in range(2)]
        tmp = T.alloc([128, 8, 512], F32)
        sq = T.alloc([128, 8, 512], BF16)
        rstd = T.alloc([128, 512], F32)
        qf = T.alloc([64, 4, 512], F32)
        kf = T.alloc([64, 4, 512], F32)
        lrs = T.alloc([16, 2, 512], BF16)
        vt = T.alloc([64, 8, 512], BF16)
        ktf = T.alloc([64, 8, 256], F32)
        zz = T.alloc([64, 8, 256], F32)
        eb = [T.alloc([64, 512], F32) for _ in range(2)]
        enb = [T.alloc([64, 512], F32) for _ in range(2)]
        so = [T.alloc([128, 512], BF16) for _ in range(4)]
        sf = [T.alloc([128, 512], F32) for _ in range(2)]
        g1 = T.alloc([128, 512], F32)
        g2 = T.alloc([128, 512], F32)
        kdt = T.alloc([64, 8, 256], BF16)
        cnt = {'so': 0, 'sf': 0, 'pb': 0, 'e': 0}

        def nxt(k, n):
            v = cnt[k] % n
            cnt[k] += 1
            return v

        cur = {}

        def fm_block(col0, M, N):
            pb = nxt('pb', 4)
            hT, hTu = cur['hT'], cur['hTu']
            for kc in range(8):
                T.op('pe', lambda e, kc=kc, pb=pb: e.matmul(
                    PS[pb][0:M, :N], lhsT=W[:, kc, col0:col0 + M], rhs=hT[:, kc, :N],
                    start=(kc == 0), stop=(kc == 7)), [hTu, Wu[kc]], [psu(pb)])
            return pb

        def prep(ci):
            st, N, isctx = CHUNKS[ci]
            who = 1 if isctx else 0
            b = ci % 2
            T.dma('sp', xT2[b][:, :, :N], R.ap()[:, :, st:st + N], reads=[('R', ci)], writes=[('xT', b)])
            rmsnorm_mod(xT2[b], ('xT', b), N, 1, 0, who, hT2[b], ('hT', b), tmp, 'tmp', sq, 'sq', rstd, 'rstd', 7)

        prep(0)
        for ci, (st, N, isctx) in enumerate(CHUNKS):
            who = 1 if isctx else 0
            nsub = N // 64
            sub0 = st // 64
            if ci + 1 < len(CHUNKS):
                prep(ci + 1)
            hT = hT2[ci % 2]
            hTu = ('hT', ci % 2)
            cur['hT'], cur['hTu'] = hT, hTu
            for blk in range(4):
                pb = fm_block(2080 + blk * 128, 128, N)
                k = nxt('sf', 2)
                T.op('act', lambda e, pb=pb, k=k: e.activation(out=sf[k][:, :N], in_=PS[pb][:, :N], func=AF.Copy),
                     [psu(pb)], [('sf', k)])
                T.dma('sp', ZX.ap()[blk][:, st:st + N], sf[k][:, :N], reads=[('sf', k)], writes=[('ZX', ci)])
            if not isctx:
                for blk in range(4):
                    pb = fm_block(1568 + blk * 128, 128, N)
                    k = nxt('sf', 2)
                    T.op('act', lambda e, pb=pb, k=k: e.activation(out=sf[k][:, :N], in_=PS[pb][:, :N], func=AF.Copy),
                         [psu(pb)], [('sf', k)])
                    T.op('dve', lambda e, k=k: e.tensor_tensor(out=g1[:, :N], in0=sf[k][:, :N], in1=sf[k][:, :N],
                                                               op=ALU.mult), [('sf', k)], ['g1'])
                    T.op('dve', lambda e: e.tensor_scalar(out=g1[:, :N], in0=g1[:, :N], scalar1=0.044715,
                                                           scalar2=1.0, op0=ALU.mult, op1=ALU.add), ['g1'], ['g1'])
                    T.op('dve', lambda e, k=k: e.tensor_tensor(out=g2[:, :N], in0=g1[:, :N], in1=sf[k][:, :N],
                                                               op=ALU.mult), ['g1', ('sf', k)], ['g2'])
                    T.op('act', lambda e: e.activation(out=g2[:, :N], in_=g2[:, :N], func=AF.Sigmoid,
                                                       scale=1.5957691216), ['g2'], ['g2'])
                    o4 = nxt('so', 4)
                    T.op('dve', lambda e, k=k, o4=o4: e.tensor_tensor(out=so[o4][:, :N], in0=sf[k][:, :N],
                                                                     in1=g2[:, :N], op=ALU.mult),
                         ['g2', ('sf', k)], [('so', o4)])
                    T.dma('sp', GZ.ap()[blk][:, st:st + N], so[o4][:, :N], reads=[('so', o4)], writes=[('GZ', ci)])
                for blk in range(4):
                    pb = fm_block(1024 + blk * 128, 128, N)
                    o4 = nxt('so', 4)
                    T.op('act', lambda e, pb=pb, o4=o4: e.activation(out=so[o4][:, :N], in_=PS[pb][:, :N],
                                                                    func=AF.Silu), [psu(pb)], [('so', o4)])
                    T.dma('sp', SG.ap()[blk][:, st:st + N], so[o4][:, :N], reads=[('so', o4)], writes=[('SG', ci)])
            for h in range(4):
                if not isctx:
                    pb = fm_block(h * 64, 64, N)
                    T.op('act', lambda e, pb=pb, h=h: e.activation(out=qf[:, h, :N], in_=PS[pb][0:64, :N],
                                                                  func=AF.Identity, scale=0.125),
                         [psu(pb)], [('qf', h)])
                    pb = fm_block(256 + h * 64, 64, N)
                    T.op('dve', lambda e, pb=pb, h=h: e.tensor_copy(out=kf[:, h, :N], in_=PS[pb][0:64, :N]),
                         [psu(pb)], [('kf', h)])
            for d in range(2):
                pb = fm_block(1536 + d * 16, 16, N)
                T.op('act', lambda e, pb=pb, d=d: e.activation(out=lrs[:, d, :N], in_=PS[pb][0:16, :N], func=AF.Copy),
                     [psu(pb)], [('lrs', d)])
            for s_ in range(nsub):
                pb = nxt('pb', 4)
                for kc in range(8):
                    T.op('pe', lambda e, kc=kc, pb=pb, s_=s_: e.matmul(
                        PS[pb][0:64, :], lhsT=hT[:, kc, s_ * 64:(s_ + 1) * 64], rhs=W[:, kc, 512:1024],
                        start=(kc == 0), stop=(kc == 7)), [hTu, Wu[kc]], [psu(pb)])
                T.op('act', lambda e, pb=pb, s_=s_: e.activation(out=vt[:, s_, :], in_=PS[pb][0:64, :], func=AF.Copy),
                     [psu(pb)], ['vt'])
            T.dma('sp', V1.ap()[st:st + N, :].rearrange("(s p) c -> p s c", p=64), vt[:, :nsub, :],
                  reads=['vt'], writes=[('V1', ci)])
            for s2 in range(0, nsub, 2):
                pb = nxt('pb', 4)
                for s_ in (s2, s2 + 1):
                    for kc in range(8):
                        T.op('pe', lambda e, kc=kc, pb=pb, s_=s_, s2=s2: e.matmul(
                            PS[pb][0:64, (s_ - s2) * 256:(s_ - s2 + 1) * 256],
                            lhsT=hT[:, kc, s_ * 64:(s_ + 1) * 64], rhs=W[:, kc, 256:512],
                            start=(kc == 0), stop=(kc == 7)), [hTu, Wu[kc]], [psu(pb)])
                T.op('dve', lambda e, pb=pb, s2=s2: e.tensor_copy(
                    out=ktf[:, s2:s2 + 2, :].rearrange("p s c -> p (s c)"), in_=PS[pb][0:64, :]), [psu(pb)], ['ktf'])
            for d in range(2):
                for s2 in range(0, nsub, 2):
                    pb = nxt('pb', 4)
                    for s_ in (s2, s2 + 1):
                        T.op('pe', lambda e, pb=pb, s_=s_, s2=s2, d=d: e.matmul(
                            PS[pb][0:64, (s_ - s2) * 256:(s_ - s2 + 1) * 256],
                            lhsT=lrs[:, d, s_ * 64:(s_ + 1) * 64], rhs=gw[:, d, :], start=True, stop=True),
                            [('lrs', d), 'gw'], [psu(pb)])
                    T.op('dve', lambda e, pb=pb, s2=s2, d=d: e.tensor_tensor(
                        out=zz[:, s2:s2 + 2, :].rearrange("p s c -> p (s c)"), in0=PS[pb][0:64, :], in1=gb2[:, d, :],
                        op=ALU.add), [psu(pb), 'gb2'], ['zz'])
                T.op('act', lambda e: e.activation(out=zz[:, :nsub, :], in_=zz[:, :nsub, :], func=AF.Exp, scale=-1.0),
                     ['zz'], ['zz'])
                T.op('act', lambda e: e.activation(out=zz[:, :nsub, :], in_=zz[:, :nsub, :], func=AF.Ln, bias=1.0),
                     ['zz'], ['zz'])
                for s2 in range(0, nsub, 2):
                    pb = nxt('pb', 4)
                    for s_ in (s2, s2 + 1):
                        T.op('pe', lambda e, pb=pb, s_=s_, s2=s2, d=d: e.matmul(
                            PS[pb][0:64, (s_ - s2) * 256:(s_ - s2 + 1) * 256],
                            lhsT=tri[:, 2 + d, :], rhs=zz[:, s_, :], start=True, stop=True),
                            ['zz', 'tri'], [psu(pb)])
                    k = nxt('sf', 2)
                    T.op('act', lambda e, pb=pb, k=k: e.activation(out=sf[k][0:64, :], in_=PS[pb][0:64, :],
                                                                  func=AF.Exp, scale=-1.0 / 16), [psu(pb)], [('sf', k)])
                    T.op('dve', lambda e, k=k, s2=s2: e.tensor_tensor(
                        out=kdt[:, s2:s2 + 2, :].rearrange("p s c -> p (s c)"),
                        in0=ktf[:, s2:s2 + 2, :].rearrange("p s c -> p (s c)"), in1=sf[k][0:64, :], op=ALU.mult),
                        [('sf', k), 'ktf'], ['kdt'])
                T.dma('sp', KD.ap()[d, st:st + N, :].rearrange("(s p) c -> p s c", p=64), kdt[:, :nsub, :],
                      reads=['kdt'], writes=[('KD', d, ci)])
                for h in range(4):
                    pb = nxt('pb', 4)
                    for s_ in range(nsub):
                        T.op('pe', lambda e, pb=pb, s_=s_, h=h, d=d: e.matmul(
                            PS[pb][0:64, s_ * 64:(s_ + 1) * 64], lhsT=zz[:, s_, h * 64:(h + 1) * 64],
                            rhs=tri[:, d, :], start=True, stop=True), ['zz', 'tri'], [psu(pb)])
                    k = nxt('e', 2)
                    T.op('act', lambda e, pb=pb, k=k: e.activation(out=eb[k][:, :N], in_=PS[pb][0:64, :N],
                                                                  func=AF.Exp, scale=-1.0 / 16), [psu(pb)], [('eb', k)])
                    col = 63 if d == 0 else 0
                    T.op('pool', lambda e, k=k, d=d, h=h, col=col, sub0=sub0, nsub=nsub: e.tensor_copy(
                        out=decs[:, d, h, sub0:sub0 + nsub],
                        in_=eb[k][:, :N].rearrange("p (s c) -> p s c", c=64)[:, :, col]),
                        [('eb', k)], ['decs'])
                    if not isctx:
                        T.op('act', lambda e, pb=pb, k=k: e.activation(out=enb[k][:, :N], in_=PS[pb][0:64, :N],
                                                                      func=AF.Exp, scale=1.0 / 16),
                             [psu(pb)], [('enb', k)])
                        o4 = nxt('so', 4)
                        T.op('dve', lambda e, k=k, h=h, o4=o4: e.tensor_tensor(
                            out=so[o4][0:64, :N], in0=qf[:, h, :N], in1=eb[k][:, :N], op=ALU.mult),
                            [('qf', h), ('eb', k)], [('so', o4)])
                        T.dma('sp', QS.ap()[d, h][:, st:st + N], so[o4][0:64, :N], reads=[('so', o4)],
                              writes=[('QS', d, ci)])
                        o4 = nxt('so', 4)
                        T.op('dve', lambda e, k=k, h=h, o4=o4: e.tensor_tensor(
                            out=so[o4][0:64, :N], in0=kf[:, h, :N], in1=enb[k][:, :N], op=ALU.mult),
                            [('kf', h), ('enb', k)], [('so', o4)])
                        T.dma('sp', KS.ap()[d, h][:, st:st + N], so[o4][0:64, :N], reads=[('so', o4)],
                              writes=[('KS', d, ci)])

    def phaseG():
        glag = T.alloc([128, 1], F32)
        T.dma('sp', glag[:, :], glag_in.ap(), writes=['glag'])
        mask4 = T.alloc([64, 2, 256], F32)
        T.dma('sp', mask4[:, :, :], mask4_in.ap().rearrange("n p k -> p n k"), writes=['mask4'])
        S32 = T.alloc([64, 4, 128], F32)
        Sbf = [T.alloc([64, 4, 128], BF16) for _ in range(2)]
        QSs = [T.alloc([64, 4, 512], BF16) for _ in range(2)]
        KSs = [T.alloc([64, 4, 512], BF16) for _ in range(2)]
        KDs = [T.alloc([64, 8, 256], BF16) for _ in range(2)]
        V1s = [T.alloc([64, 8, 512], BF16) for _ in range(2)]
        ATm = [T.alloc([64, 256], BF16) for _ in range(2)]
        of_ = [T.alloc([128, 512], F32) for _ in range(2)]
        of4 = [T.alloc([128, 512], F32) for _ in range(4)]
        ogl4 = [T.alloc([128, 512], F32) for _ in range(4)]
        sgl4 = [[T.alloc([128, 512], BF16) for _ in range(4)] for _ in range(2)]
        sqo2 = [T.alloc([128, 512], BF16) for _ in range(2)]
        rr4 = [T.alloc([128, 512], F32) for _ in range(4)]
        yo4 = [T.alloc([128, 512], BF16) for _ in range(4)]
        pending = []

        def tick():
            for p_ in pending:
                p_[0] -= 1
            while pending and pending[0][0] <= 0:
                pending.pop(0)[1]()

        def flush():
            while pending:
                pending.pop(0)[1]()
        nchk = 0
        nb = 0
        nst = 0
        nat = 0
        nev = 0
        gsteps = []
        for d in range(2):
            for ci in [0] + (list(range(1, 17)) if d == 0 else list(range(16, 0, -1))):
                gsteps.append((d, ci))

        def loadG(i):
            d, ci = gsteps[i]
            b = i % 2
            st, N, isctx = CHUNKS[ci]
            nsub = N // 64
            if not isctx:
                T.dma('sp', QSs[b][:, :, :N], QS.ap()[d].rearrange("h p t -> p h t")[:, :, st:st + N],
                      reads=[('QS', d, ci)], writes=[('QSs', b)])
                T.dma('sp', KSs[b][:, :, :N], KS.ap()[d].rearrange("h p t -> p h t")[:, :, st:st + N],
                      reads=[('KS', d, ci)], writes=[('KSs', b)])
            T.dma('sp', KDs[b][:, :nsub, :], KD.ap()[d, st:st + N, :].rearrange("(s p) c -> p s c", p=64),
                  reads=[('KD', d, ci)], writes=[('KDs', b)])
            T.dma('sp', V1s[b][:, :nsub, :], V1.ap()[st:st + N, :].rearrange("(s p) c -> p s c", p=64),
                  reads=[('V1', ci)], writes=[('V1s', b)])

        for d in range(2):
            T.op('dve', lambda e: e.memset(S32[:, :, :], 0.0), [], ['S32'])
            T.op('dve', lambda e, k=nst % 2: e.memset(Sbf[k][:, :, :], 0.0), [], [('Sbf', nst % 2)])
            order = [0] + (list(range(1, 17)) if d == 0 else list(range(16, 0, -1)))
            for ci in order:
                st, N, isctx = CHUNKS[ci]
                nsub = N // 64
                b = nb % 2
                nb += 1
                if nb == 1:
                    loadG(0)
                if nb < len(gsteps):
                    loadG(nb)
                if d == 1 and not isctx:
                    cp_ = nchk % 2
                    nchk += 1
                    for h in range(4):
                        T.dma('sp', ogl4[h][:, :N], OG.ap()[h][:, st:st + N], reads=[('OG', ci)],
                              writes=[('ogl4', h)])
                        T.dma('sp', sgl4[cp_][h][:, :N], SG.ap()[h][:, st:st + N], reads=[('SG', ci)],
                              writes=[('sgl4', cp_, h)])
                subs = list(range(nsub)) if d == 0 else list(range(nsub - 1, -1, -1))
                for s_ in subs:
                    gs = st // 64 + s_
                    cur = nst % 2
                    nx = (nst + 1) % 2
                    nst += 1
                    c0 = s_ * 64
                    if not isctx:
                        tick()
                    for h in range(4):
                        T.op('pe', lambda e, h=h, b=b, s_=s_: e.matmul(
                            PS[1][0:64, h * 128:(h + 1) * 128], lhsT=KDs[b][:, s_, h * 64:(h + 1) * 64],
                            rhs=V1s[b][:, s_, h * 128:(h + 1) * 128], start=True, stop=True),
                            [('KDs', b), ('V1s', b)], [psu(1)])
                    if not isctx:
                        am = nat % 2
                        nat += 1
                        for h in range(4):
                            T.op('pe', lambda e, h=h, b=b, c0=c0: e.matmul(
                                PS[0][0:64, h * 64:(h + 1) * 64], lhsT=KSs[b][:, h, c0:c0 + 64],
                                rhs=QSs[b][:, h, c0:c0 + 64], start=True, stop=True),
                                [('KSs', b), ('QSs', b)], [psu(0)])
                        T.op('dve', lambda e, am=am, d=d: e.tensor_tensor(
                            out=ATm[am][:, :], in0=PS[0][0:64, 0:256], in1=mask4[:, d, :], op=ALU.mult),
                            [psu(0), 'mask4'], [('ATm', am)])
                        for h in range(4):
                            T.op('pe', lambda e, h=h, b=b, c0=c0, am=am, s_=s_: e.matmul(
                                PS[4 + h][:, c0:c0 + 64], lhsT=V1s[b][:, s_, h * 128:(h + 1) * 128],
                                rhs=ATm[am][:, h * 64:(h + 1) * 64], start=True, stop=False),
                                [('V1s', b), ('ATm', am)], [psu(4 + h)])
                            T.op('pe', lambda e, h=h, b=b, c0=c0, cur=cur: e.matmul(
                                PS[4 + h][:, c0:c0 + 64], lhsT=Sbf[cur][:, h, :], rhs=QSs[b][:, h, c0:c0 + 64],
                                start=False, stop=True), [('Sbf', cur), ('QSs', b)], [psu(4 + h)])
                    for h in range(4):
                        T.op('dve', lambda e, h=h, d=d, gs=gs: e.scalar_tensor_tensor(
                            out=S32[:, h, :], in0=S32[:, h, :], scalar=decs[:, d, h, gs:gs + 1],
                            in1=PS[1][0:64, h * 128:(h + 1) * 128], op0=ALU.mult, op1=ALU.add),
                            ['S32', psu(1), 'decs'], ['S32'])
                    T.op('act', lambda e, nx=nx: e.activation(out=Sbf[nx][:, :, :], in_=S32[:, :, :], func=AF.Copy),
                         ['S32'], [('Sbf', nx)])
                if isctx:
                    continue
                flush()
                for h in range(4):
                    k = nev % 2
                    nev += 1
                    if d == 0:
                        T.op('act', lambda e, h=h, k=k: e.activation(out=of_[k][:, :N], in_=PS[4 + h][:, :N],
                                                                    func=AF.Copy), [psu(4 + h)], [('of', k)])
                        T.dma('sp', OG.ap()[h][:, st:st + N], of_[k][:, :N], reads=[('of', k)], writes=[('OG', ci)])
                    else:
                        T.op('dve', lambda e, h=h: e.tensor_tensor(
                            out=of4[h][:, :N], in0=PS[4 + h][:, :N], in1=ogl4[h][:, :N], op=ALU.add),
                            [psu(4 + h), ('ogl4', h)], [('of4', h)])

                        def st2(h=h, N=N):
                            q2 = h % 2
                            T.op('act', lambda e: e.activation(out=sqo2[q2][:, :N], in_=of4[h][:, :N], func=AF.Square),
                                 [('of4', h)], [('sqo2', q2)])
                            T.op('pe', lambda e: e.matmul(PS[2 + q2][:, :N], lhsT=onesb[:], rhs=sqo2[q2][:, :N],
                                                          start=True, stop=True), [('sqo2', q2), 'onesb'], [psu(2 + q2)])
                            T.op('act', lambda e: e.activation(out=rr4[h][:, :N], in_=PS[2 + q2][:, :N], func=AF.Sqrt,
                                                               bias=EPS, scale=1.0 / 128), [psu(2 + q2)], [('rr4', h)])

                        def st3(h=h, N=N, st=st, ci=ci, cp_=cp_):
                            T.op('dve', lambda e: e.reciprocal(out=rr4[h][:, :N], in_=rr4[h][:, :N]), [('rr4', h)],
                                 [('rr4', h)])
                            T.op('dve', lambda e: e.scalar_tensor_tensor(
                                out=of4[h][:, :N], in0=of4[h][:, :N], scalar=glag[:, 0:1], in1=rr4[h][:, :N],
                                op0=ALU.mult, op1=ALU.mult), [('of4', h), ('rr4', h), 'glag'], [('of4', h)])
                            T.op('dve', lambda e: e.tensor_tensor(
                                out=yo4[h][:, :N], in0=of4[h][:, :N], in1=sgl4[cp_][h][:, :N], op=ALU.mult),
                                [('of4', h), ('sgl4', cp_, h)], [('yo4', h)])
                            T.dma('sp', YT.ap()[h][:, st:st + N], yo4[h][:, :N], reads=[('yo4', h)],
                                  writes=[('YT', ci)])

                        pending.append([1 + h, st2])
                        pending.append([3 + h, st3])
                        pending.sort(key=lambda p_: p_[0])
        flush()

    def phaseLr():
        cw = T.alloc([128, 4, 4], F32)
        cb = T.alloc([128, 4], F32)
        ba = T.alloc([128, 2, 4], F32)
        bx = T.alloc([128, 2, 4], F32)
        lam = T.alloc([128, 2, 4], F32)
        cl = T.alloc([128, 2, 4], F32)
        cl2 = T.alloc([128, 2, 4], F32)
        T.dma('sp', cw[:, :, :], cw_in.ap(), writes=['cw'])
        T.dma('sp', cb[:, :], cb_in.ap(), writes=['cb'])
        T.dma('sp', ba[:, :, :], lba.ap(), writes=['ba'])
        T.dma('sp', bx[:, :, :], lbx.ap(), writes=['bx'])
        T.dma('sp', lam[:, :, :], llam.ap(), writes=['lam'])
        T.op('act', lambda e: e.activation(out=lam[:, :, :], in_=lam[:, :, :], func=AF.Exp, scale=-1.0), ['lam'], ['lam'])
        T.op('act', lambda e: e.activation(out=lam[:, :, :], in_=lam[:, :, :], func=AF.Ln, bias=1.0), ['lam'], ['lam'])
        T.op('dve', lambda e: e.tensor_scalar(out=cl[:, :, :], in0=lam[:, :, :], scalar1=-8.0, scalar2=None,
                                              op0=ALU.mult), ['lam'], ['cl'])
        T.op('dve', lambda e: e.tensor_scalar(out=cl2[:, :, :], in0=lam[:, :, :], scalar1=-16.0, scalar2=None,
                                              op0=ALU.mult), ['lam'], ['cl2'])
        Wbd = T.alloc([128, 2, 2, 4, 128], BF16)
        T.op('pool', lambda e: e.memset(Wbd[:, :, :, :, :], 0.0), [], ['Wbd'])
        for d in range(2):
            for gi, src in enumerate((lwa, lwx)):
                for blk in range(4):
                    for half in range(2):
                        T.dma('pool', Wbd[half * 64:(half + 1) * 64, d, gi, blk, half * 64:(half + 1) * 64],
                              src.ap()[d, blk * 2 + half], reads=[], writes=['Wbd'])
        zxh = [T.alloc([128, 4, 516], F32) for _ in range(2)]
        xr_ = [T.alloc([128, 4, 512], F32) for _ in range(2)]
        xrb_ = [T.alloc([128, 4, 512], BF16) for _ in range(2)]
        rg_ = [T.alloc([128, 4, 512], F32) for _ in range(2)]
        ig_ = [T.alloc([128, 4, 512], F32) for _ in range(2)]
        aa_ = [T.alloc([128, 4, 512], F32) for _ in range(2)]
        a2_ = [T.alloc([128, 4, 512], F32) for _ in range(2)]
        hout = [T.alloc([128, 4, 512], F32) for _ in range(2)]
        hl = [T.alloc([128, 4, 512], F32) for _ in range(2)]
        gz = [T.alloc([128, 4, 512], BF16) for _ in range(2)]
        yo = [T.alloc([128, 4, 512], BF16) for _ in range(2)]
        hprev = T.alloc([128, 4], F32)
        def stage1(d, ci, b):
            if True:
                st, N, isctx = CHUNKS[ci]
                xr, xrb, rg, ig, aa, a2 = xr_[b], xrb_[b], rg_[b], ig_[b], aa_[b], a2_[b]
                lo_seq, hi_seq = (0, LC) if isctx else (LC, NT)
                lo = max(st - 2, lo_seq)
                hi = min(st + N + 2, hi_seq)
                if lo > st - 2 or hi < st + N + 2:
                    T.op('pool', lambda e: e.memset(zxh[b][:, :, :], 0.0), [], [('zxh', b)])
                T.dma('sp', zxh[b][:, :, 2 + (lo - st):2 + (hi - st)], ZX.ap().rearrange("b p t -> p b t")[:, :, lo:hi],
                      reads=[('ZX', c2) for c2 in range(len(CHUNKS))], writes=[('zxh', b)])
                if d == 1 and not isctx:
                    T.dma('sp', hl[b][:, :, :N], HL.ap().rearrange("b p t -> p b t")[:, :, st:st + N],
                          reads=[('HL', ci)], writes=[('hl', b)])
                    T.dma('sp', gz[b][:, :, :N], GZ.ap().rearrange("b p t -> p b t")[:, :, st:st + N],
                          reads=[('GZ', ci)], writes=[('gz', b)])
                xru = [('xr', b, blk) for blk in range(4)]
                for blk in range(4):
                    T.op('dve', lambda e, blk=blk: e.tensor_scalar(
                        out=xr[:, blk, :N], in0=zxh[b][:, blk, 1:1 + N], scalar1=cw[:, blk, 0:1],
                        scalar2=cb[:, blk:blk + 1], op0=ALU.mult, op1=ALU.add), [('zxh', b), 'cw', 'cb'], [xru[blk]])
                    for j in range(1, 4):
                        T.op('dve', lambda e, blk=blk, j=j: e.scalar_tensor_tensor(
                            out=xr[:, blk, :N], in0=zxh[b][:, blk, 1 + j:1 + j + N], scalar=cw[:, blk, j:j + 1],
                            in1=xr[:, blk, :N], op0=ALU.mult, op1=ALU.add), [('zxh', b), 'cw', xru[blk]], [xru[blk]])
                T.op('act', lambda e: e.activation(out=xrb[:, :, :N], in_=xr[:, :, :N], func=AF.Copy),
                     xru, [('xrb', b)])
                for blk in range(4):
                    T.op('pe', lambda e, blk=blk: e.matmul(PS[blk][:, :N], lhsT=Wbd[:, d, 0, blk, :],
                                                         rhs=xrb[:, blk, :N], start=True, stop=True),
                         ['Wbd', ('xrb', b)], [psu(blk)])
                    T.op('pe', lambda e, blk=blk: e.matmul(PS[4 + blk][:, :N], lhsT=Wbd[:, d, 1, blk, :],
                                                         rhs=xrb[:, blk, :N], start=True, stop=True),
                         ['Wbd', ('xrb', b)], [psu(4 + blk)])
                for blk in range(4):
                    T.op('act', lambda e, blk=blk: e.activation(out=rg[:, blk, :N], in_=PS[blk][:, :N], func=AF.Sigmoid,
                                                               bias=ba[:, d, blk:blk + 1]), [psu(blk), 'ba'],
                         [('rg', b, blk)])
                for blk in range(4):
                    T.op('act', lambda e, blk=blk: e.activation(out=ig[:, blk, :N], in_=PS[4 + blk][:, :N],
                                                               func=AF.Sigmoid, bias=bx[:, d, blk:blk + 1]),
                         [psu(4 + blk), 'bx'], [('ig', b, blk)])
                for blk in range(4):
                    T.op('act', lambda e, blk=blk: e.activation(out=aa[:, blk, :N], in_=rg[:, blk, :N], func=AF.Exp,
                                                               scale=cl[:, d, blk:blk + 1]), [('rg', b, blk), 'cl'],
                         [('aa', b, blk)])
                for blk in range(4):
                    T.op('act', lambda e, blk=blk: e.activation(out=a2[:, blk, :N], in_=rg[:, blk, :N], func=AF.Exp,
                                                               scale=cl2[:, d, blk:blk + 1]), [('rg', b, blk), 'cl2'],
                         [('a2', b, blk)])
                a2u = [('a2', b, blk) for blk in range(4)]
                igu = [('ig', b, blk) for blk in range(4)]
                T.op('act', lambda e: e.activation(out=a2[:, :, :N], in_=a2[:, :, :N], func=AF.Sqrt, bias=1.0,
                                                   scale=-1.0), a2u, a2u)
                T.op('dve', lambda e: e.tensor_tensor(out=ig[:, :, :N], in0=ig[:, :, :N], in1=a2[:, :, :N],
                                                      op=ALU.mult), igu + a2u, igu)
                T.op('dve', lambda e: e.tensor_tensor(out=ig[:, :, :N], in0=ig[:, :, :N], in1=xr[:, :, :N],
                                                      op=ALU.mult), igu + xru, igu)
        def stage2(d, ci, b):
            if True:
                st, N, isctx = CHUNKS[ci]
                xr, xrb, rg, ig, aa, a2 = xr_[b], xrb_[b], rg_[b], ig_[b], aa_[b], a2_[b]
                hu = [('hout', b, blk) for blk in range(4)]
                for blk in range(4):
                    if d == 0:
                        T.op('dve', lambda e, blk=blk: e.tensor_tensor_scan(
                            out=hout[b][:, blk, :N], data0=aa[:, blk, :N], data1=ig[:, blk, :N],
                            initial=hprev[:, blk:blk + 1], op0=ALU.mult, op1=ALU.add),
                            [('aa', b, blk), ('ig', b, blk), 'hprev'], [hu[blk]])
                    else:
                        T.op('dve', lambda e, blk=blk: e.tensor_tensor_scan(
                            out=hout[b][:, blk, :N][:, ::-1], data0=aa[:, blk, :N][:, ::-1],
                            data1=ig[:, blk, :N][:, ::-1], initial=hprev[:, blk:blk + 1], op0=ALU.mult, op1=ALU.add),
                            [('aa', b, blk), ('ig', b, blk), 'hprev'], [hu[blk]])
                lastc = N - 1 if d == 0 else 0
                T.op('dve', lambda e: e.tensor_copy(out=hprev[:, :], in_=hout[b][:, :, lastc]), hu, ['hprev'])
                if isctx:
                    return
                if d == 0:
                    T.dma('sp', HL.ap().rearrange("b p t -> p b t")[:, :, st:st + N], hout[b][:, :, :N],
                          reads=hu, writes=[('HL', ci)])
                else:
                    T.op('dve', lambda e: e.tensor_tensor(out=hl[b][:, :, :N], in0=hl[b][:, :, :N],
                                                          in1=hout[b][:, :, :N], op=ALU.add),
                         hu + [('hl', b)], [('hl', b)])
                    T.op('dve', lambda e: e.tensor_tensor(out=yo[b][:, :, :N], in0=hl[b][:, :, :N],
                                                          in1=gz[b][:, :, :N], op=ALU.mult),
                         [('hl', b), ('gz', b)], [('yol', b)])
                    T.dma('sp', YT.ap()[4:8].rearrange("b p t -> p b t")[:, :, st:st + N], yo[b][:, :, :N],
                          reads=[('yol', b)], writes=[('YT', ci)])

        steps = []
        for d in range(2):
            order = [0] + (list(range(1, 17)) if d == 0 else list(range(16, 0, -1)))
            for ci in order:
                steps.append((d, ci))
        stage1(steps[0][0], steps[0][1], 0)
        for i_, (d, ci) in enumerate(steps):
            if i_ + 1 < len(steps):
                stage1(steps[i_ + 1][0], steps[i_ + 1][1], (i_ + 1) % 2)
            if ci == 0:
                T.op('dve', lambda e: e.memset(hprev[:, :], 0.0), [], ['hprev'])
            stage2(d, ci, i_ % 2)

    def phase3bc(li, chunk_ids, final):
        Wi = T.alloc([128, 8, 2 * DFF], BF16, 'Wi')
        for kc in range(8):
            T.dma('pool', Wi[:, kc, :], ffn_in.ap()[li, kc * 128:(kc + 1) * 128, :], writes=[('Wi', kc)])
        Wo = load_Wo(li)
        Wou = [('Wo2', f0) for f0 in range(0, 22, 2)]
        NS = 256
        hT = [T.alloc([128, 8, NS], BF16) for _ in range(2)]
        xT = [T.alloc([128, 8, NS], F32) for _ in range(2)]
        ac = [T.alloc([128, 22, NS], BF16) for _ in range(2)]
        sg = [T.alloc([128, NS], F32) for _ in range(2)]
        if final:
            tmp = T.alloc([128, 8, NS], F32)
            sq = T.alloc([128, 8, NS], BF16)
            rstd = T.alloc([128, NS], F32)
            otok = [T.alloc([128, D], F32) for _ in range(2)]
        subs = []
        for ci in chunk_ids:
            st, N, isctx = CHUNKS[ci]
            for s0 in range(0, N, NS):
                subs.append((ci, st + s0, isctx))
        cnt = {'n': 0, 'o': 0, 'no': 0}

        def load(si):
            ci, st, isctx = subs[si]
            b = si % 2
            T.dma('sp', hT[b][:, :, :], H2.ap()[:, :, st:st + NS], reads=[('H2', ci)], writes=[('hT', b)])
            T.dma('sp', xT[b][:, :, :], R.ap()[:, :, st:st + NS], reads=[('R', ci)], writes=[('xT', b)])

        load(0)
        for si, (ci, st, isctx) in enumerate(subs):
            b = si % 2
            who = 1 if isctx else 0
            if si + 1 < len(subs):
                load(si + 1)
            for fb in range(22):
                pg = (cnt['n'] % 3) * 2
                s2 = cnt['n'] % 2
                cnt['n'] += 1
                for (col0, pbk) in ((fb * 128, pg), (DFF + fb * 128, pg + 1)):
                    for kc in range(8):
                        T.op('pe', lambda e, col0=col0, pbk=pbk, kc=kc: e.matmul(
                            PS[pbk][:, :NS], lhsT=Wi[:, kc, col0:col0 + 128], rhs=hT[b][:, kc, :],
                            start=(kc == 0), stop=(kc == 7)), [('hT', b), ('Wi', kc)], [psu(pbk)])
                T.op('act', lambda e: e.activation(out=sg[s2][:, :], in_=PS[pg][:, :NS], func=AF.Silu),
                     [psu(pg)], [('sg', s2)])
                T.op('dve', lambda e, fb=fb: e.tensor_tensor(
                    out=ac[b][:, fb, :], in0=PS[pg + 1][:, :NS], in1=sg[s2][:, :], op=ALU.mult),
                    [psu(pg + 1), ('sg', s2)], [('ac', b, fb)])
            for j in range(8):
                pb = 6 + cnt['o'] % 2
                cnt['o'] += 1
                for fb in range(22):
                    T.op('pe', lambda e, fb=fb, j=j: e.matmul(
                        PS[pb][:, :NS], lhsT=Wo[:, fb, j * 128:(j + 1) * 128], rhs=ac[b][:, fb, :],
                        start=(fb == 0), stop=(fb == 21)), [('ac', b, fb)] + Wou, [psu(pb)])
                g_ap = modT[:, li, 5 * 8 + j, who:who + 1]
                T.op('dve', lambda e, j=j: e.scalar_tensor_tensor(
                    out=xT[b][:, j, :], in0=PS[pb][:, :NS], scalar=g_ap, in1=xT[b][:, j, :],
                    op0=ALU.mult, op1=ALU.add), [psu(pb), ('xT', b), 'modT'], [('xT', b)])
            if not final:
                T.dma('sp', R.ap()[:, :, st:st + NS], xT[b][:, :, :], reads=[('xT', b)], writes=[('R', ci)])
            else:
                rmsnorm_mod(xT[b], ('xT', b), NS, li, 0, None, None, None, tmp, 'tmp', sq, 'sq', rstd, 'rstd', 0)
                for tt in range(NS // 128):
                    o2 = cnt['no'] % 2
                    cnt['no'] += 1
                    for j in range(8):
                        pbk = 2 + (j // 4) + 2 * o2
                        T.op('pe', lambda e, j=j, tt=tt, pbk=pbk: e.transpose(
                            PS[pbk][:, (j % 4) * 128:(j % 4 + 1) * 128], tmp[:, j, tt * 128:(tt + 1) * 128],
                            ident[:]), ['tmp', 'ident'], [psu(pbk)])
                    for hf in range(2):
                        pbk = 2 + hf + 2 * o2
                        T.op('act' if hf else 'dve', (lambda e, pbk=pbk, o2=o2, hf=hf: e.activation(
                            out=otok[o2][:, hf * 512:(hf + 1) * 512], in_=PS[pbk][:, :], func=AF.Copy)) if hf else
                            (lambda e, pbk=pbk, o2=o2, hf=hf: e.tensor_copy(
                                out=otok[o2][:, hf * 512:(hf + 1) * 512], in_=PS[pbk][:, :])),
                            [psu(pbk)], [('otok', o2, hf)])
                    r0_ = st - 256 + tt * 128
                    T.dma('sp', out.ap()[r0_:r0_ + 128, :], otok[o2][:, :],
                          reads=[('otok', o2, 0), ('otok', o2, 1)], writes=[('out', ci, si, tt)])

    ALLC = list(range(len(CHUNKS)))
    LAT = list(range(1, len(CHUNKS)))
    phase0()
    T.barrier(PERSIST)
    phase1()
    T.barrier(PERSIST)
    phase2a()
    T.barrier(PERSIST)
    phase2b()
    T.barrier(PERSIST)
    phase3a(0, [(ATd.ap(), 4, 128), (BTd.ap(), 8, 64)],
            [(wo0A.ap(), 4, 128), (wo0B.ap(), 8, 64)], ALLC)
    T.barrier(PERSIST)
    phase3bc(0, ALLC, final=False)
    T.barrier(PERSIST)
    phaseP1()
    T.barrier(PERSIST)
    phaseG()
    T.barrier(PERSIST)
    phaseLr()
    T.barrier(PERSIST)
    phase3a(1, [(YT.ap(), 8, 128)], [(wo1.ap().rearrange("(q p) d -> p q d", p=128), 8, 128)], LAT)
    T.barrier(PERSIST)
    phase3bc(1, LAT, final=True)
    T.barrier(PERSIST)
    if debug:
        for ci in ALLC:
            st, N, _ = CHUNKS[ci]
            for j in range(8):
                T.dma('sp', dbg.ap()[:, j, st:st + N], R.ap()[:, j, st:st + N], reads=[('R', ci)], writes=[('dbg', ci, j)])
        for h in range(8):
            T.dma('sp', dbg2.ap()[h], YT.ap()[h], reads=[('YT', ci) for ci in ALLC], writes=[('dbg2', h)])
        T.barrier(PERSIST)
    T.emit()
    return nc


def _bf(a):
    return np.asarray(a).astype(ml_dtypes.bfloat16)


def make_consts():
    inv = (10000.0 ** (-np.arange(0, 32, 2, dtype=np.float32) / 32.0)).astype(np.float32)
    t = np.arange(L)
    row = (t // 64).astype(np.float32)
    col = (t % 64).astype(np.float32)
    C = np.ones((128, NT), np.float32)
    S = np.zeros((128, NT), np.float32)
    for p in range(128):
        dd = p % 64
        grp, within = dd // 32, dd % 32
        f, part = within % 16, within // 16
        ang = (row if grp == 0 else col) * inv[f]
        C[p, LC:] = np.cos(ang.astype(np.float32))
        sn = np.sin(ang.astype(np.float32))
        S[p, LC:] = -sn if part == 0 else sn
    j = np.arange(128)[:, None]
    q = np.arange(128)[None, :]
    m0 = np.where(j >= q, 0.0, NEG).astype(np.float32)
    m1 = np.where(j <= q, 0.0, NEG).astype(np.float32)
    maskb = np.stack([np.tile(m0, (1, 4)), np.tile(m1, (1, 4))]).astype(ml_dtypes.bfloat16)
    cp = np.arange(64)[:, None]
    c_ = np.arange(64)[None, :]
    tri = np.stack([cp <= c_, cp >= c_, cp > c_, cp < c_]).astype(np.float32)
    mk4 = np.stack([np.tile((cp <= c_), (1, 4)), np.tile((cp >= c_), (1, 4))]).astype(np.float32)
    return C, S, maskb, tri, mk4


def perm64(cols):
    cols = np.asarray(cols)
    dd = np.arange(64)
    within = dd % 32
    partner = np.where(within // 16 == 0, dd + 16, dd - 16)
    out = cols.reshape(-1, 64)[:, partner].reshape(-1)
    return out


def prep_inputs(inp, b):
    f = np.float32
    x, c, ctx = inp['x'], inp['c'], inp['ctx']
    m = {}
    m['xall'] = np.ascontiguousarray(np.concatenate([ctx[b], x[b]], 0), f)
    cc = np.stack([c[b].reshape(8, 128).T, inp['c_ctx'].reshape(8, 128).T], -1)
    m['ccols'] = np.ascontiguousarray(cc, f)
    m['ada_w'] = inp['ada_w']
    m['ada_b2'] = np.ascontiguousarray(np.repeat(inp['ada_b'][:, None, :], 2, 1), f)
    m['normg'] = np.ascontiguousarray(inp['norm_g'].reshape(2, 2, 8, 128).transpose(3, 0, 1, 2), f)
    m['finalg'] = np.ascontiguousarray(inp['final_g'].reshape(8, 128).T, f)
    w = inp['even_w_in'][0]
    qa = np.arange(0, 512)
    ka = np.arange(512, 1024)
    va = np.arange(1024, 1536)
    qb = 1536 + np.arange(512).reshape(2, 4, 64).transpose(1, 0, 2).reshape(-1)
    kb = np.arange(2048, 2176)
    vb = np.arange(2176, 2304)
    cols = np.concatenate([qa, perm64(qa), ka, perm64(ka), qb, perm64(qb), kb, perm64(kb), va, vb])
    m['w0'] = np.ascontiguousarray(w[:, cols], f)
    wo = inp['even_w_out'][0]
    m['wo0A'] = np.ascontiguousarray(wo[:512].reshape(4, 128, D).transpose(1, 0, 2), f)
    m['wo0B'] = np.ascontiguousarray(wo[512:].reshape(8, 64, D).transpose(1, 0, 2), f)
    m['difflam'] = np.ascontiguousarray(np.broadcast_to(inp['diff_lam'][0][None], (128, 4, 64)), f)
    m['sinkc'] = np.ascontiguousarray(np.broadcast_to(inp['win_sink'][0][None], (128, 8)), f)
    m['ident'] = np.eye(128, dtype=f)
    m['ffn_in'] = inp['ffn_w_in']
    m['w1'] = np.ascontiguousarray(inp['odd_w_in'][0], f)
    m['wo1'] = np.ascontiguousarray(inp['odd_w_out'][0], f)
    m['gw'] = np.ascontiguousarray(inp['gla_gate_w'][0], f)
    gb = inp['gla_gate_b'][0]
    m['gb2'] = np.ascontiguousarray(np.broadcast_to(np.concatenate([gb, gb], -1)[:, None, :], (2, 64, 512)), f)
    m['glag'] = np.ascontiguousarray(inp['gla_norm_g'][0].reshape(128, 1), f)
    m['cw'] = np.ascontiguousarray(inp['lru_conv_w'][0].reshape(4, 4, 128).transpose(2, 1, 0), f)
    m['cb'] = np.ascontiguousarray(inp['lru_conv_b'][0].reshape(4, 128).T, f)
    m['lwa'] = np.ascontiguousarray(inp['lru_wa'][0], f)
    m['lwx'] = np.ascontiguousarray(inp['lru_wx'][0], f)
    m['lba'] = np.ascontiguousarray(inp['lru_ba'][0].reshape(2, 4, 128).transpose(2, 0, 1), f)
    m['lbx'] = np.ascontiguousarray(inp['lru_bx'][0].reshape(2, 4, 128).transpose(2, 0, 1), f)
    m['llam'] = np.ascontiguousarray(inp['lru_lam'][0].reshape(2, 4, 128).transpose(2, 0, 1), f)
    m['ffn_out'] = inp['ffn_w_out']
    return m


_CACHE = {}


def kernel(**inputs):
    inp = {k: np.asarray(v) for k, v in inputs.items()}
    if 'nc' not in _CACHE:
        _CACHE['nc'] = build()
        _CACHE['consts'] = make_consts()
    nc = _CACHE['nc']
    C, S, maskb, tri, mk4 = _CACHE['consts']
    in_maps = []
    for b in range(NCORES):
        m = prep_inputs(inp, b)
        m['ropeC'], m['ropeS'], m['maskb'], m['tri'], m['mask4'] = C, S, maskb, tri, mk4
        in_maps.append(m)
    res = run_bass_kernel_spmd(nc, in_maps, core_ids=list(range(NCORES)))
    return np.stack([np.asarray(res.results[b]['out']) for b in range(NCORES)], 0).astype(np.float32)
```
